# Optimizing a Trainium2 kernel written in Bass

```python
import math
import jax, jax.numpy as jnp
from jax import lax
import numpy as np

D_MODEL = 2048
BATCH = 1
SEQ = 16384
DEPTH = 1

GRID_W = 64
CTX_LEN = 256
D_MIX = D_MODEL
ATTN_WIDTH = D_MIX // 2
REC_WIDTH = D_MIX - ATTN_WIDTH
N_HEADS = 8
DV = ATTN_WIDTH // N_HEADS
HEAD_DIM = DV // 2
N_FREQ = HEAD_DIM // 4
N_REC_BLOCKS = 8
REC_BLOCK = REC_WIDTH // N_REC_BLOCKS
REC_CONV = 4
REC_PAD = (2, 1)
RG_C = 8.0
D_FF = ((8 * D_MODEL // 3 + 127) // 128) * 128
FFN_CONV = 3
FFN_PAD = (1, 1)
IN_COLS = 3 * ATTN_WIDTH + 2 * REC_WIDTH
Q_BLOCK = 128
ROPE_THETA = 10000.0
EPS = 1e-6
SUBLN_EPS = 1e-5

kernel_name = 'hymba_diffattn_rglru_convffn_dit'


def rms_norm(t, g, eps=EPS):
    tf = t.astype(jnp.float32)
    y = tf * lax.rsqrt(jnp.mean(tf * tf, axis=-1, keepdims=True) + eps)
    return (y * g).astype(t.dtype)


def modulate(h, shift, scale):
    return h * (1 + scale) + shift


def dwconv(t, w, b, pad):
    n = t.shape[1]
    tp = jnp.pad(t, ((0, 0), pad, (0, 0)))
    y = b
    for j in range(w.shape[0]):
        y = y + tp[:, j:j + n] * w[j]
    return y


def split_in(p):
    b, n = p.shape[:2]
    q, k, v, rx, rz = jnp.split(p, [ATTN_WIDTH, 2 * ATTN_WIDTH, 3 * ATTN_WIDTH,
                                    3 * ATTN_WIDTH + REC_WIDTH], axis=-1)
    q = q.reshape(b, n, N_HEADS, 2, HEAD_DIM)
    k = k.reshape(b, n, N_HEADS, 2, HEAD_DIM)
    v = v.reshape(b, n, N_HEADS, DV)
    return q, k, v, rx, rz


def axial_rope_tables(row, col):
    inv = ROPE_THETA ** (-jnp.arange(N_FREQ, dtype=jnp.float32) / N_FREQ)
    ang = jnp.stack([row.astype(jnp.float32)[:, None] * inv,
                     col.astype(jnp.float32)[:, None] * inv], axis=1)
    return jnp.cos(ang), jnp.sin(ang)


def apply_rope(t, cos, sin):
    shp = t.shape
    tr = t.astype(jnp.float32).reshape(shp[:-1] + (2, 2, N_FREQ))
    t1, t2 = tr[..., 0, :], tr[..., 1, :]
    cs, sn = cos[None, :, None, None], sin[None, :, None, None]
    out = jnp.stack([t1 * cs - t2 * sn, t2 * cs + t1 * sn], axis=-2)
    return out.reshape(shp).astype(t.dtype)


def diff_lambda_value(dl, lam_init):
    d = dl.astype(jnp.float32)
    return jnp.exp(jnp.sum(d[0] * d[1])) - jnp.exp(jnp.sum(d[2] * d[3])) + lam_init


def diff_attend(q, k, v, lam):
    s = jnp.einsum('bqhcd,bkhcd->bhcqk', q, k).astype(jnp.float32) * (HEAD_DIM ** -0.5)
    p = jax.nn.softmax(s, axis=-1)
    w = p[:, :, 0] - lam * p[:, :, 1]
    return jnp.einsum('bhqk,bkhd->bqhd', w.astype(v.dtype), v)


def blocked_diff_attention(q, k, v, lam):
    b, s = q.shape[:2]
    nb = s // Q_BLOCK
    qb = jnp.moveaxis(q.reshape((b, nb, Q_BLOCK) + q.shape[2:]), 1, 0)
    out = lax.map(lambda qblk: diff_attend(qblk, k, v, lam), qb)
    return jnp.moveaxis(out, 0, 1).reshape((b, s) + out.shape[3:])


def finish_attn(o, g, lam_init):
    b, n = o.shape[:2]
    o = rms_norm(o, g, SUBLN_EPS) * (1 - lam_init)
    return o.reshape(b, n, ATTN_WIDTH)


def block_diag(t, w):
    b, n, _ = t.shape
    tb = t.reshape(b, n, N_REC_BLOCKS, REC_BLOCK)
    return jnp.einsum('bnkc,kcd->bnkd', tb, w).reshape(b, n, REC_WIDTH)


def _lin_combine(left, right):
    a_l, b_l = left
    a_r, b_r = right
    return a_l * a_r, a_r * b_l + b_r


def rglru_scan(t, wa, ba, wi, bi, lam, h0):
    r = jax.nn.sigmoid(block_diag(t, wa).astype(jnp.float32) + ba)
    i = jax.nn.sigmoid(block_diag(t, wi).astype(jnp.float32) + bi)
    log_a = -RG_C * r * jax.nn.softplus(-lam.astype(jnp.float32))
    a = jnp.exp(log_a)
    bvals = jnp.sqrt(-jnp.expm1(2 * log_a)) * (i * t.astype(jnp.float32))
    if h0 is not None:
        bvals = bvals.at[:, 0].add(a[:, 0] * h0)
    _, h = lax.associative_scan(_lin_combine, (a, bvals), axis=1)
    return h


def flip(t, d):
    return t[:, ::-1] if d == 1 else t


def conv_ffn(h, w_up, w_gate, cw, cb, w_down):
    u = h @ w_up
    g = dwconv(h @ w_gate, cw, cb, FFN_PAD)
    return (jax.nn.gelu(g, approximate=True) * u) @ w_down


def setup_inputs(seed: int = 0) -> dict:
    key = jax.random.key(seed)
    ks = jax.random.split(key, 26)
    f32 = jnp.float32
    L = DEPTH

    def nrm(k, shape, scale):
        return jax.random.normal(k, shape, f32) * scale

    a0 = jax.random.uniform(ks[14], (L, 2, REC_WIDTH), f32, 0.9, 0.999)
    s0 = a0 ** (1.0 / RG_C)
    return {
        'x': nrm(ks[0], (BATCH, SEQ, D_MODEL), 1.0),
        'c': nrm(ks[1], (BATCH, D_MODEL), 1.0),
        'ctx': nrm(ks[2], (BATCH, CTX_LEN, D_MODEL), 1.0),
        'c_ctx': nrm(ks[3], (D_MODEL,), 1.0),
        'w_ada': nrm(ks[4], (L, D_MODEL, 6 * D_MODEL), 0.5 * D_MODEL ** -0.5),
        'b_ada': nrm(ks[5], (L, 6 * D_MODEL), 0.02),
        'norm1_g': 1 + nrm(ks[6], (L, D_MODEL), 0.05),
        'w_in': nrm(ks[7], (L, D_MODEL, IN_COLS), D_MODEL ** -0.5),
        'rec_conv_w': nrm(ks[8], (L, REC_CONV, REC_WIDTH), REC_CONV ** -0.5),
        'rec_conv_b': nrm(ks[9], (L, REC_WIDTH), 0.02),
        'rg_wa': nrm(ks[10], (L, 2, N_REC_BLOCKS, REC_BLOCK, REC_BLOCK), REC_BLOCK ** -0.5),
        'rg_ba': nrm(ks[11], (L, 2, REC_WIDTH), 0.02),
        'rg_wi': nrm(ks[12], (L, 2, N_REC_BLOCKS, REC_BLOCK, REC_BLOCK), REC_BLOCK ** -0.5),
        'rg_bi': nrm(ks[13], (L, 2, REC_WIDTH), 0.02),
        'rg_lambda': jnp.log(s0) - jnp.log1p(-s0),
        'diff_lambda': nrm(ks[15], (L, 4, HEAD_DIM), 0.1),
        'subln_g': 1 + nrm(ks[16], (L, DV), 0.05),
        'w_out': nrm(ks[17], (L, D_MIX, D_MODEL), D_MIX ** -0.5),
        'norm2_g': 1 + nrm(ks[18], (L, D_MODEL), 0.05),
        'w_up': nrm(ks[19], (L, D_MODEL, D_FF), D_MODEL ** -0.5),
        'w_gate': nrm(ks[20], (L, D_MODEL, D_FF), D_MODEL ** -0.5),
        'ffn_conv_w': nrm(ks[21], (L, FFN_CONV, D_FF), FFN_CONV ** -0.5),
        'ffn_conv_b': nrm(ks[22], (L, D_FF), 0.02),
        'w_down': nrm(ks[23], (L, D_FF, D_MODEL), D_FF ** -0.5),
        'final_g': 1 + nrm(ks[24], (D_MODEL,), 0.05),
    }


def reference(x, c, ctx, c_ctx, w_ada, b_ada, norm1_g, w_in, rec_conv_w, rec_conv_b,
              rg_wa, rg_ba, rg_wi, rg_bi, rg_lambda, diff_lambda, subln_g, w_out,
              norm2_g, w_up, w_gate, ffn_conv_w, ffn_conv_b, w_down, final_g):
    n_lat = x.shape[1]
    rows = n_lat // GRID_W
    row, col = jnp.meshgrid(jnp.arange(rows), jnp.arange(GRID_W), indexing='ij')
    cos, sin = axial_rope_tables(row.reshape(-1), col.reshape(-1))
    cx = ctx
    for l in range(DEPTH):
        last = l == DEPTH - 1
        lam_init = 0.8 - 0.6 * math.exp(-0.3 * l)
        mod = (jax.nn.silu(c) @ w_ada[l] + b_ada[l])[:, None, :]
        mod_c = (jax.nn.silu(c_ctx) @ w_ada[l] + b_ada[l])[None, None, :]
        sh1, sc1, g1, sh2, sc2, g2 = jnp.split(mod, 6, axis=-1)
        csh1, csc1, cg1, csh2, csc2, cg2 = jnp.split(mod_c, 6, axis=-1)

        h = modulate(rms_norm(x, norm1_g[l]), sh1, sc1)
        hc = modulate(rms_norm(cx, norm1_g[l]), csh1, csc1)
        q, k, v, rx, rz = split_in(h @ w_in[l])
        qc, kc, vc, rxc, rzc = split_in(hc @ w_in[l])

        lam = diff_lambda_value(diff_lambda[l], lam_init)
        q = apply_rope(q, cos, sin)
        k = apply_rope(k, cos, sin)
        k_all = jnp.concatenate([kc, k], axis=1)
        v_all = jnp.concatenate([vc, v], axis=1)
        attn = finish_attn(blocked_diff_attention(q, k_all, v_all, lam), subln_g[l], lam_init)

        xr = dwconv(rx, rec_conv_w[l], rec_conv_b[l], REC_PAD)
        xrc = dwconv(rxc, rec_conv_w[l], rec_conv_b[l], REC_PAD)
        rec = 0.0
        rec_c = 0.0
        for d in range(2):
            prm = (rg_wa[l, d], rg_ba[l, d], rg_wi[l, d], rg_bi[l, d], rg_lambda[l, d])
            h_ctx = rglru_scan(flip(xrc, d), *prm, None)
            h_lat = rglru_scan(flip(xr, d), *prm, h_ctx[:, -1])
            rec = rec + flip(h_lat, d)
            if not last:
                rec_c = rec_c + flip(h_ctx, d)
        rec = (rec * jax.nn.gelu(rz, approximate=True)).astype(x.dtype)
        x = x + g1 * (jnp.concatenate([attn, rec], axis=-1) @ w_out[l])
        if not last:
            attn_c = finish_attn(diff_attend(qc, kc, vc, lam), subln_g[l], lam_init)
            rec_c = (rec_c * jax.nn.gelu(rzc, approximate=True)).astype(cx.dtype)
            cx = cx + cg1 * (jnp.concatenate([attn_c, rec_c], axis=-1) @ w_out[l])

        h2 = modulate(rms_norm(x, norm2_g[l]), sh2, sc2)
        x = x + g2 * conv_ffn(h2, w_up[l], w_gate[l], ffn_conv_w[l], ffn_conv_b[l], w_down[l])
        if not last:
            hc2 = modulate(rms_norm(cx, norm2_g[l]), csh2, csc2)
            cx = cx + cg2 * conv_ffn(hc2, w_up[l], w_gate[l], ffn_conv_w[l], ffn_conv_b[l], w_down[l])
    return rms_norm(x, final_g)
```

```python
import os
from contextlib import ExitStack
import numpy as np
import ml_dtypes
import concourse.bass as bass
import concourse.mybir as mybir
from concourse.bass_utils import run_bass_kernel_spmd

F32 = mybir.dt.float32
BF16 = mybir.dt.bfloat16
AF = mybir.ActivationFunctionType
ALU = mybir.AluOpType

D = 2048
S = 16384
NCTX = 256
DFF = 5504
NJ = 43
NKEY = S + NCTX
NKT = NKEY // 128
OWN = 2048
EXT = 2050
NCORES = 8
EPS = 1e-6
SUBLN_EPS = 1e-5
LAM_INIT = 0.8 - 0.6
TT = 256
SKIP = os.environ.get('P1SKIP', '')

V_BADA = 0
V_N1G = 96
V_N2G = 112
V_FCW = 128
V_FCB = V_FCW + 129
V_RCW = V_FCB + 43
V_RCB = V_RCW + 32
V_RBA = V_RCB + 8
V_RBI = V_RBA + 16
V_RLAM = V_RBI + 16
NV = V_RLAM + 16


class Sem:
    _k = 0

    def __init__(self, nc, name):
        self.h = nc.alloc_semaphore(name)
        self.n = 0
        Sem._k += 1
        self.key = Sem._k


class Eng:
    def __init__(self, nc, eng, name, is_pe=False):
        self.e = eng
        self.sem = Sem(nc, "s_" + name)
        self.seen = {}
        self.is_pe = is_pe

    def wait(self, toks):
        best = {}
        for t in toks:
            if t is None:
                continue
            sem, val = t
            if self.is_pe and sem is self.sem:
                continue
            if self.seen.get(sem.key, 0) >= val:
                continue
            if sem.key not in best or best[sem.key][1] < val:
                best[sem.key] = (sem, val)
        for sem, val in best.values():
            self.e.wait_ge(sem.h, val)
            self.seen[sem.key] = val

    def mark(self, ins):
        self.sem.n += 1
        ins.then_inc(self.sem.h, 1)
        return (self.sem, self.sem.n)


class Buf:
    def __init__(self, name=""):
        self.name = name
        self.w = None
        self.r = {}
        self.dsem = None

    def rtoks(self):
        return list(self.r.values())

    def add_r(self, tok):
        sem, val = tok
        if sem.key not in self.r or self.r[sem.key][1] < val:
            self.r[sem.key] = tok


class K:
    def __init__(self, nc):
        self.nc = nc
        self.PE = Eng(nc, nc.tensor, "pe", is_pe=True)
        self.ACT = Eng(nc, nc.scalar, "act")
        self.DVE = Eng(nc, nc.vector, "dve")
        self.POOL = Eng(nc, nc.gpsimd, "pool")
        self.SP = Eng(nc, nc.sync, "sp")
        self.nsem = 5

    def _deps(self, reads, writes):
        toks = []
        for b in reads:
            toks.append(b.w)
        for b in writes:
            toks.append(b.w)
            toks += b.rtoks()
        return toks

    def _commit(self, tok, reads, writes):
        for b in reads:
            b.add_r(tok)
        for b in writes:
            b.w = tok
            b.r = {}

    def op(self, E, fn, reads=(), writes=()):
        E.wait(self._deps(reads, writes))
        ins = fn(E.e)
        tok = E.mark(ins)
        self._commit(tok, reads, writes)
        return tok

    def pe(self, fns, reads=(), writes=()):
        E = self.PE
        E.wait(self._deps(reads, writes))
        ins = None
        for f in fns:
            ins = f(E.e)
        tok = E.mark(ins)
        self._commit(tok, reads, writes)
        return tok

    def dma(self, out, in_, sbuf, reads=(), writes=(), eng=None):
        E = eng or self.SP
        if sbuf.dsem is None:
            sbuf.dsem = Sem(self.nc, "d_%d" % self.nsem)
            self.nsem += 1
        E.wait(self._deps(reads, writes))
        sbuf.dsem.n += 16
        E.e.dma_start(out=out, in_=in_).then_inc(sbuf.dsem.h, 16)
        tok = (sbuf.dsem, sbuf.dsem.n)
        self._commit(tok, reads, writes)
        return tok


def build_program(debug=False, stop_after=99):
    nc = bass.Bass("TRN2", target_bir_lowering=False)
    k = K(nc)
    PE, ACT, DVE, POOL, SP = k.PE, k.ACT, k.DVE, k.POOL, k.SP

    def din(name, shape, dt=F32):
        return nc.dram_tensor(name, list(shape), dt, kind="ExternalInput").ap()

    x_d = din("x", [S, D])
    ctx_d = din("ctx", [NCTX, D])
    xo_d = din("xo", [17 * 128, D])
    cvec_d = din("cvec", [128, 32])
    vecs_d = din("vecs", [128, NV])
    wada_d = din("wada", [D, 6 * D])
    win_d = din("win", [D, 5120])
    wout_d = din("wout", [D, D])
    wup_d = din("wup", [NJ, 128, 2048])
    wgate_d = din("wgate", [NJ, 128, 2048])
    wdown_d = din("wdown", [NJ, 128, 2048])
    rgw_d = din("rgw", [128, 32 * 128])
    cos_d = din("cos", [128, S])
    sin_d = din("sin", [128, S])
    coso_d = din("coso", [128, 17 * 128])
    sino_d = din("sino", [128, 17 * 128])
    perm_d = din("perm", [128, 128])
    ident_d = din("identf", [128, 128])
    identb_d = din("identb", [128, 128], BF16)
    dlam_d = din("dlam", [128, 256])
    subg_d = din("subg", [128, 128])
    fing_d = din("fing", [128, D])
    mk_d = din("mk", [128, 32])
    out_d = nc.dram_tensor("out", [OWN, D], F32, kind="ExternalOutput").ap()

    KT_d = nc.dram_tensor("KT", [8, 128, NKEY], BF16).ap()
    VV_d = nc.dram_tensor("VV", [8, 128, NKT, 129], BF16).ap()
    RXL_d = nc.dram_tensor("RXL", [8, 128, S], F32).ap()
    RXC_d = nc.dram_tensor("RXC", [8, 128, NCTX], F32).ap()
    dbg = {}
    if debug:
        dbg["mod"] = nc.dram_tensor("dbg_mod", [128, 192], F32, kind="ExternalOutput").ap()
        dbg["KT"] = nc.dram_tensor("dbg_KT", [8, 128, 1024], BF16, kind="ExternalOutput").ap()
        dbg["VV"] = nc.dram_tensor("dbg_VV", [8, 128, 8, 129], BF16, kind="ExternalOutput").ap()
        dbg["RX"] = nc.dram_tensor("dbg_RX", [8, 128, 1024], F32, kind="ExternalOutput").ap()
        dbg["AT"] = nc.dram_tensor("dbg_AT", [128, 17 * 128], BF16, kind="ExternalOutput").ap()
        dbg["QT"] = nc.dram_tensor("dbg_QT", [128, 17 * 128], BF16, kind="ExternalOutput").ap()
        dbg["GZ"] = nc.dram_tensor("dbg_GZ", [128, 17 * 128], BF16, kind="ExternalOutput").ap()

    def sb(name, shape, dt):
        return nc.alloc_sbuf_tensor("sb_" + name, shape, dt)
    pst = nc.alloc_psum_tensor

    vecs = sb("vecs", [128, NV], F32)
    modx = sb("modx", [128, 96], F32)
    modc = sb("modc", [128, 96], F32)
    a1 = sb("a1", [128, 16], F32)
    a1c = sb("a1c", [128, 16], F32)
    a2 = sb("a2", [128, 16], F32)
    modacc = sb("modacc", [128, 192], F32)
    identb = sb("identb", [128, 128], BF16)
    identf = sb("identf", [128, 128], F32)
    permT = sb("permT", [128, 128], F32)
    ones_f = sb("ones_f", [128, 128], F32)
    B_vecs = Buf("vecs")
    B_mod = Buf("mod")
    B_const = Buf("const")

    k.dma(vecs[:], vecs_d[:, :], B_vecs, writes=[B_vecs])
    k.dma(identb[:], identb_d[:, :], B_const, writes=[B_const])
    k.dma(identf[:], ident_d[:, :], B_const, writes=[B_const])
    k.dma(permT[:], perm_d[:, :], B_const, writes=[B_const])
    k.op(k.DVE, lambda e: e.memset(ones_f[:], 1.0), writes=[B_const])

    psall = pst("psall", [128, 4096], F32)
    banks = [psall[:, i * 512:(i + 1) * 512] for i in range(8)]
    Bbank = [Buf("bank%d" % i) for i in range(8)]
    tpbs = [banks[6].bitcast(BF16), banks[7].bitcast(BF16)]
    Btp = [Bbank[6], Bbank[7]]

    with nc.sbuf_tensor("sb_wk0", [128, 6 * D], F32) as wk0, nc.sbuf_tensor("sb_wk1", [128, 6 * D], F32) as wk1, \
            nc.sbuf_tensor("sb_cvec", [128, 32], F32) as cvec, nc.sbuf_tensor("sb_scv", [128, 32], F32) as scv:
        wk = [wk0, wk1]
        Bwk = [Buf("wk0"), Buf("wk1")]
        B_cv = Buf("cvec")
        B_scv = Buf("scv")
        k.dma(cvec[:], cvec_d[:, :], B_cv, writes=[B_cv])
        k.op(ACT, lambda e: e.activation(out=scv[:], in_=cvec[:], func=AF.Silu), reads=[B_cv], writes=[B_scv])
        macc = modacc
        B_macc = Buf("macc")
        for kc in range(16):
            s = kc % 2
            for q in range(4):
                k.dma(wk[s][:, q * 3072:(q + 1) * 3072], wada_d[kc * 128:(kc + 1) * 128, q * 3072:(q + 1) * 3072],
                      Bwk[s], writes=[Bwk[s]] if q == 0 else [])
            Bwk[s].w = (Bwk[s].dsem, Bwk[s].dsem.n)
            psm = banks[s]
            fns = []
            for j in range(96):
                fns.append(lambda e, j=j, s=s, kc=kc, psm=psm: e.matmul(
                    psm[:, 2 * j:2 * j + 2], lhsT=wk[s][:, j * 128:(j + 1) * 128], rhs=scv[:, 2 * kc:2 * kc + 2],
                    start=True, stop=True))
            k.pe(fns, reads=[Bwk[s], B_scv], writes=[Bbank[s]])
            if kc == 0:
                k.op(DVE, lambda e, psm=psm: e.tensor_copy(macc[:], psm[:, 0:192]), reads=[Bbank[s]], writes=[B_macc])
            else:
                k.op(DVE, lambda e, psm=psm: e.tensor_tensor(out=macc[:], in0=macc[:], in1=psm[:, 0:192], op=ALU.add),
                     reads=[Bbank[s], B_macc], writes=[B_macc])
        psv = macc[:].rearrange("p (j t) -> p j t", t=2)
        k.op(DVE, lambda e: e.tensor_tensor(out=modx[:], in0=psv[:, :, 0], in1=vecs[:, V_BADA:V_BADA + 96], op=ALU.add),
             reads=[B_macc, B_vecs], writes=[B_mod])
        k.op(DVE, lambda e: e.tensor_tensor(out=modc[:], in0=psv[:, :, 1], in1=vecs[:, V_BADA:V_BADA + 96], op=ALU.add),
             reads=[B_macc, B_vecs], writes=[B_mod])
        k.op(DVE, lambda e: e.scalar_tensor_tensor(out=a1[:], in0=modx[:, 16:32], scalar=1.0, in1=vecs[:, V_N1G:V_N1G + 16],
                                                   op0=ALU.add, op1=ALU.mult), reads=[B_mod, B_vecs], writes=[B_mod])
        k.op(DVE, lambda e: e.scalar_tensor_tensor(out=a1c[:], in0=modc[:, 16:32], scalar=1.0, in1=vecs[:, V_N1G:V_N1G + 16],
                                                   op0=ALU.add, op1=ALU.mult), reads=[B_mod, B_vecs], writes=[B_mod])
        k.op(DVE, lambda e: e.scalar_tensor_tensor(out=a2[:], in0=modx[:, 64:80], scalar=1.0, in1=vecs[:, V_N2G:V_N2G + 16],
                                                   op0=ALU.add, op1=ALU.mult), reads=[B_mod, B_vecs], writes=[B_mod])
        if debug:
            B_dm = Buf("dbgmod")
            k.dma(dbg["mod"][:, 0:96], modx[:], B_dm, reads=[B_mod])
            k.dma(dbg["mod"][:, 96:192], modc[:], B_dm, reads=[B_mod])
        bar = [Bwk[0].w, Bwk[1].w, Bbank[0].w, Bbank[1].w, B_mod.w] + Bwk[0].rtoks() + Bwk[1].rtoks() + B_scv.rtoks()
    for E in (PE, ACT, DVE, POOL, SP):
        E.wait(bar)
    if debug and stop_after == 0:
        SP.wait([(B_dm.dsem, B_dm.dsem.n)])
        return nc
    if stop_after == 0:
        return nc

    epsc = sb("epsc", [128, 2], F32)
    k.op(DVE, lambda e: e.memset(epsc[:, 0:1], EPS), writes=[B_const])
    k.op(DVE, lambda e: e.memset(epsc[:, 1:2], SUBLN_EPS), writes=[B_const])
    junk = sb("junk", [128, D], BF16)
    B_junk = Buf("junk")
    xn = [sb("xn%d" % i, [128, D], BF16) for i in range(2)]
    Bxn = [Buf("xn%d" % i) for i in range(2)]
    ssq = [sb("ssq%d" % i, [128, 4], F32) for i in range(2)]
    Bssq = [Buf("ssq%d" % i) for i in range(2)]
    cnt = {"nt": 0, "tp": 0, "ev": 0}

    def norm_transpose(src_ap, Bsrc, rows, avec, bvec, dst_fn, Bdst, boff=0):
        i = cnt["nt"] % 2
        cnt["nt"] += 1
        sq = ssq[i]
        k.op(ACT, lambda e: e.activation(out=junk[0:rows, :], in_=src_ap, func=AF.Square, accum_out=sq[0:rows, 0:1]),
             reads=[Bsrc], writes=[B_junk, Bssq[i]])
        k.op(ACT, lambda e: e.activation(out=sq[0:rows, 1:2], in_=sq[0:rows, 0:1], func=AF.Ln, scale=1.0 / D, bias=epsc[0:rows, 0:1]),
             reads=[Bssq[i], B_const], writes=[Bssq[i]])
        k.op(ACT, lambda e: e.activation(out=sq[0:rows, 2:3], in_=sq[0:rows, 1:2], func=AF.Exp, scale=-0.5),
             reads=[Bssq[i]], writes=[Bssq[i]])
        k.op(DVE, lambda e: e.tensor_scalar(out=xn[i][0:rows, :], in0=src_ap, scalar1=sq[0:rows, 2:3], scalar2=None, op0=ALU.mult),
             reads=[Bsrc, Bssq[i]], writes=[Bxn[i]])
        for g in range(2):
            hb = cnt["tp"] % 2
            cnt["tp"] += 1
            tpb = tpbs[hb]
            fns = []
            for q in range(8):
                kc = g * 8 + q
                fns.append(lambda e, kc=kc, q=q, tpb=tpb: e.transpose(tpb[:, q * 128: q * 128 + rows],
                                                                      xn[i][0:rows, kc * 128:(kc + 1) * 128], identb[0:rows, 0:rows]))
            k.pe(fns, reads=[Bxn[i], B_const], writes=[Btp[hb]])
            for q in range(8):
                kc = g * 8 + q
                src = tpb[:, q * 128: q * 128 + rows]
                k.op(ACT, lambda e, kc=kc, src=src: e.activation(out=dst_fn(kc), in_=src, func=AF.Identity,
                                                                scale=avec[:, kc:kc + 1], bias=bvec[:, boff + kc:boff + kc + 1]),
                     reads=[Btp[hb], B_mod], writes=[Bdst])

    with ExitStack() as es:
        def sc(name, shape, dt):
            return es.enter_context(nc.sbuf_tensor("sb_" + name, shape, dt))
        wkvr = sc("wkvr", [128, 16, 3072], BF16)
        wst0 = sc("wst0", [128, 1024], F32); wst1 = sc("wst1", [128, 1024], F32)
        xs0 = sc("xs0", [128, 2, D], F32); xs1 = sc("xs1", [128, 2, D], F32)
        hT0 = sc("hT0", [128, 16, TT], BF16); hT1 = sc("hT1", [128, 16, TT], BF16)
        cs0 = sc("cs0", [128, 2, TT], F32); cs1 = sc("cs1", [128, 2, TT], F32)
        k32a = sc("k32a", [128, TT], F32); k32b = sc("k32b", [128, TT], F32)
        t1a = sc("t1a", [128, TT], F32); t1b = sc("t1b", [128, TT], F32)
        t2a = sc("t2a", [128, TT], F32); t2b = sc("t2b", [128, TT], F32)
        ko = sc("ko", [128, 4, TT], BF16); rxo = sc("rxo", [128, 4, TT], F32)
        vt0 = sc("vt0", [128, 8, 2, 129], BF16); vt1 = sc("vt1", [128, 8, 2, 129], BF16)
        wst = [wst0, wst1]
        Bwst = [Buf(), Buf()]
        B_w = Buf("wkvr")
        n = 0
        for kc in range(16 if 'W' not in SKIP else 0):
            for gi in range(3):
                s_ = n % 2
                n += 1
                k.dma(wst[s_][:], win_d[kc * 128:(kc + 1) * 128, 1024 + gi * 1024: 2048 + gi * 1024], Bwst[s_], writes=[Bwst[s_]])
                k.op(ACT, lambda e, s_=s_, kc=kc, gi=gi: e.activation(out=wkvr[:, kc, gi * 1024:(gi + 1) * 1024], in_=wst[s_][:], func=AF.Identity),
                     reads=[Bwst[s_]], writes=[B_w])
        xs = [xs0, xs1]
        Bxs = [Buf(), Buf()]
        hT = [hT0, hT1]
        BhT = [Buf(), Buf()]
        cs = [cs0, cs1]
        Bcs = [Buf(), Buf()]
        k32 = [k32a, k32b]
        Bk32 = [Buf(), Buf()]
        t1 = [t1a, t1b]
        Bt1 = [Buf(), Buf()]
        t2 = [t2a, t2b]
        Bt2 = [Buf(), Buf()]
        Bko = [Buf() for _ in range(4)]
        Brxo = [Buf() for _ in range(4)]
        vt = [vt0, vt1]
        Bvt = [Buf(), Buf()]
        for v_ in vt:
            k.op(POOL, lambda e, v_=v_: e.memset(v_[:, :, :, 128:129], 1.0), writes=[Bvt[0], Bvt[1]])
        B_scr = Buf("scratch")
        ntiles = 1 + S // TT
        if stop_after == 1 and debug:
            ntiles = 5
        NT_P1 = ntiles
        gslot = {"o": 0, "p": 0, "v": 0, "ko": 0, "rx": 0, "k32": 0}

        def load_tile(ti):
            s_ = ti % 2
            src = ctx_d if ti == 0 else x_d
            r0 = 0 if ti == 0 else (ti - 1) * TT
            k.dma(xs[s_][:], src[r0:r0 + TT, :].rearrange("(s p) d -> p s d", p=128), Bxs[s_], writes=[Bxs[s_]])
            if ti > 0:
                k.dma(cs[s_][:, 0, :], cos_d[:, r0:r0 + TT], Bcs[s_], writes=[Bcs[s_]])
                k.dma(cs[s_][:, 1, :], sin_d[:, r0:r0 + TT], Bcs[s_], writes=[])
                Bcs[s_].w = (Bcs[s_].dsem, Bcs[s_].dsem.n)

        def nt_tile(ti):
            s_ = ti % 2
            av, bv = (a1c, modc) if ti == 0 else (a1, modx)
            for su in range(2):
                norm_transpose(xs[s_][:, su, :], Bxs[s_], 128, av, bv,
                               lambda kc, su=su, s_=s_: hT[s_][:, kc, su * 128:(su + 1) * 128], BhT[s_])

        load_tile(0)
        nt_tile(0)
        for ti in range(ntiles):
            s_ = ti % 2
            if ti + 1 < ntiles:
                load_tile(ti + 1)
            key0 = 0 if ti == 0 else NCTX + (ti - 1) * TT
            pending = []
            for h in range(8):
                oslot = gslot["o"] % 2
                gslot["o"] += 1
                ob = banks[oslot][:, 0:TT]
                Bo = Bbank[oslot]
                k.pe([lambda e, kc=kc, h=h, ob=ob: e.matmul(ob, lhsT=wkvr[:, kc, h * 128:(h + 1) * 128], rhs=hT[s_][:, kc, :],
                                                            start=(kc == 0), stop=(kc == 15)) for kc in range(16)],
                     reads=[B_w, BhT[s_]], writes=[Bo])
                ks = gslot["ko"] % 4
                gslot["ko"] += 1
                if ti == 0:
                    k.op(ACT, lambda e, ob=ob, ks=ks: e.activation(out=ko[:, ks, :], in_=ob, func=AF.Identity), reads=[Bo], writes=[Bko[ks]])
                    k.dma(KT_d[h, :, key0:key0 + TT], ko[:, ks, :], Bko[ks], reads=[Bko[ks]])
                    continue
                q_ = gslot["k32"] % 2
                gslot["k32"] += 1
                k.op(ACT, lambda e, ob=ob, q_=q_: e.activation(out=k32[q_][:], in_=ob, func=AF.Identity), reads=[Bo], writes=[Bk32[q_]])

                def post(h=h, q_=q_, ks=ks):
                    pb = banks[2][:, 0:TT]
                    k.pe([lambda e, pb=pb, q_=q_: e.matmul(pb, lhsT=permT[:], rhs=k32[q_][:], start=True, stop=True)],
                         reads=[Bk32[q_], B_const], writes=[Bbank[2]])
                    k.op(DVE, lambda e, pb=pb, q_=q_: e.tensor_tensor(out=t1[q_][:], in0=pb, in1=cs[s_][:, 1, :], op=ALU.mult),
                         reads=[Bbank[2], Bcs[s_]], writes=[Bt1[q_]])
                    k.op(POOL, lambda e, q_=q_: e.tensor_tensor(out=t2[q_][:], in0=k32[q_][:], in1=cs[s_][:, 0, :], op=ALU.mult),
                         reads=[Bk32[q_], Bcs[s_]], writes=[Bt2[q_]])
                    k.op(POOL, lambda e, q_=q_, ks=ks: e.tensor_tensor(out=ko[:, ks, :], in0=t1[q_][:], in1=t2[q_][:], op=ALU.add),
                         reads=[Bt1[q_], Bt2[q_]], writes=[Bko[ks]])
                    k.dma(KT_d[h, :, key0:key0 + TT], ko[:, ks, :], Bko[ks], reads=[Bko[ks]])
                if pending:
                    pending.pop(0)()
                pending.append(post)
            if ti + 1 < ntiles:
                nt_tile(ti + 1)
            while pending:
                pending.pop(0)()
            for b in range(0 if 'R' not in SKIP else 8, 8):
                oslot = gslot["o"] % 2
                gslot["o"] += 1
                ob = banks[oslot][:, 0:TT]
                Bo = Bbank[oslot]
                k.pe([lambda e, kc=kc, b=b, ob=ob: e.matmul(ob, lhsT=wkvr[:, kc, 2048 + b * 128:2048 + (b + 1) * 128], rhs=hT[s_][:, kc, :],
                                                            start=(kc == 0), stop=(kc == 15)) for kc in range(16)],
                     reads=[B_w, BhT[s_]], writes=[Bo])
                rs = gslot["rx"] % 4
                gslot["rx"] += 1
                k.op(ACT, lambda e, ob=ob, rs=rs: e.activation(out=rxo[:, rs, :], in_=ob, func=AF.Identity), reads=[Bo], writes=[Brxo[rs]])
                dst = RXC_d[b, :, :] if ti == 0 else RXL_d[b, :, (ti - 1) * TT: ti * TT]
                k.dma(dst, rxo[:, rs, :], Brxo[rs], reads=[Brxo[rs]])
            vs_ = ti % 2
            for su in range(0 if 'V' not in SKIP else 2, 2):
                for half in range(2):
                    vb_i = 3 + gslot["v"] % 3
                    gslot["v"] += 1
                    vb = banks[vb_i]
                    k.pe([lambda e, kc=kc, su=su, half=half, vb=vb: e.matmul(
                        vb, lhsT=hT[s_][:, kc, su * 128:(su + 1) * 128], rhs=wkvr[:, kc, 1024 + half * 512:1024 + (half + 1) * 512],
                        start=(kc == 0), stop=(kc == 15)) for kc in range(16)],
                        reads=[B_w, BhT[s_]], writes=[Bbank[vb_i]])
                    k.op(DVE, lambda e, su=su, half=half, vb=vb, vs_=vs_: e.tensor_copy(
                        vt[vs_][:, half * 4:(half + 1) * 4, su, 0:128], vb.rearrange("p (h d) -> p h d", h=4)),
                        reads=[Bbank[vb_i]], writes=[Bvt[vs_]])
            kt0 = key0 // 128
            for h in range(0 if 'V' not in SKIP else 8, 8):
                k.dma(VV_d[h, :, kt0:kt0 + 2, :], vt[vs_][:, h, :, :], Bvt[vs_], reads=[Bvt[vs_]])
        bar = []
        for b_ in Bko + Brxo + Bvt + Bxs + Bcs + BhT + Bk32 + Bt1 + Bt2 + Bwst + [B_w] + Bbank + Btp + Bxn + Bssq + [B_junk]:
            bar.append(b_.w)
            bar += b_.rtoks()
            if b_.dsem is not None:
                bar.append((b_.dsem, b_.dsem.n))
    for E in (PE, ACT, DVE, POOL, SP):
        E.wait(bar)

    if debug and stop_after == 1 and 'D' in SKIP:
        return nc
    if debug and stop_after == 1:
        with nc.sbuf_tensor("sb_dbt", [128, 8, 129 * 8], BF16) as dbt, nc.sbuf_tensor("sb_dbf", [128, 8, 1024], F32) as dbf:
            Bd = Buf()
            k.dma(dbt[:, :, 0:1024], KT_d[:, :, 0:1024].rearrange("h p n -> p h n"), Bd, writes=[Bd])
            k.dma(dbg["KT"].rearrange("h p n -> p h n"), dbt[:, :, 0:1024], Bd, reads=[Bd])
            Bd2 = Buf()
            k.dma(dbf[:], RXL_d[:, :, 0:1024].rearrange("h p n -> p h n"), Bd2, writes=[Bd2])
            k.dma(dbg["RX"].rearrange("h p n -> p h n"), dbf[:], Bd2, reads=[Bd2])
            Bd3 = Buf()
            k.dma(dbt[:].rearrange("p h (t d) -> p h t d", d=129), VV_d[:, :, 0:8, :].rearrange("h p t d -> p h t d"), Bd3,
                  reads=[Bd], writes=[Bd3])
            k.dma(dbg["VV"].rearrange("h p t d -> p h t d"), dbt[:].rearrange("p h (t d) -> p h t d", d=129), Bd3, reads=[Bd3])
            fin = [(b_.dsem, b_.dsem.n) for b_ in (Bd, Bd2, Bd3)]
            SP.wait(fin)
        return nc


    EXTP = 17 * 128
    B_QT, B_GZ, B_AT = Buf("QT"), Buf("GZ"), Buf("AT")
    dl = sb("dl", [128, 256], F32)
    subg8 = sb("subg8", [128, 128], F32)
    lamv = sb("lamv", [128, 8], F32)
    mk = sb("mk", [128, 32], F32)
    B_misc = Buf("misc")
    k.dma(dl[:], dlam_d[:, :], B_misc, writes=[B_misc])
    k.dma(subg8[:], subg_d[:, :], B_misc, writes=[B_misc])
    k.dma(mk[:], mk_d[:, :], B_misc, writes=[B_misc])
    k.op(DVE, lambda e: e.tensor_tensor(out=dl[:, 0:64], in0=dl[:, 0:64], in1=dl[:, 64:128], op=ALU.mult), reads=[B_misc], writes=[B_misc])
    k.op(DVE, lambda e: e.tensor_tensor(out=dl[:, 128:192], in0=dl[:, 128:192], in1=dl[:, 192:256], op=ALU.mult), reads=[B_misc], writes=[B_misc])
    k.op(ACT, lambda e: e.activation(out=dl[:, 64:128], in_=dl[:, 0:64], func=AF.Identity, accum_out=lamv[:, 0:1]), reads=[B_misc], writes=[B_misc])
    k.op(ACT, lambda e: e.activation(out=dl[:, 192:256], in_=dl[:, 128:192], func=AF.Identity, accum_out=lamv[:, 1:2]), reads=[B_misc], writes=[B_misc])
    k.op(ACT, lambda e: e.activation(out=lamv[:, 2:4], in_=lamv[:, 0:2], func=AF.Exp), reads=[B_misc], writes=[B_misc])
    k.op(DVE, lambda e: e.tensor_tensor(out=lamv[:, 4:5], in0=lamv[:, 3:4], in1=lamv[:, 2:3], op=ALU.subtract), reads=[B_misc], writes=[B_misc])
    k.op(DVE, lambda e: e.tensor_scalar(out=lamv[:, 4:5], in0=lamv[:, 4:5], scalar1=-LAM_INIT, scalar2=None, op0=ALU.add), reads=[B_misc], writes=[B_misc])
    k.op(DVE, lambda e: e.tensor_scalar(out=subg8[:], in0=subg8[:], scalar1=1.0 - LAM_INIT, scalar2=None, op0=ALU.mult), reads=[B_misc], writes=[B_misc])

    X1_d = nc.dram_tensor("X1s", [17 * 128, D], F32).ap()
    XR_d = nc.dram_tensor("XRs", [128, NKEY], F32).ap()
    WB_d = nc.dram_tensor("WBs", [3, NJ, 128, 2048], BF16).ap()
    es_mix = ExitStack()
    QT = es_mix.enter_context(nc.sbuf_tensor("sb_QT", [128, 8, EXTP], BF16))
    GZ = es_mix.enter_context(nc.sbuf_tensor("sb_GZ", [128, 8, EXTP], BF16))
    AT = es_mix.enter_context(nc.sbuf_tensor("sb_AT", [128, 8, EXTP], BF16))
    with ExitStack() as es:
        def sc(name, shape, dt):
            return es.enter_context(nc.sbuf_tensor("sb_" + name, shape, dt))
        wqz = sc("wqz", [128, 16, 1024], BF16)
        wst0_ = sc("b_wst0", [128, 1024], F32)
        wst = [wst0_, wst0_]
        xs0_ = sc("b_xs0", [128, 2, D], F32)
        xs = [xs0_, xs0_]
        hT = [sc("b_hT0", [128, 16, TT], BF16), sc("b_hT1", [128, 16, TT], BF16)]
        cs = [sc("b_cs0", [128, 2, TT], F32), sc("b_cs1", [128, 2, TT], F32)]
        k32 = [sc("b_k32a", [128, TT], F32), sc("b_k32b", [128, TT], F32)]
        t1 = [sc("b_t1a", [128, TT], F32), sc("b_t1b", [128, TT], F32)]
        t2 = [sc("b_t2a", [128, TT], F32), sc("b_t2b", [128, TT], F32)]
        Bwst, Bxs, BhT, Bcs, Bk32, Bt1, Bt2 = ([Buf(), Buf()] for _ in range(7))
        Bxs[1] = Bxs[0]
        Bwst[1] = Bwst[0]
        B_w = Buf("wqz")
        NOT = 9

        def load_own(ti):
            s_ = ti % 2
            nsub = 2 if ti < 8 else 1
            ncol = nsub * 128
            k.dma(xs[s_][:, 0:nsub, :], xo_d[ti * TT: ti * TT + ncol, :].rearrange("(s p) d -> p s d", p=128), Bxs[s_], writes=[Bxs[s_]])
            k.dma(cs[s_][:, 0, 0:ncol], coso_d[:, ti * TT: ti * TT + ncol], Bcs[s_], writes=[Bcs[s_]])
            k.dma(cs[s_][:, 1, 0:ncol], sino_d[:, ti * TT: ti * TT + ncol], Bcs[s_], writes=[])
            Bcs[s_].w = (Bcs[s_].dsem, Bcs[s_].dsem.n)

        go = 0
        for pas in range(2):
            c0w = 0 if pas == 0 else 4096
            for kc in range(16):
                k.dma(wst[0][:], win_d[kc * 128:(kc + 1) * 128, c0w:c0w + 1024], Bwst[0], writes=[Bwst[0]])
                k.op(ACT, lambda e, kc=kc: e.activation(out=wqz[:, kc, :], in_=wst[0][:], func=AF.Identity), reads=[Bwst[0]], writes=[B_w])
            load_own(0)
            for ti in range(NOT):
                s_ = ti % 2
                nsub = 2 if ti < 8 else 1
                ncol = nsub * 128
                col0 = ti * TT
                for su in range(nsub):
                    norm_transpose(xs[s_][:, su, :], Bxs[s_], 128, a1, modx,
                                   lambda kc, su=su, s_=s_: hT[s_][:, kc, su * 128:(su + 1) * 128], BhT[s_])
                if ti + 1 < NOT:
                    load_own(ti + 1)
                for h in range(8 if pas == 0 else 0):
                    oslot = go % 2
                    go += 1
                    ob = banks[oslot][:, 0:ncol]
                    Bo = Bbank[oslot]
                    k.pe([lambda e, kc=kc, h=h, ob=ob: e.matmul(ob, lhsT=wqz[:, kc, h * 128:(h + 1) * 128], rhs=hT[s_][:, kc, 0:ncol],
                                                                start=(kc == 0), stop=(kc == 15)) for kc in range(16)],
                         reads=[B_w, BhT[s_]], writes=[Bo])
                    q_ = h % 2
                    k.op(ACT, lambda e, ob=ob, q_=q_: e.activation(out=k32[q_][:, 0:ncol], in_=ob, func=AF.Identity), reads=[Bo], writes=[Bk32[q_]])
                    pb = banks[2][:, 0:ncol]
                    k.pe([lambda e, pb=pb, q_=q_: e.matmul(pb, lhsT=permT[:], rhs=k32[q_][:, 0:ncol], start=True, stop=True)],
                         reads=[Bk32[q_], B_const], writes=[Bbank[2]])
                    k.op(DVE, lambda e, pb=pb, q_=q_: e.tensor_tensor(out=t1[q_][:, 0:ncol], in0=pb, in1=cs[s_][:, 1, 0:ncol], op=ALU.mult),
                         reads=[Bbank[2], Bcs[s_]], writes=[Bt1[q_]])
                    k.op(POOL, lambda e, q_=q_: e.tensor_tensor(out=t2[q_][:, 0:ncol], in0=k32[q_][:, 0:ncol], in1=cs[s_][:, 0, 0:ncol], op=ALU.mult),
                         reads=[Bk32[q_], Bcs[s_]], writes=[Bt2[q_]])
                    k.op(POOL, lambda e, q_=q_, h=h: e.tensor_tensor(out=QT[:, h, col0:col0 + ncol], in0=t1[q_][:, 0:ncol], in1=t2[q_][:, 0:ncol], op=ALU.add),
                         reads=[Bt1[q_], Bt2[q_]], writes=[B_QT])
                for b in range(8 if pas == 1 else 0):
                    oslot = go % 2
                    go += 1
                    ob = banks[oslot][:, 0:ncol]
                    Bo = Bbank[oslot]
                    k.pe([lambda e, kc=kc, b=b, ob=ob: e.matmul(ob, lhsT=wqz[:, kc, b * 128:(b + 1) * 128], rhs=hT[s_][:, kc, 0:ncol],
                                                                start=(kc == 0), stop=(kc == 15)) for kc in range(16)],
                         reads=[B_w, BhT[s_]], writes=[Bo])
                    k.op(ACT, lambda e, ob=ob, b=b: e.activation(out=GZ[:, b, col0:col0 + ncol], in_=ob, func=AF.Gelu_apprx_tanh),
                         reads=[Bo], writes=[B_GZ])
        bar = []
        for b_ in Bwst + Bxs + BhT + Bcs + Bk32 + Bt1 + Bt2 + [B_w, B_QT, B_GZ] + Bbank + Bxn + Bssq + [B_junk]:
            bar.append(b_.w)
            bar += b_.rtoks()
    for E in (PE, ACT, DVE, POOL, SP):
        E.wait(bar)

    with ExitStack() as es:
        def sc(name, shape, dt):
            return es.enter_context(nc.sbuf_tensor("sb_" + name, shape, dt))
        Kh = sc("Kh", [128, NKEY], BF16)
        Vh = sc("Vh", [128, NKT, 129], BF16)
        pt = [sc("pt%d" % i, [128, 2, 512], BF16) for i in range(3)]
        Bpt = [Buf() for _ in range(3)]
        o32 = [sc("o32_%d" % i, [128, 128], F32) for i in range(2)]
        Bo32 = [Buf(), Buf()]
        onb = [sc("onb%d" % i, [128, 128], BF16) for i in range(2)]
        Bonb = [Buf(), Buf()]
        r4 = [sc("r4_%d" % i, [128, 8], F32) for i in range(2)]
        Br4 = [Buf(), Buf()]
        B_K, B_V = Buf("Kh"), Buf("Vh")
        pc_f = sc("pc_f", [128, 1024], F32)
        pc_b = sc("pc_b", [128, 1024], BF16)
        B_pcf, B_pcb = Buf(), Buf()
        pc = {"i": 0}
        wsrc = [wup_d, wgate_d, wdown_d]

        def precast_step():
            i_ = pc["i"]
            if i_ >= 3 * NJ * 2:
                return
            pc["i"] += 1
            wi_, j_, hf_ = i_ // (NJ * 2), (i_ // 2) % NJ, i_ % 2
            k.dma(pc_f[:], wsrc[wi_][j_, :, hf_ * 1024:(hf_ + 1) * 1024], B_pcf, writes=[B_pcf])
            k.op(POOL, lambda e: e.tensor_copy(pc_b[:], pc_f[:]), reads=[B_pcf], writes=[B_pcb])
            k.dma(WB_d[wi_, j_, :, hf_ * 1024:(hf_ + 1) * 1024], pc_b[:], B_pcb, reads=[B_pcb])
        nheads = 8 if stop_after > 3 else 1
        qblocks = [(q0, 512) for q0 in range(0, 2048, 512)] + [(2048, 2)]
        if stop_after == 3 and debug:
            qblocks = [(0, 512), (2048, 2)]
        k.op(DVE, lambda e: e.memset(AT[:, :, EXT:EXTP], 0.0), writes=[B_AT])
        it = 0
        fz = 0
        for h in range(nheads):
            k.dma(Kh[:], KT_d[h, :, :], B_K, writes=[B_K])
            k.dma(Vh[:], VV_d[h, :, :, :], B_V, writes=[B_V])
            for (q0, nq) in qblocks:
                nqs = (nq + 127) // 128
                rows = min(128, nq)
                for _ in range(7):
                    precast_step()
                def emit_s(kt, it_):
                    sbuf_i = it_ % 2
                    r = it_ % 3
                    b0 = 2 * sbuf_i
                    k.pe([lambda e, c=c, kt=kt, b0=b0: e.matmul(banks[b0 + c][:, 0:nq], lhsT=Kh[c * 64:(c + 1) * 64, kt * 128:(kt + 1) * 128],
                                                               rhs=QT[c * 64:(c + 1) * 64, h, q0:q0 + nq], start=True, stop=True) for c in range(2)],
                         reads=[B_K, B_QT], writes=[Bbank[b0], Bbank[b0 + 1]])
                    sview = psall[:, b0 * 512:(b0 + 2) * 512].rearrange("p (c n) -> p c n", c=2)[:, :, 0:nq]
                    k.op(ACT, lambda e, sview=sview, r=r: e.activation(out=pt[r][:, :, 0:nq], in_=sview, func=AF.Exp, scale=0.125),
                         reads=[Bbank[b0], Bbank[b0 + 1]], writes=[Bpt[r]])

                def emit_pv(kt, it_):
                    r = it_ % 3
                    fns = []
                    accs = []
                    for c in range(2):
                        for qs in range(nqs):
                            ab = 4 + 2 * c + qs // 2
                            co = (qs % 2) * 256
                            if Bbank[ab] not in accs:
                                accs.append(Bbank[ab])
                            fns.append(lambda e, c=c, qs=qs, ab=ab, co=co, kt=kt, r=r: e.matmul(
                                banks[ab][0:rows, co:co + 129], lhsT=pt[r][:, c, qs * 128:qs * 128 + rows], rhs=Vh[:, kt, :],
                                start=(kt == 0 and qs % 2 == 0), stop=(kt == NKT - 1)))
                    k.pe(fns, reads=[Bpt[r], B_V], writes=accs)

                it0 = it
                for kt in range(NKT):
                    emit_s(kt, it0 + kt)
                    if kt >= 1:
                        emit_pv(kt - 1, it0 + kt - 1)
                emit_pv(NKT - 1, it0 + NKT - 1)
                it = it0 + NKT
                R_ = rows
                for qs in range(nqs):
                    f = fz % 2
                    fz += 1
                    a0 = banks[4 + qs // 2][0:R_, (qs % 2) * 256:(qs % 2) * 256 + 129]
                    a1_ = banks[6 + qs // 2][0:R_, (qs % 2) * 256:(qs % 2) * 256 + 129]
                    k.op(DVE, lambda e, f=f, a0=a0: e.reciprocal(out=r4[f][0:R_, 0:1], in_=a0[:, 128:129]), reads=[Bbank[4 + qs // 2]], writes=[Br4[f]])
                    k.op(DVE, lambda e, f=f, a1_=a1_: e.reciprocal(out=r4[f][0:R_, 1:2], in_=a1_[:, 128:129]), reads=[Bbank[6 + qs // 2]], writes=[Br4[f]])
                    k.op(DVE, lambda e, f=f: e.tensor_tensor(out=r4[f][0:R_, 2:3], in0=r4[f][0:R_, 1:2], in1=lamv[0:R_, 4:5], op=ALU.mult),
                         reads=[Br4[f], B_misc], writes=[Br4[f]])
                    k.op(DVE, lambda e, f=f, a0=a0: e.tensor_scalar(out=o32[f][0:R_, :], in0=a0[:, 0:128], scalar1=r4[f][0:R_, 0:1], scalar2=None, op0=ALU.mult),
                         reads=[Bbank[4 + qs // 2], Br4[f]], writes=[Bo32[f]])
                    k.op(DVE, lambda e, f=f, a1_=a1_: e.scalar_tensor_tensor(out=o32[f][0:R_, :], in0=a1_[:, 0:128], scalar=r4[f][0:R_, 2:3], in1=o32[f][0:R_, :],
                                                                           op0=ALU.mult, op1=ALU.add),
                         reads=[Bbank[6 + qs // 2], Br4[f], Bo32[f]], writes=[Bo32[f]])
                    k.op(ACT, lambda e, f=f: e.activation(out=junk[0:R_, 0:128], in_=o32[f][0:R_, :], func=AF.Square, accum_out=r4[f][0:R_, 3:4]),
                         reads=[Bo32[f]], writes=[B_junk, Br4[f]])
                    k.op(ACT, lambda e, f=f: e.activation(out=r4[f][0:R_, 4:5], in_=r4[f][0:R_, 3:4], func=AF.Ln, scale=1.0 / 128, bias=epsc[0:R_, 1:2]),
                         reads=[Br4[f], B_const], writes=[Br4[f]])
                    k.op(ACT, lambda e, f=f: e.activation(out=r4[f][0:R_, 5:6], in_=r4[f][0:R_, 4:5], func=AF.Exp, scale=-0.5),
                         reads=[Br4[f]], writes=[Br4[f]])
                    k.op(DVE, lambda e, f=f: e.scalar_tensor_tensor(out=onb[f][0:R_, :], in0=o32[f][0:R_, :], scalar=r4[f][0:R_, 5:6], in1=subg8[0:R_, :],
                                                                   op0=ALU.mult, op1=ALU.mult),
                         reads=[Bo32[f], Br4[f], B_misc], writes=[Bonb[f]])
                    tb = banks[0].bitcast(BF16)
                    k.pe([lambda e, f=f, tb=tb: e.transpose(tb[:, 0:R_], onb[f][0:R_, :], identb[0:R_, 0:R_])], reads=[Bonb[f], B_const], writes=[Bbank[0]])
                    k.op(ACT, lambda e, tb=tb, qs=qs: e.activation(out=AT[:, h, q0 + qs * 128: q0 + qs * 128 + R_], in_=tb[:, 0:R_], func=AF.Identity),
                         reads=[Bbank[0]], writes=[B_AT])
        while pc["i"] < 3 * NJ * 2:
            precast_step()
        bar = []
        for b_ in [B_K, B_V, B_QT, B_AT, B_pcf, B_pcb] + Bpt + Bo32 + Bonb + Br4 + Bbank + [B_junk]:
            bar.append(b_.w)
            bar += b_.rtoks()
            if b_.dsem is not None:
                bar.append((b_.dsem, b_.dsem.n))
    for E in (PE, ACT, DVE, POOL, SP):
        E.wait(bar)
    if debug and stop_after == 3:
        Bd = Buf()
        k.dma(dbg["AT"][:, :], AT[:, 0, :], Bd, reads=[B_AT])
        k.dma(dbg["QT"][:, :], QT[:, 0, :], Bd, reads=[B_QT])
        k.dma(dbg["GZ"][:, :], GZ[:, 0, :], Bd, reads=[B_GZ])
        SP.wait([(Bd.dsem, Bd.dsem.n)])
        return nc


    RT, B_RT = QT, B_QT
    k.op(POOL, lambda e: e.memset(RT[:, :, EXT:EXTP], 0.0), writes=[B_RT])
    TN = 1024
    with ExitStack() as es:
        def sc(name, shape, dt):
            return es.enter_context(nc.sbuf_tensor("sb_" + name, shape, dt))
        rgwb = sc("rgwb", [128, 32 * 128], BF16)
        rgst = sc("rgst", [128, 1024], F32)
        cst = sc("cst", [128, 72], F32)
        xt2 = [sc("r_xt%d" % i, [128, TN + 4], F32) for i in range(2)]
        xr2 = [sc("r_xr%d" % i, [128, TN], F32) for i in range(2)]
        xrb2 = [sc("r_xrb%d" % i, [128, TN], BF16) for i in range(2)]
        EA2 = [sc("r_EA%d" % i, [128, TN], F32) for i in range(2)]
        EI2 = [sc("r_EI%d" % i, [128, TN], F32) for i in range(2)]
        A_2 = [sc("r_A%d" % i, [128, TN], F32) for i in range(2)]
        A22 = [sc("r_A2%d" % i, [128, TN], F32) for i in range(2)]
        H2 = [sc("r_H%d" % i, [128, TN], F32) for i in range(2)]
        Hacc = sc("r_Hacc", [128, 2052], F32)
        hcar = sc("r_hcar", [128, 2], F32)
        B_rgw, B_rgst, B_cst, B_Hacc, B_hcar = (Buf() for _ in range(5))
        Bxt2, Bxr2, Bxrb2, BEA2, BEI2, BA2, BA22, BH2 = ([Buf(), Buf()] for _ in range(8))
        for q in range(4):
            k.dma(rgst[:], rgw_d[:, q * 1024:(q + 1) * 1024], B_rgst, writes=[B_rgst])
            k.op(ACT, lambda e, q=q: e.activation(out=rgwb[:, q * 1024:(q + 1) * 1024], in_=rgst[:], func=AF.Identity), reads=[B_rgst], writes=[B_rgw])
        k.op(DVE, lambda e: e.memset(cst[:, 64:65], 1.0), writes=[B_cst])
        k.op(ACT, lambda e: e.activation(out=cst[:, 0:16], in_=vecs[:, V_RLAM:V_RLAM + 16], func=AF.Exp, scale=-1.0), reads=[B_vecs], writes=[B_cst])
        k.op(ACT, lambda e: e.activation(out=cst[:, 0:16], in_=cst[:, 0:16], func=AF.Ln, scale=1.0, bias=cst[:, 64:65]), reads=[B_cst], writes=[B_cst])
        k.op(DVE, lambda e: e.tensor_scalar(out=cst[:, 16:32], in0=cst[:, 0:16], scalar1=-16.0, scalar2=None, op0=ALU.mult), reads=[B_cst], writes=[B_cst])
        k.op(DVE, lambda e: e.tensor_scalar(out=cst[:, 0:16], in0=cst[:, 0:16], scalar1=-8.0, scalar2=None, op0=ALU.mult), reads=[B_cst], writes=[B_cst])
        k.op(DVE, lambda e: e.tensor_scalar(out=cst[:, 32:48], in0=vecs[:, V_RBA:V_RBA + 16], scalar1=-1.0, scalar2=None, op0=ALU.mult), reads=[B_vecs], writes=[B_cst])
        k.op(DVE, lambda e: e.tensor_scalar(out=cst[:, 48:64], in0=vecs[:, V_RBI:V_RBI + 16], scalar1=-1.0, scalar2=None, op0=ALU.mult), reads=[B_vecs], writes=[B_cst])
        nblocks = 8 if stop_after > 2 else 1
        gs = 0
        tc_ = 0
        B_XRd = [Buf() for _ in range(1 + S // TN)]
        for b in range(nblocks):
            k.op(DVE, lambda e: e.memset(Hacc[:], 0.0), writes=[B_Hacc])
            for d in range(2):
                idx = d * 8 + b
                wa = rgwb[:, ((0 * 2 + d) * 8 + b) * 128:((0 * 2 + d) * 8 + b + 1) * 128]
                wi = rgwb[:, ((1 * 2 + d) * 8 + b) * 128:((1 * 2 + d) * 8 + b + 1) * 128]
                k.op(DVE, lambda e: e.memset(hcar[:, 0:1], 0.0), writes=[B_hcar])
                nlt = S // TN
                tiles = [("c", 0)] + [("l", t) for t in (range(nlt) if d == 0 else range(nlt - 1, -1, -1))]
                def tile_geom(kind, t):
                    n = 256 if kind == "c" else TN
                    seqlen = NCTX if kind == "c" else S
                    src = RXC_d if kind == "c" else RXL_d
                    t0 = t * TN
                    lo, hi = max(0, t0 - 2), min(seqlen, t0 + n + 1)
                    xslot = 0 if kind == "c" else 1 + t
                    xoff = 0 if kind == "c" else NCTX + t0
                    return n, src, t0, lo, hi, xslot, xoff

                def emit_load(kind, t, u):
                    n, src, t0, lo, hi, xslot, xoff = tile_geom(kind, t)
                    if d == 0:
                        xt = xt2[u]
                        k.op(DVE, lambda e, xt=xt: e.memset(xt[:, 0:2], 0.0), writes=[Bxt2[u]])
                        k.op(DVE, lambda e, xt=xt, n=n: e.memset(xt[:, n + 2:n + 3], 0.0), writes=[Bxt2[u]])
                        k.dma(xt[:, lo - t0 + 2: hi - t0 + 2], src[b, :, lo:hi], Bxt2[u], writes=[Bxt2[u]])
                    else:
                        k.dma(xr2[u][:, 0:n], XR_d[:, xoff:xoff + n], Bxr2[u], reads=[B_XRd[xslot]], writes=[Bxr2[u]])

                emit_load(tiles[0][0], tiles[0][1], tc_ % 2)
                for ti_, (kind, t) in enumerate(tiles):
                    u = tc_ % 2
                    tc_ += 1
                    xt, xr, xrb, EA, EI, A_, A2, H = xt2[u], xr2[u], xrb2[u], EA2[u], EI2[u], A_2[u], A22[u], H2[u]
                    B_xt, B_xr, B_xrb, B_EA, B_EI, B_A, B_A2, B_H = Bxt2[u], Bxr2[u], Bxrb2[u], BEA2[u], BEI2[u], BA2[u], BA22[u], BH2[u]
                    n, src, t0, lo, hi, xslot, xoff = tile_geom(kind, t)
                    if d == 0:
                        w_ = lambda j: vecs[:, V_RCW + j * 8 + b: V_RCW + j * 8 + b + 1]
                        k.op(ACT, lambda e, n=n, xt=xt, xr=xr: e.activation(out=xr[:, 0:n], in_=xt[:, 0:n], func=AF.Identity, scale=w_(0),
                                                                           bias=vecs[:, V_RCB + b:V_RCB + b + 1]),
                             reads=[B_xt, B_vecs], writes=[B_xr])
                        for j in range(1, 4):
                            k.op(DVE, lambda e, n=n, j=j, xt=xt, xr=xr: e.scalar_tensor_tensor(out=xr[:, 0:n], in0=xt[:, j:j + n], scalar=w_(j), in1=xr[:, 0:n],
                                                                                              op0=ALU.mult, op1=ALU.add),
                                 reads=[B_xt, B_vecs, B_xr], writes=[B_xr])
                    if ti_ + 1 < len(tiles):
                        emit_load(tiles[ti_ + 1][0], tiles[ti_ + 1][1], tc_ % 2)
                    if d == 0:
                        k.dma(XR_d[:, xoff:xoff + n], xr[:, 0:n], B_xr, reads=[B_xr], writes=[B_XRd[xslot]])
                    k.op(ACT, lambda e, n=n, xr=xr, xrb=xrb: e.activation(out=xrb[:, 0:n], in_=xr[:, 0:n], func=AF.Identity), reads=[B_xr], writes=[B_xrb])
                    base = 4 * (gs % 2)
                    gs += 1
                    for (w_g, off, dstE, Bd_, nb_) in ((wa, 0, EA, B_EA, cst[:, 32 + idx:33 + idx]), (wi, 2, EI, B_EI, cst[:, 48 + idx:49 + idx])):
                        fns = []
                        for m0 in range(0, n, 512):
                            mw = min(512, n - m0)
                            fns.append(lambda e, w_g=w_g, m0=m0, mw=mw, off=off, base=base, xrb=xrb: e.matmul(
                                psall[:, (base + off) * 512 + m0:(base + off) * 512 + m0 + mw], lhsT=w_g, rhs=xrb[:, m0:m0 + mw],
                                start=True, stop=True))
                        k.pe(fns, reads=[B_rgw, B_xrb], writes=[Bbank[base + off], Bbank[base + off + 1]])
                        k.op(ACT, lambda e, off=off, base=base, n=n, dstE=dstE, nb_=nb_: e.activation(
                            out=dstE[:, 0:n], in_=psall[:, (base + off) * 512:(base + off) * 512 + n], func=AF.Exp, scale=-1.0, bias=nb_),
                            reads=[Bbank[base + off], Bbank[base + off + 1], B_cst], writes=[Bd_])
                    k.op(ACT, lambda e, n=n, EA=EA: e.activation(out=EA[:, 0:n], in_=EA[:, 0:n], func=AF.Ln, scale=1.0, bias=cst[:, 64:65]), reads=[B_EA, B_cst], writes=[B_EA])
                    k.op(ACT, lambda e, n=n, EA=EA: e.activation(out=EA[:, 0:n], in_=EA[:, 0:n], func=AF.Exp, scale=-1.0), reads=[B_EA], writes=[B_EA])
                    k.op(ACT, lambda e, n=n, EI=EI: e.activation(out=EI[:, 0:n], in_=EI[:, 0:n], func=AF.Ln, scale=1.0, bias=cst[:, 64:65]), reads=[B_EI, B_cst], writes=[B_EI])
                    k.op(ACT, lambda e, n=n, EI=EI: e.activation(out=EI[:, 0:n], in_=EI[:, 0:n], func=AF.Exp, scale=-1.0), reads=[B_EI], writes=[B_EI])
                    k.op(ACT, lambda e, n=n, A_=A_, EA=EA: e.activation(out=A_[:, 0:n], in_=EA[:, 0:n], func=AF.Exp, scale=cst[:, idx:idx + 1]), reads=[B_EA, B_cst], writes=[B_A])
                    k.op(DVE, lambda e, n=n, A2=A2, A_=A_: e.tensor_tensor(out=A2[:, 0:n], in0=A_[:, 0:n], in1=A_[:, 0:n], op=ALU.mult), reads=[B_A], writes=[B_A2])
                    k.op(ACT, lambda e, n=n, A2=A2: e.activation(out=A2[:, 0:n], in_=A2[:, 0:n], func=AF.Ln, scale=-1.0, bias=cst[:, 64:65]), reads=[B_A2, B_cst], writes=[B_A2])
                    k.op(ACT, lambda e, n=n, A2=A2: e.activation(out=A2[:, 0:n], in_=A2[:, 0:n], func=AF.Exp, scale=0.5), reads=[B_A2], writes=[B_A2])
                    k.op(DVE, lambda e, n=n, EI=EI, xr=xr: e.tensor_tensor(out=EI[:, 0:n], in0=EI[:, 0:n], in1=xr[:, 0:n], op=ALU.mult), reads=[B_EI, B_xr], writes=[B_EI])
                    k.op(DVE, lambda e, n=n, EI=EI, A2=A2: e.tensor_tensor(out=EI[:, 0:n], in0=EI[:, 0:n], in1=A2[:, 0:n], op=ALU.mult), reads=[B_EI, B_A2], writes=[B_EI])
                    if d == 0:
                        k.op(DVE, lambda e, n=n, H=H, A_=A_, EI=EI: e.tensor_tensor_scan(out=H[:, 0:n], data0=A_[:, 0:n], data1=EI[:, 0:n], initial=hcar[:, 0:1],
                                                                                      op0=ALU.mult, op1=ALU.add), reads=[B_A, B_EI, B_hcar], writes=[B_H])
                        k.op(DVE, lambda e, n=n, H=H: e.tensor_copy(hcar[:, 0:1], H[:, n - 1:n]), reads=[B_H], writes=[B_hcar])
                    else:
                        k.op(DVE, lambda e, n=n, H=H, A_=A_, EI=EI: e.tensor_tensor_scan(out=H[:, 0:n][:, ::-1], data0=A_[:, 0:n][:, ::-1], data1=EI[:, 0:n][:, ::-1],
                                                                                      initial=hcar[:, 0:1], op0=ALU.mult, op1=ALU.add), reads=[B_A, B_EI, B_hcar], writes=[B_H])
                        k.op(DVE, lambda e, H=H: e.tensor_copy(hcar[:, 0:1], H[:, 0:1]), reads=[B_H], writes=[B_hcar])
                    if kind == "l":
                        per = 2048 // TN
                        tb_, hf_ = t // per, t % per
                        k.op(DVE, lambda e, tb_=tb_, hf_=hf_, H=H: e.scalar_tensor_tensor(out=Hacc[:, hf_ * TN:(hf_ + 1) * TN], in0=H[:, 0:TN], scalar=mk[:, tb_:tb_ + 1],
                                                                                         in1=Hacc[:, hf_ * TN:(hf_ + 1) * TN], op0=ALU.mult, op1=ALU.add),
                             reads=[B_H, B_misc, B_Hacc], writes=[B_Hacc])
                        if hf_ == per - 1:
                            k.op(DVE, lambda e, tb_=tb_, H=H: e.scalar_tensor_tensor(out=Hacc[:, 2048:2049], in0=H[:, TN - 1:TN], scalar=mk[:, 8 + tb_:9 + tb_], in1=Hacc[:, 2048:2049],
                                                                                   op0=ALU.mult, op1=ALU.add), reads=[B_H, B_misc, B_Hacc], writes=[B_Hacc])
                        if hf_ == 0:
                            k.op(DVE, lambda e, tb_=tb_, H=H: e.scalar_tensor_tensor(out=Hacc[:, 2049:2050], in0=H[:, 0:1], scalar=mk[:, 16 + tb_:17 + tb_], in1=Hacc[:, 2049:2050],
                                                                                   op0=ALU.mult, op1=ALU.add), reads=[B_H, B_misc, B_Hacc], writes=[B_Hacc])
            k.op(DVE, lambda e, b=b: e.tensor_tensor(out=RT[:, b, 0:EXT], in0=Hacc[:, 0:EXT], in1=GZ[:, b, 0:EXT], op=ALU.mult),
                 reads=[B_Hacc, B_GZ], writes=[B_RT])
        bar = []
        for b_ in [B_rgw, B_rgst, B_cst, B_Hacc, B_hcar, B_RT, B_GZ] + Bxt2 + Bxr2 + Bxrb2 + BEA2 + BEI2 + BA2 + BA22 + BH2 + Bbank:
            bar.append(b_.w)
            bar += b_.rtoks()
            if b_.dsem is not None:
                bar.append((b_.dsem, b_.dsem.n))
    for E in (PE, ACT, DVE, POOL, SP):
        E.wait(bar)
    if debug and stop_after == 2:
        Bd = Buf()
        k.dma(dbg["QT"][:, :], RT[:, 0, :], Bd, reads=[B_RT])
        SP.wait([(Bd.dsem, Bd.dsem.n)])
        return nc


    gz32 = GZ[:].rearrange("p a b -> p (a b)").bitcast(F32)
    x1t = gz32[:, 0:2048]
    xst = gz32[:, 2048:4096]
    g1b = gz32[:, 4096:6144]
    dgt = gz32[:, 6144:6272]
    B_x1t, B_xst, B_g1b, B_dg, B_X1 = Buf(), Buf(), Buf(), Buf(), Buf()

    def row_broadcast(dst, Bdst, col0):
        for q in range(4):
            for kc4 in range(4):
                kc = q * 4 + kc4
                k.op(DVE, lambda e, kc=kc: e.tensor_scalar(out=dgt, in0=identf[:], scalar1=modx[:, col0 + kc:col0 + kc + 1], scalar2=None, op0=ALU.mult),
                     reads=[B_const, B_mod], writes=[B_dg])
                k.pe([lambda e, kc4=kc4: e.matmul(banks[0][:, kc4 * 128:(kc4 + 1) * 128], lhsT=ones_f[:], rhs=dgt, start=True, stop=True)],
                     reads=[B_dg, B_const], writes=[Bbank[0]])
            k.op(ACT, lambda e, q=q: e.activation(out=dst[:, q * 512:(q + 1) * 512], in_=banks[0], func=AF.Identity), reads=[Bbank[0]], writes=[Bdst])

    row_broadcast(g1b, B_g1b, 32)
    with ExitStack() as es:
        def sc(name, shape, dt):
            return es.enter_context(nc.sbuf_tensor("sb_" + name, shape, dt))
        wo = sc("wo", [128, 16, D], BF16)
        B_wo = Buf()
        for ch in range(16):
            for hf in range(2):
                k.dma(xst[:, 0:1024], wout_d[ch * 128:(ch + 1) * 128, hf * 1024:(hf + 1) * 1024], B_xst, writes=[B_xst])
                k.op(ACT, lambda e, ch=ch, hf=hf: e.activation(out=wo[:, ch, hf * 1024:(hf + 1) * 1024], in_=xst[:, 0:1024], func=AF.Identity), reads=[B_xst], writes=[B_wo])
        for ts in range(17):
            k.dma(xst, xo_d[ts * 128:(ts + 1) * 128, :], B_xst, writes=[B_xst])
            for nb in range(4):
                k.pe([lambda e, ch=ch, nb=nb, ts=ts: e.matmul(banks[nb], lhsT=(AT if ch < 8 else RT)[:, ch % 8, ts * 128:(ts + 1) * 128],
                                                              rhs=wo[:, ch, nb * 512:(nb + 1) * 512], start=(ch == 0), stop=(ch == 15)) for ch in range(16)],
                     reads=[B_AT, B_RT, B_wo], writes=[Bbank[nb]])
            k.op(DVE, lambda e: e.tensor_tensor(out=x1t, in0=psall[:, 0:2048], in1=g1b, op=ALU.mult),
                 reads=[Bbank[0], Bbank[1], Bbank[2], Bbank[3], B_g1b], writes=[B_x1t])
            k.op(DVE, lambda e: e.tensor_tensor(out=x1t, in0=x1t, in1=xst, op=ALU.add), reads=[B_x1t, B_xst], writes=[B_x1t])
            k.dma(X1_d[ts * 128:(ts + 1) * 128, :], x1t, B_x1t, reads=[B_x1t], writes=[B_X1])
        bar = []
        for b_ in [B_wo, B_x1t, B_xst, B_g1b, B_dg, B_AT, B_RT, B_X1] + Bbank:
            bar.append(b_.w)
            bar += b_.rtoks()
            if b_.dsem is not None:
                bar.append((b_.dsem, b_.dsem.n))
    for E in (PE, ACT, DVE, POOL, SP):
        E.wait(bar)
    es_mix.close()
    if debug and stop_after == 4:
        return nc

    with ExitStack() as es:
        def sc(name, shape, dt):
            return es.enter_context(nc.sbuf_tensor("sb_" + name, shape, dt))
        h2T = sc("h2T", [128, 16, EXT], BF16)
        actT = sc("actT", [128, NJ, 512], BF16)
        hTh = actT[:, 0:4, :].rearrange("p a (b c) -> p (a b) c", c=128)
        x1t = sc("f_x1t", [128, D], F32)
        x2t = sc("f_x2t", [128, D], F32)
        g2b = sc("f_g2b", [128, D], F32)
        fgb = sc("f_fgb", [128, D], F32)
        wug = [sc("f_wug%d" % i, [128, 2, D], BF16) for i in range(3)]
        wdb = [sc("f_wd%d" % i, [128, D], BF16) for i in range(3)]
        cvt = sc("f_cvt", [128, 512], F32)
        glt = sc("f_glt", [128, 512], F32)
        fr = sc("f_fr", [128, 4], F32)
        B_h2T, B_hTh, B_act, B_x1t, B_x2t, B_g2b, B_fgb, B_cvt, B_glt, B_fr = (Buf() for _ in range(10))
        B_hTh = B_act
        Bwst = []
        Bwug = [Buf(), Buf(), Buf()]
        Bwd = [Buf(), Buf(), Buf()]
        dgt = x2t[:, 0:128]
        B_dg = B_x2t

        def row_broadcast2(dst, Bdst, col0):
            for q in range(4):
                for kc4 in range(4):
                    kc = q * 4 + kc4
                    k.op(DVE, lambda e, kc=kc: e.tensor_scalar(out=dgt, in0=identf[:], scalar1=modx[:, col0 + kc:col0 + kc + 1], scalar2=None, op0=ALU.mult),
                         reads=[B_const, B_mod], writes=[B_dg])
                    k.pe([lambda e, kc4=kc4: e.matmul(banks[0][:, kc4 * 128:(kc4 + 1) * 128], lhsT=ones_f[:], rhs=dgt, start=True, stop=True)],
                         reads=[B_dg, B_const], writes=[Bbank[0]])
                k.op(ACT, lambda e, q=q: e.activation(out=dst[:, q * 512:(q + 1) * 512], in_=banks[0], func=AF.Identity), reads=[Bbank[0]], writes=[Bdst])

        row_broadcast2(g2b, B_g2b, 80)
        k.dma(fgb[:], fing_d[:, :], B_fgb, writes=[B_fgb])
        for ts in range(17):
            k.dma(x1t[:], X1_d[ts * 128:(ts + 1) * 128, :], B_x1t, writes=[B_x1t])
            if ts < 16:
                norm_transpose(x1t[:], B_x1t, 128, a2, modx, lambda kc, ts=ts: h2T[:, kc, 1 + ts * 128: 1 + (ts + 1) * 128], B_h2T, boff=48)
            else:
                norm_transpose(x1t[:], B_x1t, 128, a2, modx, lambda kc: hTh[:, kc, :], B_hTh, boff=48)
                k.op(POOL, lambda e: e.tensor_scalar(out=h2T[:, :, 0:1], in0=hTh[:, :, 0:1], scalar1=mk[:, 24:25], scalar2=None, op0=ALU.mult),
                     reads=[B_hTh, B_misc], writes=[B_h2T])
                k.op(POOL, lambda e: e.tensor_scalar(out=h2T[:, :, 2049:2050], in0=hTh[:, :, 1:2], scalar1=mk[:, 25:26], scalar2=None, op0=ALU.mult),
                     reads=[B_hTh, B_misc], writes=[B_h2T])
        wcnt = {"s": 0, "ug": 0, "d": 0}

        def load_cast(src_ap, dst_ap, Bdst_):
            s_ = wcnt["s"] % 2
            wcnt["s"] += 1
            k.dma(wst[s_][:], src_ap, Bwst[s_], writes=[Bwst[s_]])
            k.op(ACT, lambda e, s_=s_: e.activation(out=dst_ap, in_=wst[s_][:], func=AF.Identity), reads=[Bwst[s_]], writes=[Bdst_])

        nwin = 4 if stop_after > 5 else 1
        for w in range(nwin):
            c0 = 512 * w
            for j in range(NJ):
                u_ = wcnt["ug"] % 3
                wcnt["ug"] += 1
                k.dma(wug[u_][:, 0, :], WB_d[0, j, :, :], Bwug[u_], writes=[Bwug[u_]])
                k.dma(wug[u_][:, 1, :], WB_d[1, j, :, :], Bwug[u_], writes=[])
                Bwug[u_].w = (Bwug[u_].dsem, Bwug[u_].dsem.n)
                k.pe([lambda e, kc=kc, u_=u_: e.matmul(banks[4], lhsT=wug[u_][:, 0, kc * 128:(kc + 1) * 128], rhs=h2T[:, kc, c0 + 1:c0 + 513],
                                                       start=(kc == 0), stop=(kc == 15)) for kc in range(16)],
                     reads=[Bwug[u_], B_h2T], writes=[Bbank[4]])
                k.pe([lambda e, kc=kc, u_=u_: e.matmul(banks[5], lhsT=wug[u_][:, 1, kc * 128:(kc + 1) * 128], rhs=h2T[:, kc, c0:c0 + 512],
                                                       start=(kc == 0), stop=(kc == 15)) for kc in range(16)],
                     reads=[Bwug[u_], B_h2T], writes=[Bbank[5]])
                k.pe([lambda e, kc=kc, u_=u_: e.matmul(banks[6][:, 0:2], lhsT=wug[u_][:, 1, kc * 128:(kc + 1) * 128], rhs=h2T[:, kc, c0 + 512:c0 + 514],
                                                       start=(kc == 0), stop=(kc == 15)) for kc in range(16)],
                     reads=[Bwug[u_], B_h2T], writes=[Bbank[6]])
                gps = psall[:, 5 * 512: 5 * 512 + 514]
                cw = lambda t_: vecs[:, V_FCW + t_ * 43 + j: V_FCW + t_ * 43 + j + 1]
                k.op(DVE, lambda e: e.tensor_scalar(out=cvt[:], in0=gps[:, 0:512], scalar1=cw(0), scalar2=None, op0=ALU.mult),
                     reads=[Bbank[5], Bbank[6], B_vecs], writes=[B_cvt])
                k.op(DVE, lambda e: e.scalar_tensor_tensor(out=cvt[:], in0=gps[:, 1:513], scalar=cw(1), in1=cvt[:], op0=ALU.mult, op1=ALU.add),
                     reads=[Bbank[5], Bbank[6], B_vecs, B_cvt], writes=[B_cvt])
                k.op(DVE, lambda e: e.scalar_tensor_tensor(out=cvt[:], in0=gps[:, 2:514], scalar=cw(2), in1=cvt[:], op0=ALU.mult, op1=ALU.add),
                     reads=[Bbank[5], Bbank[6], B_vecs, B_cvt], writes=[B_cvt])
                k.op(ACT, lambda e, j=j: e.activation(out=glt[:], in_=cvt[:], func=AF.Gelu_apprx_tanh, bias=vecs[:, V_FCB + j:V_FCB + j + 1]),
                     reads=[B_cvt, B_vecs], writes=[B_glt])
                k.op(DVE, lambda e, j=j: e.tensor_tensor(out=actT[:, j, :], in0=banks[4], in1=glt[:], op=ALU.mult),
                     reads=[Bbank[4], B_glt], writes=[B_act])
            for pair in range(2):
                for j in range(NJ):
                    d_ = wcnt["d"] % 3
                    wcnt["d"] += 1
                    k.dma(wdb[d_][:], WB_d[2, j, :, :], Bwd[d_], writes=[Bwd[d_]])
                    fns = []
                    for t2_ in range(2):
                        ts4 = pair * 2 + t2_
                        for nb in range(4):
                            fns.append(lambda e, j=j, ts4=ts4, nb=nb, t2_=t2_, d_=d_: e.matmul(
                                banks[t2_ * 4 + nb], lhsT=actT[:, j, ts4 * 128:(ts4 + 1) * 128], rhs=wdb[d_][:, nb * 512:(nb + 1) * 512],
                                start=(j == 0), stop=(j == NJ - 1)))
                    k.pe(fns, reads=[B_act, Bwd[d_]], writes=Bbank)
                for t2_ in range(2):
                    ts4 = pair * 2 + t2_
                    row0 = c0 + ts4 * 128
                    k.dma(x1t[:], X1_d[row0:row0 + 128, :], B_x1t, writes=[B_x1t])
                    k.op(DVE, lambda e, t2_=t2_: e.tensor_tensor(out=x2t[:], in0=psall[:, t2_ * 2048:(t2_ + 1) * 2048], in1=g2b[:], op=ALU.mult),
                         reads=Bbank + [B_g2b], writes=[B_x2t])
                    k.op(DVE, lambda e: e.tensor_tensor(out=x2t[:], in0=x2t[:], in1=x1t[:], op=ALU.add), reads=[B_x2t, B_x1t], writes=[B_x2t])
                    k.op(ACT, lambda e: e.activation(out=junk[:, :], in_=x2t[:], func=AF.Square, accum_out=fr[:, 0:1]), reads=[B_x2t], writes=[B_junk, B_fr])
                    k.op(ACT, lambda e: e.activation(out=fr[:, 1:2], in_=fr[:, 0:1], func=AF.Ln, scale=1.0 / D, bias=epsc[:, 0:1]), reads=[B_fr, B_const], writes=[B_fr])
                    k.op(ACT, lambda e: e.activation(out=fr[:, 2:3], in_=fr[:, 1:2], func=AF.Exp, scale=-0.5), reads=[B_fr], writes=[B_fr])
                    k.op(DVE, lambda e: e.scalar_tensor_tensor(out=x1t[:], in0=x2t[:], scalar=fr[:, 2:3], in1=fgb[:], op0=ALU.mult, op1=ALU.mult),
                         reads=[B_x2t, B_fr, B_fgb, B_x1t], writes=[B_x1t])
                    k.dma(out_d[row0:row0 + 128, :], x1t[:], B_x1t, reads=[B_x1t])
        fin = [(B_x1t.dsem, B_x1t.dsem.n)]
        SP.wait(fin)
        bar = []
        for b_ in [B_h2T, B_hTh, B_act, B_x1t, B_x2t, B_g2b, B_fgb, B_cvt, B_glt, B_fr] + Bwst + Bwug + Bwd + Bbank:
            bar.append(b_.w)
            bar += b_.rtoks()
    for E in (PE, ACT, DVE, POOL, SP):
        E.wait(bar)
    return nc


def rope_tables(tok):
    inv = (10000.0 ** (-np.arange(16, dtype=np.float32) / 16)).astype(np.float32)
    tok = np.asarray(tok)
    row = (tok // 64).astype(np.float32)
    col = (tok % 64).astype(np.float32)
    ang = np.stack([row[None, :] * inv[:, None], col[None, :] * inv[:, None]], 0).astype(np.float32)
    cos = np.cos(ang).astype(np.float32)
    sin = np.sin(ang).astype(np.float32)
    C = np.zeros((128, len(tok)), np.float32)
    Sn = np.zeros((128, len(tok)), np.float32)
    for c in range(2):
        for ax in range(2):
            for half in range(2):
                p0 = c * 64 + ax * 32 + half * 16
                C[p0:p0 + 16] = cos[ax]
                Sn[p0:p0 + 16] = sin[ax] * (-1.0 if half == 0 else 1.0)
    return C, Sn


def pcl(v):
    v = np.asarray(v, np.float32)
    return np.ascontiguousarray(v.reshape(-1, 128).T)


def host_inputs(inp):
    f32 = np.float32
    x = np.ascontiguousarray(inp["x"][0], f32)
    ctx = np.ascontiguousarray(inp["ctx"][0], f32)
    shared = {}
    shared["x"] = x
    shared["ctx"] = ctx
    cv = np.stack([pcl(inp["c"][0]), pcl(inp["c_ctx"])], -1)
    shared["cvec"] = np.ascontiguousarray(cv.reshape(128, 32))
    vecs = np.zeros((128, NV), f32)
    vecs[:, V_BADA:V_BADA + 96] = pcl(inp["b_ada"][0])
    vecs[:, V_N1G:V_N1G + 16] = pcl(inp["norm1_g"][0])
    vecs[:, V_N2G:V_N2G + 16] = pcl(inp["norm2_g"][0])
    for j in range(3):
        vecs[:, V_FCW + j * 43:V_FCW + (j + 1) * 43] = pcl(inp["ffn_conv_w"][0, j])
    vecs[:, V_FCB:V_FCB + 43] = pcl(inp["ffn_conv_b"][0])
    for j in range(4):
        vecs[:, V_RCW + j * 8:V_RCW + (j + 1) * 8] = pcl(inp["rec_conv_w"][0, j])
    vecs[:, V_RCB:V_RCB + 8] = pcl(inp["rec_conv_b"][0])
    for d in range(2):
        vecs[:, V_RBA + d * 8:V_RBA + (d + 1) * 8] = pcl(inp["rg_ba"][0, d])
        vecs[:, V_RBI + d * 8:V_RBI + (d + 1) * 8] = pcl(inp["rg_bi"][0, d])
        vecs[:, V_RLAM + d * 8:V_RLAM + (d + 1) * 8] = pcl(inp["rg_lambda"][0, d])
    shared["vecs"] = vecs
    shared["wada"] = np.ascontiguousarray(inp["w_ada"][0], f32)
    shared["win"] = np.ascontiguousarray(inp["w_in"][0], f32)
    shared["wout"] = np.ascontiguousarray(inp["w_out"][0], f32)
    for nm, key in (("wup", "w_up"), ("wgate", "w_gate")):
        w = np.asarray(inp[key][0], f32).reshape(16, 128, NJ, 128)
        shared[nm] = np.ascontiguousarray(w.transpose(2, 1, 0, 3).reshape(NJ, 128, 2048))
    shared["wdown"] = np.ascontiguousarray(np.asarray(inp["w_down"][0], f32).reshape(NJ, 128, 2048))
    rg = np.stack([np.asarray(inp["rg_wa"][0], f32), np.asarray(inp["rg_wi"][0], f32)], 0)
    shared["rgw"] = np.ascontiguousarray(rg.transpose(3, 0, 1, 2, 4).reshape(128, 32 * 128))
    C, Sn = rope_tables(np.arange(S))
    shared["cos"] = C
    shared["sin"] = Sn
    perm = np.zeros((128, 128), f32)
    for p in range(128):
        perm[p ^ 16, p] = 1.0
    shared["perm"] = perm
    shared["identf"] = np.eye(128, dtype=f32)
    shared["identb"] = np.eye(128).astype(ml_dtypes.bfloat16)
    shared["dlam"] = np.ascontiguousarray(np.broadcast_to(np.asarray(inp["diff_lambda"][0], f32).reshape(1, 256), (128, 256)))
    shared["subg"] = np.ascontiguousarray(np.broadcast_to(np.asarray(inp["subln_g"][0], f32).reshape(1, 128), (128, 128)))
    shared["fing"] = np.ascontiguousarray(np.broadcast_to(np.asarray(inp["final_g"], f32).reshape(1, D), (128, D)))
    maps = []
    for c in range(NCORES):
        m = dict(shared)
        xo = np.zeros((17 * 128, D), f32)
        t0 = c * OWN
        xo[0:OWN] = x[t0:t0 + OWN]
        toks = np.zeros(EXT, np.int64)
        toks[0:OWN] = np.arange(t0, t0 + OWN)
        if c > 0:
            xo[OWN] = x[t0 - 1]
            toks[OWN] = t0 - 1
        if c < NCORES - 1:
            xo[OWN + 1] = x[t0 + OWN]
            toks[OWN + 1] = t0 + OWN
        m["xo"] = xo
        Co, So = rope_tables(toks)
        Cp = np.zeros((128, 17 * 128), f32)
        Sp_ = np.zeros((128, 17 * 128), f32)
        Cp[:, :EXT] = Co
        Sp_[:, :EXT] = So
        m["coso"] = Cp
        m["sino"] = Sp_
        mk = np.zeros((128, 32), f32)
        mk[:, c] = 1.0
        if c > 0:
            mk[:, 8 + c - 1] = 1.0
            mk[:, 24] = 1.0
        if c < NCORES - 1:
            mk[:, 16 + c + 1] = 1.0
            mk[:, 25] = 1.0
        m["mk"] = mk
        maps.append(m)
    return maps


STOP_AFTER = 99


def kernel(**inputs):
    maps = host_inputs(inputs)
    nc = build_program(stop_after=STOP_AFTER)
    res = run_bass_kernel_spmd(nc, maps, core_ids=list(range(NCORES)))
    out = np.concatenate([np.asarray(r["out"], np.float32) for r in res.results], 0)
    return out.reshape(1, S, D)
```

```python
import os
from contextlib import ExitStack
import numpy as np
import ml_dtypes
import concourse.bass as bass
import concourse.mybir as mybir
from concourse.bass_utils import run_bass_kernel_spmd

F32 = mybir.dt.float32
BF16 = mybir.dt.bfloat16
AF = mybir.ActivationFunctionType
ALU = mybir.AluOpType

D = 2048
S = 16384
NCTX = 256
DFF = 5504
NJ = 43
NKEY = S + NCTX
NKT = NKEY // 128
OWN = 2048
EXT = 2050
NCORES = 8
EPS = 1e-6
SUBLN_EPS = 1e-5
LAM_INIT = 0.8 - 0.6
TT = 256
SKIP = os.environ.get('P1SKIP', '')

V_BADA = 0
V_N1G = 96
V_N2G = 112
V_FCW = 128
V_FCB = V_FCW + 129
V_RCW = V_FCB + 43
V_RCB = V_RCW + 32
V_RBA = V_RCB + 8
V_RBI = V_RBA + 16
V_RLAM = V_RBI + 16
NV = V_RLAM + 16


class Sem:
    _k = 0

    def __init__(self, nc, name):
        self.h = nc.alloc_semaphore(name)
        self.n = 0
        Sem._k += 1
        self.key = Sem._k


class Eng:
    def __init__(self, nc, eng, name, is_pe=False):
        self.e = eng
        self.sem = Sem(nc, "s_" + name)
        self.seen = {}
        self.is_pe = is_pe

    def wait(self, toks):
        best = {}
        for t in toks:
            if t is None:
                continue
            sem, val = t
            if self.is_pe and sem is self.sem:
                continue
            if self.seen.get(sem.key, 0) >= val:
                continue
            if sem.key not in best or best[sem.key][1] < val:
                best[sem.key] = (sem, val)
        for sem, val in best.values():
            self.e.wait_ge(sem.h, val)
            self.seen[sem.key] = val

    def mark(self, ins):
        self.sem.n += 1
        ins.then_inc(self.sem.h, 1)
        return (self.sem, self.sem.n)


class Buf:
    def __init__(self, name=""):
        self.name = name
        self.w = None
        self.r = {}
        self.dsem = None

    def rtoks(self):
        return list(self.r.values())

    def add_r(self, tok):
        sem, val = tok
        if sem.key not in self.r or self.r[sem.key][1] < val:
            self.r[sem.key] = tok


class K:
    def __init__(self, nc):
        self.nc = nc
        self.PE = Eng(nc, nc.tensor, "pe", is_pe=True)
        self.ACT = Eng(nc, nc.scalar, "act")
        self.DVE = Eng(nc, nc.vector, "dve")
        self.POOL = Eng(nc, nc.gpsimd, "pool")
        self.SP = Eng(nc, nc.sync, "sp")
        self.nsem = 5

    def _deps(self, reads, writes):
        toks = []
        for b in reads:
            toks.append(b.w)
        for b in writes:
            toks.append(b.w)
            toks += b.rtoks()
        return toks

    def _commit(self, tok, reads, writes):
        for b in reads:
            b.add_r(tok)
        for b in writes:
            b.w = tok
            b.r = {}

    def op(self, E, fn, reads=(), writes=()):
        E.wait(self._deps(reads, writes))
        ins = fn(E.e)
        tok = E.mark(ins)
        self._commit(tok, reads, writes)
        return tok

    def pe(self, fns, reads=(), writes=()):
        E = self.PE
        E.wait(self._deps(reads, writes))
        ins = None
        for f in fns:
            ins = f(E.e)
        tok = E.mark(ins)
        self._commit(tok, reads, writes)
        return tok

    def dma(self, out, in_, sbuf, reads=(), writes=(), eng=None):
        E = eng or self.SP
        if sbuf.dsem is None:
            sbuf.dsem = Sem(self.nc, "d_%d" % self.nsem)
            self.nsem += 1
        E.wait(self._deps(reads, writes))
        sbuf.dsem.n += 16
        E.e.dma_start(out=out, in_=in_).then_inc(sbuf.dsem.h, 16)
        tok = (sbuf.dsem, sbuf.dsem.n)
        self._commit(tok, reads, writes)
        return tok


def build_program(debug=False, stop_after=99):
    nc = bass.Bass("TRN2", target_bir_lowering=False)
    k = K(nc)
    PE, ACT, DVE, POOL, SP = k.PE, k.ACT, k.DVE, k.POOL, k.SP

    def din(name, shape, dt=F32):
        return nc.dram_tensor(name, list(shape), dt, kind="ExternalInput").ap()

    x_d = din("x", [S, D])
    ctx_d = din("ctx", [NCTX, D])
    xo_d = din("xo", [17 * 128, D])
    cvec_d = din("cvec", [128, 32])
    vecs_d = din("vecs", [128, NV])
    wada_d = din("wada", [D, 6 * D])
    win_d = din("win", [D, 5120])
    wout_d = din("wout", [D, D])
    wup_d = din("wup", [NJ, 128, 2048])
    wgate_d = din("wgate", [NJ, 128, 2048])
    wdown_d = din("wdown", [NJ, 128, 2048])
    rgw_d = din("rgw", [128, 32 * 128])
    cos_d = din("cos", [128, S])
    sin_d = din("sin", [128, S])
    coso_d = din("coso", [128, 17 * 128])
    sino_d = din("sino", [128, 17 * 128])
    perm_d = din("perm", [128, 128])
    ident_d = din("identf", [128, 128])
    identb_d = din("identb", [128, 128], BF16)
    dlam_d = din("dlam", [128, 256])
    subg_d = din("subg", [128, 128])
    fing_d = din("fing", [128, D])
    mk_d = din("mk", [128, 32])
    out_d = nc.dram_tensor("out", [OWN, D], F32, kind="ExternalOutput").ap()

    KT_d = nc.dram_tensor("KT", [8, 128, NKEY], BF16).ap()
    VV_d = nc.dram_tensor("VV", [8, 128, NKT, 129], BF16).ap()
    RXL_d = nc.dram_tensor("RXL", [8, 128, S], F32).ap()
    RXC_d = nc.dram_tensor("RXC", [8, 128, NCTX], F32).ap()
    dbg = {}
    if debug:
        dbg["mod"] = nc.dram_tensor("dbg_mod", [128, 192], F32, kind="ExternalOutput").ap()
        dbg["KT"] = nc.dram_tensor("dbg_KT", [8, 128, 1024], BF16, kind="ExternalOutput").ap()
        dbg["VV"] = nc.dram_tensor("dbg_VV", [8, 128, 8, 129], BF16, kind="ExternalOutput").ap()
        dbg["RX"] = nc.dram_tensor("dbg_RX", [8, 128, 1024], F32, kind="ExternalOutput").ap()
        dbg["AT"] = nc.dram_tensor("dbg_AT", [128, 17 * 128], BF16, kind="ExternalOutput").ap()
        dbg["QT"] = nc.dram_tensor("dbg_QT", [128, 17 * 128], BF16, kind="ExternalOutput").ap()
        dbg["GZ"] = nc.dram_tensor("dbg_GZ", [128, 17 * 128], BF16, kind="ExternalOutput").ap()

    def sb(name, shape, dt):
        return nc.alloc_sbuf_tensor("sb_" + name, shape, dt)
    pst = nc.alloc_psum_tensor

    vecs = sb("vecs", [128, NV], F32)
    modx = sb("modx", [128, 96], F32)
    modc = sb("modc", [128, 96], F32)
    a1 = sb("a1", [128, 16], F32)
    a1c = sb("a1c", [128, 16], F32)
    a2 = sb("a2", [128, 16], F32)
    modacc = sb("modacc", [128, 192], F32)
    identb = sb("identb", [128, 128], BF16)
    identf = sb("identf", [128, 128], F32)
    permT = sb("permT", [128, 128], F32)
    ones_f = sb("ones_f", [128, 128], F32)
    B_vecs = Buf("vecs")
    B_mod = Buf("mod")
    B_const = Buf("const")

    k.dma(vecs[:], vecs_d[:, :], B_vecs, writes=[B_vecs])
    k.dma(identb[:], identb_d[:, :], B_const, writes=[B_const])
    k.dma(identf[:], ident_d[:, :], B_const, writes=[B_const])
    k.dma(permT[:], perm_d[:, :], B_const, writes=[B_const])
    k.op(k.DVE, lambda e: e.memset(ones_f[:], 1.0), writes=[B_const])

    psall = pst("psall", [128, 4096], F32)
    banks = [psall[:, i * 512:(i + 1) * 512] for i in range(8)]
    Bbank = [Buf("bank%d" % i) for i in range(8)]
    tpbs = [banks[6].bitcast(BF16), banks[7].bitcast(BF16)]
    Btp = [Bbank[6], Bbank[7]]

    with nc.sbuf_tensor("sb_wk0", [128, 6 * D], F32) as wk0, nc.sbuf_tensor("sb_wk1", [128, 6 * D], F32) as wk1, \
            nc.sbuf_tensor("sb_cvec", [128, 32], F32) as cvec, nc.sbuf_tensor("sb_scv", [128, 32], F32) as scv:
        wk = [wk0, wk1]
        Bwk = [Buf("wk0"), Buf("wk1")]
        B_cv = Buf("cvec")
        B_scv = Buf("scv")
        k.dma(cvec[:], cvec_d[:, :], B_cv, writes=[B_cv])
        k.op(ACT, lambda e: e.activation(out=scv[:], in_=cvec[:], func=AF.Silu), reads=[B_cv], writes=[B_scv])
        macc = modacc
        B_macc = Buf("macc")
        for kc in range(16):
            s = kc % 2
            for q in range(4):
                k.dma(wk[s][:, q * 3072:(q + 1) * 3072], wada_d[kc * 128:(kc + 1) * 128, q * 3072:(q + 1) * 3072],
                      Bwk[s], writes=[Bwk[s]] if q == 0 else [])
            Bwk[s].w = (Bwk[s].dsem, Bwk[s].dsem.n)
            psm = banks[s]
            fns = []
            for j in range(96):
                fns.append(lambda e, j=j, s=s, kc=kc, psm=psm: e.matmul(
                    psm[:, 2 * j:2 * j + 2], lhsT=wk[s][:, j * 128:(j + 1) * 128], rhs=scv[:, 2 * kc:2 * kc + 2],
                    start=True, stop=True))
            k.pe(fns, reads=[Bwk[s], B_scv], writes=[Bbank[s]])
            if kc == 0:
                k.op(DVE, lambda e, psm=psm: e.tensor_copy(macc[:], psm[:, 0:192]), reads=[Bbank[s]], writes=[B_macc])
            else:
                k.op(DVE, lambda e, psm=psm: e.tensor_tensor(out=macc[:], in0=macc[:], in1=psm[:, 0:192], op=ALU.add),
                     reads=[Bbank[s], B_macc], writes=[B_macc])
        psv = macc[:].rearrange("p (j t) -> p j t", t=2)
        k.op(DVE, lambda e: e.tensor_tensor(out=modx[:], in0=psv[:, :, 0], in1=vecs[:, V_BADA:V_BADA + 96], op=ALU.add),
             reads=[B_macc, B_vecs], writes=[B_mod])
        k.op(DVE, lambda e: e.tensor_tensor(out=modc[:], in0=psv[:, :, 1], in1=vecs[:, V_BADA:V_BADA + 96], op=ALU.add),
             reads=[B_macc, B_vecs], writes=[B_mod])
        k.op(DVE, lambda e: e.scalar_tensor_tensor(out=a1[:], in0=modx[:, 16:32], scalar=1.0, in1=vecs[:, V_N1G:V_N1G + 16],
                                                   op0=ALU.add, op1=ALU.mult), reads=[B_mod, B_vecs], writes=[B_mod])
        k.op(DVE, lambda e: e.scalar_tensor_tensor(out=a1c[:], in0=modc[:, 16:32], scalar=1.0, in1=vecs[:, V_N1G:V_N1G + 16],
                                                   op0=ALU.add, op1=ALU.mult), reads=[B_mod, B_vecs], writes=[B_mod])
        k.op(DVE, lambda e: e.scalar_tensor_tensor(out=a2[:], in0=modx[:, 64:80], scalar=1.0, in1=vecs[:, V_N2G:V_N2G + 16],
                                                   op0=ALU.add, op1=ALU.mult), reads=[B_mod, B_vecs], writes=[B_mod])
        if debug:
            B_dm = Buf("dbgmod")
            k.dma(dbg["mod"][:, 0:96], modx[:], B_dm, reads=[B_mod])
            k.dma(dbg["mod"][:, 96:192], modc[:], B_dm, reads=[B_mod])
        bar = [Bwk[0].w, Bwk[1].w, Bbank[0].w, Bbank[1].w, B_mod.w] + Bwk[0].rtoks() + Bwk[1].rtoks() + B_scv.rtoks()
    for E in (PE, ACT, DVE, POOL, SP):
        E.wait(bar)
    if debug and stop_after == 0:
        SP.wait([(B_dm.dsem, B_dm.dsem.n)])
        return nc
    if stop_after == 0:
        return nc

    epsc = sb("epsc", [128, 2], F32)
    k.op(DVE, lambda e: e.memset(epsc[:, 0:1], EPS), writes=[B_const])
    k.op(DVE, lambda e: e.memset(epsc[:, 1:2], SUBLN_EPS), writes=[B_const])
    junk = sb("junk", [128, D], BF16)
    B_junk = Buf("junk")
    xn = [sb("xn%d" % i, [128, D], BF16) for i in range(2)]
    Bxn = [Buf("xn%d" % i) for i in range(2)]
    ssq = [sb("ssq%d" % i, [128, 4], F32) for i in range(2)]
    Bssq = [Buf("ssq%d" % i) for i in range(2)]
    cnt = {"nt": 0, "tp": 0, "ev": 0}

    def norm_transpose(src_ap, Bsrc, rows, avec, bvec, dst_fn, Bdst, boff=0):
        i = cnt["nt"] % 2
        cnt["nt"] += 1
        sq = ssq[i]
        k.op(ACT, lambda e: e.activation(out=junk[0:rows, :], in_=src_ap, func=AF.Square, accum_out=sq[0:rows, 0:1]),
             reads=[Bsrc], writes=[B_junk, Bssq[i]])
        k.op(ACT, lambda e: e.activation(out=sq[0:rows, 1:2], in_=sq[0:rows, 0:1], func=AF.Ln, scale=1.0 / D, bias=epsc[0:rows, 0:1]),
             reads=[Bssq[i], B_const], writes=[Bssq[i]])
        k.op(ACT, lambda e: e.activation(out=sq[0:rows, 2:3], in_=sq[0:rows, 1:2], func=AF.Exp, scale=-0.5),
             reads=[Bssq[i]], writes=[Bssq[i]])
        k.op(DVE, lambda e: e.tensor_scalar(out=xn[i][0:rows, :], in0=src_ap, scalar1=sq[0:rows, 2:3], scalar2=None, op0=ALU.mult),
             reads=[Bsrc, Bssq[i]], writes=[Bxn[i]])
        for g in range(2):
            hb = cnt["tp"] % 2
            cnt["tp"] += 1
            tpb = tpbs[hb]
            fns = []
            for q in range(8):
                kc = g * 8 + q
                fns.append(lambda e, kc=kc, q=q, tpb=tpb: e.transpose(tpb[:, q * 128: q * 128 + rows],
                                                                      xn[i][0:rows, kc * 128:(kc + 1) * 128], identb[0:rows, 0:rows]))
            k.pe(fns, reads=[Bxn[i], B_const], writes=[Btp[hb]])
            for q in range(8):
                kc = g * 8 + q
                src = tpb[:, q * 128: q * 128 + rows]
                k.op(ACT, lambda e, kc=kc, src=src: e.activation(out=dst_fn(kc), in_=src, func=AF.Identity,
                                                                scale=avec[:, kc:kc + 1], bias=bvec[:, boff + kc:boff + kc + 1]),
                     reads=[Btp[hb], B_mod], writes=[Bdst])

    with ExitStack() as es:
        def sc(name, shape, dt):
            return es.enter_context(nc.sbuf_tensor("sb_" + name, shape, dt))
        wkvr = sc("wkvr", [128, 16, 3072], BF16)
        wst0 = sc("wst0", [128, 1024], F32); wst1 = sc("wst1", [128, 1024], F32)
        xs0 = sc("xs0", [128, 2, D], F32); xs1 = sc("xs1", [128, 2, D], F32)
        hT0 = sc("hT0", [128, 16, TT], BF16); hT1 = sc("hT1", [128, 16, TT], BF16)
        cs0 = sc("cs0", [128, 2, TT], F32); cs1 = sc("cs1", [128, 2, TT], F32)
        k32a = sc("k32a", [128, TT], F32); k32b = sc("k32b", [128, TT], F32)
        t1a = sc("t1a", [128, TT], F32); t1b = sc("t1b", [128, TT], F32)
        t2a = sc("t2a", [128, TT], F32); t2b = sc("t2b", [128, TT], F32)
        ko = sc("ko", [128, 4, TT], BF16); rxo = sc("rxo", [128, 4, TT], F32)
        vt0 = sc("vt0", [128, 8, 2, 129], BF16); vt1 = sc("vt1", [128, 8, 2, 129], BF16)
        wst = [wst0, wst1]
        Bwst = [Buf(), Buf()]
        B_w = Buf("wkvr")
        n = 0
        for kc in range(16 if 'W' not in SKIP else 0):
            for gi in range(3):
                s_ = n % 2
                n += 1
                k.dma(wst[s_][:], win_d[kc * 128:(kc + 1) * 128, 1024 + gi * 1024: 2048 + gi * 1024], Bwst[s_], writes=[Bwst[s_]])
                k.op(ACT, lambda e, s_=s_, kc=kc, gi=gi: e.activation(out=wkvr[:, kc, gi * 1024:(gi + 1) * 1024], in_=wst[s_][:], func=AF.Identity),
                     reads=[Bwst[s_]], writes=[B_w])
        xs = [xs0, xs1]
        Bxs = [Buf(), Buf()]
        hT = [hT0, hT1]
        BhT = [Buf(), Buf()]
        cs = [cs0, cs1]
        Bcs = [Buf(), Buf()]
        k32 = [k32a, k32b]
        Bk32 = [Buf(), Buf()]
        t1 = [t1a, t1b]
        Bt1 = [Buf(), Buf()]
        t2 = [t2a, t2b]
        Bt2 = [Buf(), Buf()]
        Bko = [Buf() for _ in range(4)]
        Brxo = [Buf() for _ in range(4)]
        vt = [vt0, vt1]
        Bvt = [Buf(), Buf()]
        for v_ in vt:
            k.op(POOL, lambda e, v_=v_: e.memset(v_[:, :, :, 128:129], 1.0), writes=[Bvt[0], Bvt[1]])
        B_scr = Buf("scratch")
        ntiles = 1 + S // TT
        if stop_after == 1 and debug:
            ntiles = 5
        NT_P1 = ntiles
        gslot = {"o": 0, "p": 0, "v": 0, "ko": 0, "rx": 0, "k32": 0}

        def load_tile(ti):
            s_ = ti % 2
            src = ctx_d if ti == 0 else x_d
            r0 = 0 if ti == 0 else (ti - 1) * TT
            k.dma(xs[s_][:], src[r0:r0 + TT, :].rearrange("(s p) d -> p s d", p=128), Bxs[s_], writes=[Bxs[s_]])
            if ti > 0:
                k.dma(cs[s_][:, 0, :], cos_d[:, r0:r0 + TT], Bcs[s_], writes=[Bcs[s_]])
                k.dma(cs[s_][:, 1, :], sin_d[:, r0:r0 + TT], Bcs[s_], writes=[])
                Bcs[s_].w = (Bcs[s_].dsem, Bcs[s_].dsem.n)

        def nt_tile(ti):
            s_ = ti % 2
            av, bv = (a1c, modc) if ti == 0 else (a1, modx)
            for su in range(2):
                norm_transpose(xs[s_][:, su, :], Bxs[s_], 128, av, bv,
                               lambda kc, su=su, s_=s_: hT[s_][:, kc, su * 128:(su + 1) * 128], BhT[s_])

        load_tile(0)
        nt_tile(0)
        for ti in range(ntiles):
            s_ = ti % 2
            if ti + 1 < ntiles:
                load_tile(ti + 1)
            key0 = 0 if ti == 0 else NCTX + (ti - 1) * TT
            pending = []
            for h in range(8):
                oslot = gslot["o"] % 2
                gslot["o"] += 1
                ob = banks[oslot][:, 0:TT]
                Bo = Bbank[oslot]
                k.pe([lambda e, kc=kc, h=h, ob=ob: e.matmul(ob, lhsT=wkvr[:, kc, h * 128:(h + 1) * 128], rhs=hT[s_][:, kc, :],
                                                            start=(kc == 0), stop=(kc == 15)) for kc in range(16)],
                     reads=[B_w, BhT[s_]], writes=[Bo])
                ks = gslot["ko"] % 4
                gslot["ko"] += 1
                if ti == 0:
                    k.op(ACT, lambda e, ob=ob, ks=ks: e.activation(out=ko[:, ks, :], in_=ob, func=AF.Identity), reads=[Bo], writes=[Bko[ks]])
                    k.dma(KT_d[h, :, key0:key0 + TT], ko[:, ks, :], Bko[ks], reads=[Bko[ks]])
                    continue
                q_ = gslot["k32"] % 2
                gslot["k32"] += 1
                k.op(ACT, lambda e, ob=ob, q_=q_: e.activation(out=k32[q_][:], in_=ob, func=AF.Identity), reads=[Bo], writes=[Bk32[q_]])

                def post(h=h, q_=q_, ks=ks):
                    pb = banks[2][:, 0:TT]
                    k.pe([lambda e, pb=pb, q_=q_: e.matmul(pb, lhsT=permT[:], rhs=k32[q_][:], start=True, stop=True)],
                         reads=[Bk32[q_], B_const], writes=[Bbank[2]])
                    k.op(DVE, lambda e, pb=pb, q_=q_: e.tensor_tensor(out=t1[q_][:], in0=pb, in1=cs[s_][:, 1, :], op=ALU.mult),
                         reads=[Bbank[2], Bcs[s_]], writes=[Bt1[q_]])
                    k.op(POOL, lambda e, q_=q_: e.tensor_tensor(out=t2[q_][:], in0=k32[q_][:], in1=cs[s_][:, 0, :], op=ALU.mult),
                         reads=[Bk32[q_], Bcs[s_]], writes=[Bt2[q_]])
                    k.op(POOL, lambda e, q_=q_, ks=ks: e.tensor_tensor(out=ko[:, ks, :], in0=t1[q_][:], in1=t2[q_][:], op=ALU.add),
                         reads=[Bt1[q_], Bt2[q_]], writes=[Bko[ks]])
                    k.dma(KT_d[h, :, key0:key0 + TT], ko[:, ks, :], Bko[ks], reads=[Bko[ks]])
                if pending:
                    pending.pop(0)()
                pending.append(post)
            if ti + 1 < ntiles:
                nt_tile(ti + 1)
            while pending:
                pending.pop(0)()
            for b in range(0 if 'R' not in SKIP else 8, 8):
                oslot = gslot["o"] % 2
                gslot["o"] += 1
                ob = banks[oslot][:, 0:TT]
                Bo = Bbank[oslot]
                k.pe([lambda e, kc=kc, b=b, ob=ob: e.matmul(ob, lhsT=wkvr[:, kc, 2048 + b * 128:2048 + (b + 1) * 128], rhs=hT[s_][:, kc, :],
                                                            start=(kc == 0), stop=(kc == 15)) for kc in range(16)],
                     reads=[B_w, BhT[s_]], writes=[Bo])
                rs = gslot["rx"] % 4
                gslot["rx"] += 1
                k.op(ACT, lambda e, ob=ob, rs=rs: e.activation(out=rxo[:, rs, :], in_=ob, func=AF.Identity), reads=[Bo], writes=[Brxo[rs]])
                dst = RXC_d[b, :, :] if ti == 0 else RXL_d[b, :, (ti - 1) * TT: ti * TT]
                k.dma(dst, rxo[:, rs, :], Brxo[rs], reads=[Brxo[rs]])
            vs_ = ti % 2
            for su in range(0 if 'V' not in SKIP else 2, 2):
                for half in range(2):
                    vb_i = 3 + gslot["v"] % 3
                    gslot["v"] += 1
                    vb = banks[vb_i]
                    k.pe([lambda e, kc=kc, su=su, half=half, vb=vb: e.matmul(
                        vb, lhsT=hT[s_][:, kc, su * 128:(su + 1) * 128], rhs=wkvr[:, kc, 1024 + half * 512:1024 + (half + 1) * 512],
                        start=(kc == 0), stop=(kc == 15)) for kc in range(16)],
                        reads=[B_w, BhT[s_]], writes=[Bbank[vb_i]])
                    k.op(DVE, lambda e, su=su, half=half, vb=vb, vs_=vs_: e.tensor_copy(
                        vt[vs_][:, half * 4:(half + 1) * 4, su, 0:128], vb.rearrange("p (h d) -> p h d", h=4)),
                        reads=[Bbank[vb_i]], writes=[Bvt[vs_]])
            kt0 = key0 // 128
            for h in range(0 if 'V' not in SKIP else 8, 8):
                k.dma(VV_d[h, :, kt0:kt0 + 2, :], vt[vs_][:, h, :, :], Bvt[vs_], reads=[Bvt[vs_]])
        bar = []
        for b_ in Bko + Brxo + Bvt + Bxs + Bcs + BhT + Bk32 + Bt1 + Bt2 + Bwst + [B_w] + Bbank + Btp + Bxn + Bssq + [B_junk]:
            bar.append(b_.w)
            bar += b_.rtoks()
            if b_.dsem is not None:
                bar.append((b_.dsem, b_.dsem.n))
    for E in (PE, ACT, DVE, POOL, SP):
        E.wait(bar)

    if debug and stop_after == 1 and 'D' in SKIP:
        return nc
    if debug and stop_after == 1:
        with nc.sbuf_tensor("sb_dbt", [128, 8, 129 * 8], BF16) as dbt, nc.sbuf_tensor("sb_dbf", [128, 8, 1024], F32) as dbf:
            Bd = Buf()
            k.dma(dbt[:, :, 0:1024], KT_d[:, :, 0:1024].rearrange("h p n -> p h n"), Bd, writes=[Bd])
            k.dma(dbg["KT"].rearrange("h p n -> p h n"), dbt[:, :, 0:1024], Bd, reads=[Bd])
            Bd2 = Buf()
            k.dma(dbf[:], RXL_d[:, :, 0:1024].rearrange("h p n -> p h n"), Bd2, writes=[Bd2])
            k.dma(dbg["RX"].rearrange("h p n -> p h n"), dbf[:], Bd2, reads=[Bd2])
            Bd3 = Buf()
            k.dma(dbt[:].rearrange("p h (t d) -> p h t d", d=129), VV_d[:, :, 0:8, :].rearrange("h p t d -> p h t d"), Bd3,
                  reads=[Bd], writes=[Bd3])
            k.dma(dbg["VV"].rearrange("h p t d -> p h t d"), dbt[:].rearrange("p h (t d) -> p h t d", d=129), Bd3, reads=[Bd3])
            fin = [(b_.dsem, b_.dsem.n) for b_ in (Bd, Bd2, Bd3)]
            SP.wait(fin)
        return nc


    EXTP = 17 * 128
    B_QT, B_GZ, B_AT = Buf("QT"), Buf("GZ"), Buf("AT")
    dl = sb("dl", [128, 256], F32)
    subg8 = sb("subg8", [128, 128], F32)
    lamv = sb("lamv", [128, 8], F32)
    mk = sb("mk", [128, 32], F32)
    B_misc = Buf("misc")
    k.dma(dl[:], dlam_d[:, :], B_misc, writes=[B_misc])
    k.dma(subg8[:], subg_d[:, :], B_misc, writes=[B_misc])
    k.dma(mk[:], mk_d[:, :], B_misc, writes=[B_misc])
    k.op(DVE, lambda e: e.tensor_tensor(out=dl[:, 0:64], in0=dl[:, 0:64], in1=dl[:, 64:128], op=ALU.mult), reads=[B_misc], writes=[B_misc])
    k.op(DVE, lambda e: e.tensor_tensor(out=dl[:, 128:192], in0=dl[:, 128:192], in1=dl[:, 192:256], op=ALU.mult), reads=[B_misc], writes=[B_misc])
    k.op(ACT, lambda e: e.activation(out=dl[:, 64:128], in_=dl[:, 0:64], func=AF.Identity, accum_out=lamv[:, 0:1]), reads=[B_misc], writes=[B_misc])
    k.op(ACT, lambda e: e.activation(out=dl[:, 192:256], in_=dl[:, 128:192], func=AF.Identity, accum_out=lamv[:, 1:2]), reads=[B_misc], writes=[B_misc])
    k.op(ACT, lambda e: e.activation(out=lamv[:, 2:4], in_=lamv[:, 0:2], func=AF.Exp), reads=[B_misc], writes=[B_misc])
    k.op(DVE, lambda e: e.tensor_tensor(out=lamv[:, 4:5], in0=lamv[:, 3:4], in1=lamv[:, 2:3], op=ALU.subtract), reads=[B_misc], writes=[B_misc])
    k.op(DVE, lambda e: e.tensor_scalar(out=lamv[:, 4:5], in0=lamv[:, 4:5], scalar1=-LAM_INIT, scalar2=None, op0=ALU.add), reads=[B_misc], writes=[B_misc])
    k.op(DVE, lambda e: e.tensor_scalar(out=subg8[:], in0=subg8[:], scalar1=1.0 - LAM_INIT, scalar2=None, op0=ALU.mult), reads=[B_misc], writes=[B_misc])

    X1_d = nc.dram_tensor("X1s", [17 * 128, D], F32).ap()
    XR_d = nc.dram_tensor("XRs", [128, NKEY], F32).ap()
    WB_d = nc.dram_tensor("WBs", [3, NJ, 128, 2048], BF16).ap()
    es_mix = ExitStack()
    QT = es_mix.enter_context(nc.sbuf_tensor("sb_QT", [128, 8, EXTP], BF16))
    GZ = es_mix.enter_context(nc.sbuf_tensor("sb_GZ", [128, 8, EXTP], BF16))
    AT = es_mix.enter_context(nc.sbuf_tensor("sb_AT", [128, 8, EXTP], BF16))
    with ExitStack() as es:
        def sc(name, shape, dt):
            return es.enter_context(nc.sbuf_tensor("sb_" + name, shape, dt))
        wqz = sc("wqz", [128, 16, 1024], BF16)
        wst0_ = sc("b_wst0", [128, 1024], F32)
        wst = [wst0_, wst0_]
        xs0_ = sc("b_xs0", [128, 2, D], F32)
        xs = [xs0_, xs0_]
        hT = [sc("b_hT0", [128, 16, TT], BF16), sc("b_hT1", [128, 16, TT], BF16)]
        cs = [sc("b_cs0", [128, 2, TT], F32), sc("b_cs1", [128, 2, TT], F32)]
        k32 = [sc("b_k32a", [128, TT], F32), sc("b_k32b", [128, TT], F32)]
        t1 = [sc("b_t1a", [128, TT], F32), sc("b_t1b", [128, TT], F32)]
        t2 = [sc("b_t2a", [128, TT], F32), sc("b_t2b", [128, TT], F32)]
        Bwst, Bxs, BhT, Bcs, Bk32, Bt1, Bt2 = ([Buf(), Buf()] for _ in range(7))
        Bxs[1] = Bxs[0]
        Bwst[1] = Bwst[0]
        B_w = Buf("wqz")
        NOT = 9

        def load_own(ti):
            s_ = ti % 2
            nsub = 2 if ti < 8 else 1
            ncol = nsub * 128
            k.dma(xs[s_][:, 0:nsub, :], xo_d[ti * TT: ti * TT + ncol, :].rearrange("(s p) d -> p s d", p=128), Bxs[s_], writes=[Bxs[s_]])
            k.dma(cs[s_][:, 0, 0:ncol], coso_d[:, ti * TT: ti * TT + ncol], Bcs[s_], writes=[Bcs[s_]])
            k.dma(cs[s_][:, 1, 0:ncol], sino_d[:, ti * TT: ti * TT + ncol], Bcs[s_], writes=[])
            Bcs[s_].w = (Bcs[s_].dsem, Bcs[s_].dsem.n)

        go = 0
        for pas in range(2):
            c0w = 0 if pas == 0 else 4096
            for kc in range(16):
                k.dma(wst[0][:], win_d[kc * 128:(kc + 1) * 128, c0w:c0w + 1024], Bwst[0], writes=[Bwst[0]])
                k.op(ACT, lambda e, kc=kc: e.activation(out=wqz[:, kc, :], in_=wst[0][:], func=AF.Identity), reads=[Bwst[0]], writes=[B_w])
            load_own(0)
            for ti in range(NOT):
                s_ = ti % 2
                nsub = 2 if ti < 8 else 1
                ncol = nsub * 128
                col0 = ti * TT
                for su in range(nsub):
                    norm_transpose(xs[s_][:, su, :], Bxs[s_], 128, a1, modx,
                                   lambda kc, su=su, s_=s_: hT[s_][:, kc, su * 128:(su + 1) * 128], BhT[s_])
                if ti + 1 < NOT:
                    load_own(ti + 1)
                for h in range(8 if pas == 0 else 0):
                    oslot = go % 2
                    go += 1
                    ob = banks[oslot][:, 0:ncol]
                    Bo = Bbank[oslot]
                    k.pe([lambda e, kc=kc, h=h, ob=ob: e.matmul(ob, lhsT=wqz[:, kc, h * 128:(h + 1) * 128], rhs=hT[s_][:, kc, 0:ncol],
                                                                start=(kc == 0), stop=(kc == 15)) for kc in range(16)],
                         reads=[B_w, BhT[s_]], writes=[Bo])
                    q_ = h % 2
                    k.op(ACT, lambda e, ob=ob, q_=q_: e.activation(out=k32[q_][:, 0:ncol], in_=ob, func=AF.Identity), reads=[Bo], writes=[Bk32[q_]])
                    pb = banks[2][:, 0:ncol]
                    k.pe([lambda e, pb=pb, q_=q_: e.matmul(pb, lhsT=permT[:], rhs=k32[q_][:, 0:ncol], start=True, stop=True)],
                         reads=[Bk32[q_], B_const], writes=[Bbank[2]])
                    k.op(DVE, lambda e, pb=pb, q_=q_: e.tensor_tensor(out=t1[q_][:, 0:ncol], in0=pb, in1=cs[s_][:, 1, 0:ncol], op=ALU.mult),
                         reads=[Bbank[2], Bcs[s_]], writes=[Bt1[q_]])
                    k.op(POOL, lambda e, q_=q_: e.tensor_tensor(out=t2[q_][:, 0:ncol], in0=k32[q_][:, 0:ncol], in1=cs[s_][:, 0, 0:ncol], op=ALU.mult),
                         reads=[Bk32[q_], Bcs[s_]], writes=[Bt2[q_]])
                    k.op(POOL, lambda e, q_=q_, h=h: e.tensor_tensor(out=QT[:, h, col0:col0 + ncol], in0=t1[q_][:, 0:ncol], in1=t2[q_][:, 0:ncol], op=ALU.add),
                         reads=[Bt1[q_], Bt2[q_]], writes=[B_QT])
                for b in range(8 if pas == 1 else 0):
                    oslot = go % 2
                    go += 1
                    ob = banks[oslot][:, 0:ncol]
                    Bo = Bbank[oslot]
                    k.pe([lambda e, kc=kc, b=b, ob=ob: e.matmul(ob, lhsT=wqz[:, kc, b * 128:(b + 1) * 128], rhs=hT[s_][:, kc, 0:ncol],
                                                                start=(kc == 0), stop=(kc == 15)) for kc in range(16)],
                         reads=[B_w, BhT[s_]], writes=[Bo])
                    k.op(ACT, lambda e, ob=ob, b=b: e.activation(out=GZ[:, b, col0:col0 + ncol], in_=ob, func=AF.Gelu_apprx_tanh),
                         reads=[Bo], writes=[B_GZ])
        bar = []
        for b_ in Bwst + Bxs + BhT + Bcs + Bk32 + Bt1 + Bt2 + [B_w, B_QT, B_GZ] + Bbank + Bxn + Bssq + [B_junk]:
            bar.append(b_.w)
            bar += b_.rtoks()
    for E in (PE, ACT, DVE, POOL, SP):
        E.wait(bar)

    with ExitStack() as es:
        def sc(name, shape, dt):
            return es.enter_context(nc.sbuf_tensor("sb_" + name, shape, dt))
        Kh = sc("Kh", [128, NKEY], BF16)
        Vh = sc("Vh", [128, NKT, 129], BF16)
        pt = [sc("pt%d" % i, [128, 2, 512], BF16) for i in range(3)]
        Bpt = [Buf() for _ in range(3)]
        o32 = [sc("o32_%d" % i, [128, 128], F32) for i in range(2)]
        Bo32 = [Buf(), Buf()]
        onb = [sc("onb%d" % i, [128, 128], BF16) for i in range(2)]
        Bonb = [Buf(), Buf()]
        r4 = [sc("r4_%d" % i, [128, 8], F32) for i in range(2)]
        Br4 = [Buf(), Buf()]
        B_K, B_V = Buf("Kh"), Buf("Vh")
        pc_f = sc("pc_f", [128, 1024], F32)
        pc_b = sc("pc_b", [128, 1024], BF16)
        B_pcf, B_pcb = Buf(), Buf()
        pc = {"i": 0}
        wsrc = [wup_d, wgate_d, wdown_d]

        def precast_step():
            i_ = pc["i"]
            if i_ >= 3 * NJ * 2:
                return
            pc["i"] += 1
            wi_, j_, hf_ = i_ // (NJ * 2), (i_ // 2) % NJ, i_ % 2
            k.dma(pc_f[:], wsrc[wi_][j_, :, hf_ * 1024:(hf_ + 1) * 1024], B_pcf, writes=[B_pcf])
            k.op(POOL, lambda e: e.tensor_copy(pc_b[:], pc_f[:]), reads=[B_pcf], writes=[B_pcb])
            k.dma(WB_d[wi_, j_, :, hf_ * 1024:(hf_ + 1) * 1024], pc_b[:], B_pcb, reads=[B_pcb])
        nheads = 8 if stop_after > 3 else 1
        qblocks = [(q0, 512) for q0 in range(0, 2048, 512)] + [(2048, 2)]
        if stop_after == 3 and debug:
            qblocks = [(0, 512), (2048, 2)]
        k.op(DVE, lambda e: e.memset(AT[:, :, EXT:EXTP], 0.0), writes=[B_AT])
        it = 0
        fz = 0
        for h in range(nheads):
            k.dma(Kh[:], KT_d[h, :, :], B_K, writes=[B_K])
            k.dma(Vh[:], VV_d[h, :, :, :], B_V, writes=[B_V])
            for (q0, nq) in qblocks:
                nqs = (nq + 127) // 128
                rows = min(128, nq)
                for _ in range(7):
                    precast_step()
                def emit_s(kt, it_):
                    sbuf_i = it_ % 2
                    r = it_ % 3
                    b0 = 2 * sbuf_i
                    k.pe([lambda e, c=c, kt=kt, b0=b0: e.matmul(banks[b0 + c][:, 0:nq], lhsT=Kh[c * 64:(c + 1) * 64, kt * 128:(kt + 1) * 128],
                                                               rhs=QT[c * 64:(c + 1) * 64, h, q0:q0 + nq], start=True, stop=True) for c in range(2)],
                         reads=[B_K, B_QT], writes=[Bbank[b0], Bbank[b0 + 1]])
                    sview = psall[:, b0 * 512:(b0 + 2) * 512].rearrange("p (c n) -> p c n", c=2)[:, :, 0:nq]
                    k.op(ACT, lambda e, sview=sview, r=r: e.activation(out=pt[r][:, :, 0:nq], in_=sview, func=AF.Exp, scale=0.125),
                         reads=[Bbank[b0], Bbank[b0 + 1]], writes=[Bpt[r]])

                def emit_pv(kt, it_):
                    r = it_ % 3
                    fns = []
                    accs = []
                    for c in range(2):
                        for qs in range(nqs):
                            ab = 4 + 2 * c + qs // 2
                            co = (qs % 2) * 256
                            if Bbank[ab] not in accs:
                                accs.append(Bbank[ab])
                            fns.append(lambda e, c=c, qs=qs, ab=ab, co=co, kt=kt, r=r: e.matmul(
                                banks[ab][0:rows, co:co + 129], lhsT=pt[r][:, c, qs * 128:qs * 128 + rows], rhs=Vh[:, kt, :],
                                start=(kt == 0 and qs % 2 == 0), stop=(kt == NKT - 1)))
                    k.pe(fns, reads=[Bpt[r], B_V], writes=accs)

                it0 = it
                for kt in range(NKT):
                    emit_s(kt, it0 + kt)
                    if kt >= 1:
                        emit_pv(kt - 1, it0 + kt - 1)
                emit_pv(NKT - 1, it0 + NKT - 1)
                it = it0 + NKT
                R_ = rows
                for qs in range(nqs):
                    f = fz % 2
                    fz += 1
                    a0 = banks[4 + qs // 2][0:R_, (qs % 2) * 256:(qs % 2) * 256 + 129]
                    a1_ = banks[6 + qs // 2][0:R_, (qs % 2) * 256:(qs % 2) * 256 + 129]
                    k.op(DVE, lambda e, f=f, a0=a0: e.reciprocal(out=r4[f][0:R_, 0:1], in_=a0[:, 128:129]), reads=[Bbank[4 + qs // 2]], writes=[Br4[f]])
                    k.op(DVE, lambda e, f=f, a1_=a1_: e.reciprocal(out=r4[f][0:R_, 1:2], in_=a1_[:, 128:129]), reads=[Bbank[6 + qs // 2]], writes=[Br4[f]])
                    k.op(DVE, lambda e, f=f: e.tensor_tensor(out=r4[f][0:R_, 2:3], in0=r4[f][0:R_, 1:2], in1=lamv[0:R_, 4:5], op=ALU.mult),
                         reads=[Br4[f], B_misc], writes=[Br4[f]])
                    k.op(DVE, lambda e, f=f, a0=a0: e.tensor_scalar(out=o32[f][0:R_, :], in0=a0[:, 0:128], scalar1=r4[f][0:R_, 0:1], scalar2=None, op0=ALU.mult),
                         reads=[Bbank[4 + qs // 2], Br4[f]], writes=[Bo32[f]])
                    k.op(DVE, lambda e, f=f, a1_=a1_: e.scalar_tensor_tensor(out=o32[f][0:R_, :], in0=a1_[:, 0:128], scalar=r4[f][0:R_, 2:3], in1=o32[f][0:R_, :],
                                                                           op0=ALU.mult, op1=ALU.add),
                         reads=[Bbank[6 + qs // 2], Br4[f], Bo32[f]], writes=[Bo32[f]])
                    k.op(ACT, lambda e, f=f: e.activation(out=junk[0:R_, 0:128], in_=o32[f][0:R_, :], func=AF.Square, accum_out=r4[f][0:R_, 3:4]),
                         reads=[Bo32[f]], writes=[B_junk, Br4[f]])
                    k.op(ACT, lambda e, f=f: e.activation(out=r4[f][0:R_, 4:5], in_=r4[f][0:R_, 3:4], func=AF.Ln, scale=1.0 / 128, bias=epsc[0:R_, 1:2]),
                         reads=[Br4[f], B_const], writes=[Br4[f]])
                    k.op(ACT, lambda e, f=f: e.activation(out=r4[f][0:R_, 5:6], in_=r4[f][0:R_, 4:5], func=AF.Exp, scale=-0.5),
                         reads=[Br4[f]], writes=[Br4[f]])
                    k.op(DVE, lambda e, f=f: e.scalar_tensor_tensor(out=onb[f][0:R_, :], in0=o32[f][0:R_, :], scalar=r4[f][0:R_, 5:6], in1=subg8[0:R_, :],
                                                                   op0=ALU.mult, op1=ALU.mult),
                         reads=[Bo32[f], Br4[f], B_misc], writes=[Bonb[f]])
                    tb = banks[0].bitcast(BF16)
                    k.pe([lambda e, f=f, tb=tb: e.transpose(tb[:, 0:R_], onb[f][0:R_, :], identb[0:R_, 0:R_])], reads=[Bonb[f], B_const], writes=[Bbank[0]])
                    k.op(ACT, lambda e, tb=tb, qs=qs: e.activation(out=AT[:, h, q0 + qs * 128: q0 + qs * 128 + R_], in_=tb[:, 0:R_], func=AF.Identity),
                         reads=[Bbank[0]], writes=[B_AT])
        while pc["i"] < 3 * NJ * 2:
            precast_step()
        bar = []
        for b_ in [B_K, B_V, B_QT, B_AT, B_pcf, B_pcb] + Bpt + Bo32 + Bonb + Br4 + Bbank + [B_junk]:
            bar.append(b_.w)
            bar += b_.rtoks()
            if b_.dsem is not None:
                bar.append((b_.dsem, b_.dsem.n))
    for E in (PE, ACT, DVE, POOL, SP):
        E.wait(bar)
    if debug and stop_after == 3:
        Bd = Buf()
        k.dma(dbg["AT"][:, :], AT[:, 0, :], Bd, reads=[B_AT])
        k.dma(dbg["QT"][:, :], QT[:, 0, :], Bd, reads=[B_QT])
        k.dma(dbg["GZ"][:, :], GZ[:, 0, :], Bd, reads=[B_GZ])
        SP.wait([(Bd.dsem, Bd.dsem.n)])
        return nc


    RT, B_RT = QT, B_QT
    k.op(POOL, lambda e: e.memset(RT[:, :, EXT:EXTP], 0.0), writes=[B_RT])
    TN = 1024
    with ExitStack() as es:
        def sc(name, shape, dt):
            return es.enter_context(nc.sbuf_tensor("sb_" + name, shape, dt))
        rgwb = sc("rgwb", [128, 32 * 128], BF16)
        rgst = sc("rgst", [128, 1024], F32)
        cst = sc("cst", [128, 72], F32)
        xt2 = [sc("r_xt%d" % i, [128, TN + 4], F32) for i in range(2)]
        xr2 = [sc("r_xr%d" % i, [128, TN], F32) for i in range(2)]
        xrb2 = [sc("r_xrb%d" % i, [128, TN], BF16) for i in range(2)]
        EA2 = [sc("r_EA%d" % i, [128, TN], F32) for i in range(2)]
        EI2 = [sc("r_EI%d" % i, [128, TN], F32) for i in range(2)]
        A_2 = [sc("r_A%d" % i, [128, TN], F32) for i in range(2)]
        A22 = [sc("r_A2%d" % i, [128, TN], F32) for i in range(2)]
        H2 = [sc("r_H%d" % i, [128, TN], F32) for i in range(2)]
        Hacc = sc("r_Hacc", [128, 2052], F32)
        hcar = sc("r_hcar", [128, 2], F32)
        B_rgw, B_rgst, B_cst, B_Hacc, B_hcar = (Buf() for _ in range(5))
        Bxt2, Bxr2, Bxrb2, BEA2, BEI2, BA2, BA22, BH2 = ([Buf(), Buf()] for _ in range(8))
        for q in range(4):
            k.dma(rgst[:], rgw_d[:, q * 1024:(q + 1) * 1024], B_rgst, writes=[B_rgst])
            k.op(ACT, lambda e, q=q: e.activation(out=rgwb[:, q * 1024:(q + 1) * 1024], in_=rgst[:], func=AF.Identity), reads=[B_rgst], writes=[B_rgw])
        k.op(DVE, lambda e: e.memset(cst[:, 64:65], 1.0), writes=[B_cst])
        k.op(ACT, lambda e: e.activation(out=cst[:, 0:16], in_=vecs[:, V_RLAM:V_RLAM + 16], func=AF.Exp, scale=-1.0), reads=[B_vecs], writes=[B_cst])
        k.op(ACT, lambda e: e.activation(out=cst[:, 0:16], in_=cst[:, 0:16], func=AF.Ln, scale=1.0, bias=cst[:, 64:65]), reads=[B_cst], writes=[B_cst])
        k.op(DVE, lambda e: e.tensor_scalar(out=cst[:, 16:32], in0=cst[:, 0:16], scalar1=-16.0, scalar2=None, op0=ALU.mult), reads=[B_cst], writes=[B_cst])
        k.op(DVE, lambda e: e.tensor_scalar(out=cst[:, 0:16], in0=cst[:, 0:16], scalar1=-8.0, scalar2=None, op0=ALU.mult), reads=[B_cst], writes=[B_cst])
        k.op(DVE, lambda e: e.tensor_scalar(out=cst[:, 32:48], in0=vecs[:, V_RBA:V_RBA + 16], scalar1=-1.0, scalar2=None, op0=ALU.mult), reads=[B_vecs], writes=[B_cst])
        k.op(DVE, lambda e: e.tensor_scalar(out=cst[:, 48:64], in0=vecs[:, V_RBI:V_RBI + 16], scalar1=-1.0, scalar2=None, op0=ALU.mult), reads=[B_vecs], writes=[B_cst])
        nblocks = 8 if stop_after > 2 else 1
        gs = 0
        tc_ = 0
        B_XRd = [Buf() for _ in range(1 + S // TN)]
        for b in range(nblocks):
            k.op(DVE, lambda e: e.memset(Hacc[:], 0.0), writes=[B_Hacc])
            for d in range(2):
                idx = d * 8 + b
                wa = rgwb[:, ((0 * 2 + d) * 8 + b) * 128:((0 * 2 + d) * 8 + b + 1) * 128]
                wi = rgwb[:, ((1 * 2 + d) * 8 + b) * 128:((1 * 2 + d) * 8 + b + 1) * 128]
                k.op(DVE, lambda e: e.memset(hcar[:, 0:1], 0.0), writes=[B_hcar])
                nlt = S // TN
                tiles = [("c", 0)] + [("l", t) for t in (range(nlt) if d == 0 else range(nlt - 1, -1, -1))]
                def tile_geom(kind, t):
                    n = 256 if kind == "c" else TN
                    seqlen = NCTX if kind == "c" else S
                    src = RXC_d if kind == "c" else RXL_d
                    t0 = t * TN
                    lo, hi = max(0, t0 - 2), min(seqlen, t0 + n + 1)
                    xslot = 0 if kind == "c" else 1 + t
                    xoff = 0 if kind == "c" else NCTX + t0
                    return n, src, t0, lo, hi, xslot, xoff

                def emit_load(kind, t, u):
                    n, src, t0, lo, hi, xslot, xoff = tile_geom(kind, t)
                    if d == 0:
                        xt = xt2[u]
                        k.op(DVE, lambda e, xt=xt: e.memset(xt[:, 0:2], 0.0), writes=[Bxt2[u]])
                        k.op(DVE, lambda e, xt=xt, n=n: e.memset(xt[:, n + 2:n + 3], 0.0), writes=[Bxt2[u]])
                        k.dma(xt[:, lo - t0 + 2: hi - t0 + 2], src[b, :, lo:hi], Bxt2[u], writes=[Bxt2[u]])
                    else:
                        k.dma(xr2[u][:, 0:n], XR_d[:, xoff:xoff + n], Bxr2[u], reads=[B_XRd[xslot]], writes=[Bxr2[u]])

                def stage_a(kind, t, u):
                    emit_load(kind, t, u)
                    n, src, t0, lo, hi, xslot, xoff = tile_geom(kind, t)
                    if d == 0:
                        xt, xr = xt2[u], xr2[u]
                        w_ = lambda j: vecs[:, V_RCW + j * 8 + b: V_RCW + j * 8 + b + 1]
                        k.op(ACT, lambda e, n=n, xt=xt, xr=xr: e.activation(out=xr[:, 0:n], in_=xt[:, 0:n], func=AF.Identity, scale=w_(0),
                                                                           bias=vecs[:, V_RCB + b:V_RCB + b + 1]),
                             reads=[Bxt2[u], B_vecs], writes=[Bxr2[u]])
                        for j in range(1, 4):
                            k.op(DVE, lambda e, n=n, j=j, xt=xt, xr=xr: e.scalar_tensor_tensor(out=xr[:, 0:n], in0=xt[:, j:j + n], scalar=w_(j), in1=xr[:, 0:n],
                                                                                              op0=ALU.mult, op1=ALU.add),
                                 reads=[Bxt2[u], B_vecs, Bxr2[u]], writes=[Bxr2[u]])

                stage_a(tiles[0][0], tiles[0][1], tc_ % 2)
                for ti_, (kind, t) in enumerate(tiles):
                    u = tc_ % 2
                    tc_ += 1
                    xt, xr, xrb, EA, EI, A_, A2, H = xt2[u], xr2[u], xrb2[u], EA2[u], EI2[u], A_2[u], A22[u], H2[u]
                    B_xt, B_xr, B_xrb, B_EA, B_EI, B_A, B_A2, B_H = Bxt2[u], Bxr2[u], Bxrb2[u], BEA2[u], BEI2[u], BA2[u], BA22[u], BH2[u]
                    n, src, t0, lo, hi, xslot, xoff = tile_geom(kind, t)
                    if ti_ + 1 < len(tiles):
                        stage_a(tiles[ti_ + 1][0], tiles[ti_ + 1][1], tc_ % 2)
                    if d == 0:
                        k.dma(XR_d[:, xoff:xoff + n], xr[:, 0:n], B_xr, reads=[B_xr], writes=[B_XRd[xslot]])
                    k.op(ACT, lambda e, n=n, xr=xr, xrb=xrb: e.activation(out=xrb[:, 0:n], in_=xr[:, 0:n], func=AF.Identity), reads=[B_xr], writes=[B_xrb])
                    base = 4 * (gs % 2)
                    gs += 1
                    for (w_g, off, dstE, Bd_, nb_) in ((wa, 0, EA, B_EA, cst[:, 32 + idx:33 + idx]), (wi, 2, EI, B_EI, cst[:, 48 + idx:49 + idx])):
                        fns = []
                        for m0 in range(0, n, 512):
                            mw = min(512, n - m0)
                            fns.append(lambda e, w_g=w_g, m0=m0, mw=mw, off=off, base=base, xrb=xrb: e.matmul(
                                psall[:, (base + off) * 512 + m0:(base + off) * 512 + m0 + mw], lhsT=w_g, rhs=xrb[:, m0:m0 + mw],
                                start=True, stop=True))
                        k.pe(fns, reads=[B_rgw, B_xrb], writes=[Bbank[base + off], Bbank[base + off + 1]])
                        k.op(ACT, lambda e, off=off, base=base, n=n, dstE=dstE, nb_=nb_: e.activation(
                            out=dstE[:, 0:n], in_=psall[:, (base + off) * 512:(base + off) * 512 + n], func=AF.Exp, scale=-1.0, bias=nb_),
                            reads=[Bbank[base + off], Bbank[base + off + 1], B_cst], writes=[Bd_])
                    k.op(ACT, lambda e, n=n, EA=EA: e.activation(out=EA[:, 0:n], in_=EA[:, 0:n], func=AF.Ln, scale=1.0, bias=cst[:, 64:65]), reads=[B_EA, B_cst], writes=[B_EA])
                    k.op(ACT, lambda e, n=n, EA=EA: e.activation(out=EA[:, 0:n], in_=EA[:, 0:n], func=AF.Exp, scale=-1.0), reads=[B_EA], writes=[B_EA])
                    k.op(ACT, lambda e, n=n, EI=EI: e.activation(out=EI[:, 0:n], in_=EI[:, 0:n], func=AF.Ln, scale=1.0, bias=cst[:, 64:65]), reads=[B_EI, B_cst], writes=[B_EI])
                    k.op(ACT, lambda e, n=n, EI=EI: e.activation(out=EI[:, 0:n], in_=EI[:, 0:n], func=AF.Exp, scale=-1.0), reads=[B_EI], writes=[B_EI])
                    k.op(ACT, lambda e, n=n, A_=A_, EA=EA: e.activation(out=A_[:, 0:n], in_=EA[:, 0:n], func=AF.Exp, scale=cst[:, idx:idx + 1]), reads=[B_EA, B_cst], writes=[B_A])
                    k.op(ACT, lambda e, n=n, A2=A2, EA=EA: e.activation(out=A2[:, 0:n], in_=EA[:, 0:n], func=AF.Exp, scale=cst[:, 16 + idx:17 + idx]), reads=[B_EA, B_cst], writes=[B_A2])
                    k.op(ACT, lambda e, n=n, A2=A2: e.activation(out=A2[:, 0:n], in_=A2[:, 0:n], func=AF.Ln, scale=-1.0, bias=cst[:, 64:65]), reads=[B_A2, B_cst], writes=[B_A2])
                    k.op(ACT, lambda e, n=n, A2=A2: e.activation(out=A2[:, 0:n], in_=A2[:, 0:n], func=AF.Exp, scale=0.5), reads=[B_A2], writes=[B_A2])
                    k.op(DVE, lambda e, n=n, EI=EI, xr=xr: e.tensor_tensor(out=EI[:, 0:n], in0=EI[:, 0:n], in1=xr[:, 0:n], op=ALU.mult), reads=[B_EI, B_xr], writes=[B_EI])
                    k.op(DVE, lambda e, n=n, EI=EI, A2=A2: e.tensor_tensor(out=EI[:, 0:n], in0=EI[:, 0:n], in1=A2[:, 0:n], op=ALU.mult), reads=[B_EI, B_A2], writes=[B_EI])
                    if d == 0:
                        k.op(DVE, lambda e, n=n, H=H, A_=A_, EI=EI: e.tensor_tensor_scan(out=H[:, 0:n], data0=A_[:, 0:n], data1=EI[:, 0:n], initial=hcar[:, 0:1],
                                                                                      op0=ALU.mult, op1=ALU.add), reads=[B_A, B_EI, B_hcar], writes=[B_H])
                        k.op(DVE, lambda e, n=n, H=H: e.tensor_copy(hcar[:, 0:1], H[:, n - 1:n]), reads=[B_H], writes=[B_hcar])
                    else:
                        k.op(DVE, lambda e, n=n, H=H, A_=A_, EI=EI: e.tensor_tensor_scan(out=H[:, 0:n][:, ::-1], data0=A_[:, 0:n][:, ::-1], data1=EI[:, 0:n][:, ::-1],
                                                                                      initial=hcar[:, 0:1], op0=ALU.mult, op1=ALU.add), reads=[B_A, B_EI, B_hcar], writes=[B_H])
                        k.op(DVE, lambda e, H=H: e.tensor_copy(hcar[:, 0:1], H[:, 0:1]), reads=[B_H], writes=[B_hcar])
                    if kind == "l":
                        per = 2048 // TN
                        tb_, hf_ = t // per, t % per
                        k.op(DVE, lambda e, tb_=tb_, hf_=hf_, H=H: e.scalar_tensor_tensor(out=Hacc[:, hf_ * TN:(hf_ + 1) * TN], in0=H[:, 0:TN], scalar=mk[:, tb_:tb_ + 1],
                                                                                         in1=Hacc[:, hf_ * TN:(hf_ + 1) * TN], op0=ALU.mult, op1=ALU.add),
                             reads=[B_H, B_misc, B_Hacc], writes=[B_Hacc])
                        if hf_ == per - 1:
                            k.op(DVE, lambda e, tb_=tb_, H=H: e.scalar_tensor_tensor(out=Hacc[:, 2048:2049], in0=H[:, TN - 1:TN], scalar=mk[:, 8 + tb_:9 + tb_], in1=Hacc[:, 2048:2049],
                                                                                   op0=ALU.mult, op1=ALU.add), reads=[B_H, B_misc, B_Hacc], writes=[B_Hacc])
                        if hf_ == 0:
                            k.op(DVE, lambda e, tb_=tb_, H=H: e.scalar_tensor_tensor(out=Hacc[:, 2049:2050], in0=H[:, 0:1], scalar=mk[:, 16 + tb_:17 + tb_], in1=Hacc[:, 2049:2050],
                                                                                   op0=ALU.mult, op1=ALU.add), reads=[B_H, B_misc, B_Hacc], writes=[B_Hacc])
            k.op(DVE, lambda e, b=b: e.tensor_tensor(out=RT[:, b, 0:EXT], in0=Hacc[:, 0:EXT], in1=GZ[:, b, 0:EXT], op=ALU.mult),
                 reads=[B_Hacc, B_GZ], writes=[B_RT])
        bar = []
        for b_ in [B_rgw, B_rgst, B_cst, B_Hacc, B_hcar, B_RT, B_GZ] + Bxt2 + Bxr2 + Bxrb2 + BEA2 + BEI2 + BA2 + BA22 + BH2 + Bbank:
            bar.append(b_.w)
            bar += b_.rtoks()
            if b_.dsem is not None:
                bar.append((b_.dsem, b_.dsem.n))
    for E in (PE, ACT, DVE, POOL, SP):
        E.wait(bar)
    if debug and stop_after == 2:
        Bd = Buf()
        k.dma(dbg["QT"][:, :], RT[:, 0, :], Bd, reads=[B_RT])
        SP.wait([(Bd.dsem, Bd.dsem.n)])
        return nc


    gz32 = GZ[:].rearrange("p a b -> p (a b)").bitcast(F32)
    x1t = gz32[:, 0:2048]
    xst = gz32[:, 2048:4096]
    g1b = gz32[:, 4096:6144]
    dgt = gz32[:, 6144:6272]
    B_x1t, B_xst, B_g1b, B_dg, B_X1 = Buf(), Buf(), Buf(), Buf(), Buf()

    def row_broadcast(dst, Bdst, col0):
        for q in range(4):
            for kc4 in range(4):
                kc = q * 4 + kc4
                k.op(DVE, lambda e, kc=kc: e.tensor_scalar(out=dgt, in0=identf[:], scalar1=modx[:, col0 + kc:col0 + kc + 1], scalar2=None, op0=ALU.mult),
                     reads=[B_const, B_mod], writes=[B_dg])
                k.pe([lambda e, kc4=kc4: e.matmul(banks[0][:, kc4 * 128:(kc4 + 1) * 128], lhsT=ones_f[:], rhs=dgt, start=True, stop=True)],
                     reads=[B_dg, B_const], writes=[Bbank[0]])
            k.op(ACT, lambda e, q=q: e.activation(out=dst[:, q * 512:(q + 1) * 512], in_=banks[0], func=AF.Identity), reads=[Bbank[0]], writes=[Bdst])

    row_broadcast(g1b, B_g1b, 32)
    with ExitStack() as es:
        def sc(name, shape, dt):
            return es.enter_context(nc.sbuf_tensor("sb_" + name, shape, dt))
        wo = sc("wo", [128, 16, D], BF16)
        B_wo = Buf()
        for ch in range(16):
            for hf in range(2):
                k.dma(xst[:, 0:1024], wout_d[ch * 128:(ch + 1) * 128, hf * 1024:(hf + 1) * 1024], B_xst, writes=[B_xst])
                k.op(ACT, lambda e, ch=ch, hf=hf: e.activation(out=wo[:, ch, hf * 1024:(hf + 1) * 1024], in_=xst[:, 0:1024], func=AF.Identity), reads=[B_xst], writes=[B_wo])
        for ts in range(17):
            k.dma(xst, xo_d[ts * 128:(ts + 1) * 128, :], B_xst, writes=[B_xst])
            for nb in range(4):
                k.pe([lambda e, ch=ch, nb=nb, ts=ts: e.matmul(banks[nb], lhsT=(AT if ch < 8 else RT)[:, ch % 8, ts * 128:(ts + 1) * 128],
                                                              rhs=wo[:, ch, nb * 512:(nb + 1) * 512], start=(ch == 0), stop=(ch == 15)) for ch in range(16)],
                     reads=[B_AT, B_RT, B_wo], writes=[Bbank[nb]])
            k.op(DVE, lambda e: e.tensor_tensor(out=x1t, in0=psall[:, 0:2048], in1=g1b, op=ALU.mult),
                 reads=[Bbank[0], Bbank[1], Bbank[2], Bbank[3], B_g1b], writes=[B_x1t])
            k.op(DVE, lambda e: e.tensor_tensor(out=x1t, in0=x1t, in1=xst, op=ALU.add), reads=[B_x1t, B_xst], writes=[B_x1t])
            k.dma(X1_d[ts * 128:(ts + 1) * 128, :], x1t, B_x1t, reads=[B_x1t], writes=[B_X1])
        bar = []
        for b_ in [B_wo, B_x1t, B_xst, B_g1b, B_dg, B_AT, B_RT, B_X1] + Bbank:
            bar.append(b_.w)
            bar += b_.rtoks()
            if b_.dsem is not None:
                bar.append((b_.dsem, b_.dsem.n))
    for E in (PE, ACT, DVE, POOL, SP):
        E.wait(bar)
    es_mix.close()
    if debug and stop_after == 4:
        return nc

    with ExitStack() as es:
        def sc(name, shape, dt):
            return es.enter_context(nc.sbuf_tensor("sb_" + name, shape, dt))
        h2T = sc("h2T", [128, 16, EXT], BF16)
        actT = sc("actT", [128, NJ, 512], BF16)
        hTh = actT[:, 0:4, :].rearrange("p a (b c) -> p (a b) c", c=128)
        x1t = sc("f_x1t", [128, D], F32)
        x2t = sc("f_x2t", [128, D], F32)
        g2b = sc("f_g2b", [128, D], F32)
        fgb = sc("f_fgb", [128, D], F32)
        wug = [sc("f_wug%d" % i, [128, 2, D], BF16) for i in range(3)]
        wdb = [sc("f_wd%d" % i, [128, D], BF16) for i in range(3)]
        cvt = sc("f_cvt", [128, 512], F32)
        glt = sc("f_glt", [128, 512], F32)
        fr = sc("f_fr", [128, 4], F32)
        B_h2T, B_hTh, B_act, B_x1t, B_x2t, B_g2b, B_fgb, B_cvt, B_glt, B_fr = (Buf() for _ in range(10))
        B_hTh = B_act
        Bwst = []
        Bwug = [Buf(), Buf(), Buf()]
        Bwd = [Buf(), Buf(), Buf()]
        dgt = x2t[:, 0:128]
        B_dg = B_x2t

        def row_broadcast2(dst, Bdst, col0):
            for q in range(4):
                for kc4 in range(4):
                    kc = q * 4 + kc4
                    k.op(DVE, lambda e, kc=kc: e.tensor_scalar(out=dgt, in0=identf[:], scalar1=modx[:, col0 + kc:col0 + kc + 1], scalar2=None, op0=ALU.mult),
                         reads=[B_const, B_mod], writes=[B_dg])
                    k.pe([lambda e, kc4=kc4: e.matmul(banks[0][:, kc4 * 128:(kc4 + 1) * 128], lhsT=ones_f[:], rhs=dgt, start=True, stop=True)],
                         reads=[B_dg, B_const], writes=[Bbank[0]])
                k.op(ACT, lambda e, q=q: e.activation(out=dst[:, q * 512:(q + 1) * 512], in_=banks[0], func=AF.Identity), reads=[Bbank[0]], writes=[Bdst])

        row_broadcast2(g2b, B_g2b, 80)
        k.dma(fgb[:], fing_d[:, :], B_fgb, writes=[B_fgb])
        for ts in range(17):
            k.dma(x1t[:], X1_d[ts * 128:(ts + 1) * 128, :], B_x1t, writes=[B_x1t])
            if ts < 16:
                norm_transpose(x1t[:], B_x1t, 128, a2, modx, lambda kc, ts=ts: h2T[:, kc, 1 + ts * 128: 1 + (ts + 1) * 128], B_h2T, boff=48)
            else:
                norm_transpose(x1t[:], B_x1t, 128, a2, modx, lambda kc: hTh[:, kc, :], B_hTh, boff=48)
                k.op(POOL, lambda e: e.tensor_scalar(out=h2T[:, :, 0:1], in0=hTh[:, :, 0:1], scalar1=mk[:, 24:25], scalar2=None, op0=ALU.mult),
                     reads=[B_hTh, B_misc], writes=[B_h2T])
                k.op(POOL, lambda e: e.tensor_scalar(out=h2T[:, :, 2049:2050], in0=hTh[:, :, 1:2], scalar1=mk[:, 25:26], scalar2=None, op0=ALU.mult),
                     reads=[B_hTh, B_misc], writes=[B_h2T])
        wcnt = {"s": 0, "ug": 0, "d": 0}

        def load_cast(src_ap, dst_ap, Bdst_):
            s_ = wcnt["s"] % 2
            wcnt["s"] += 1
            k.dma(wst[s_][:], src_ap, Bwst[s_], writes=[Bwst[s_]])
            k.op(ACT, lambda e, s_=s_: e.activation(out=dst_ap, in_=wst[s_][:], func=AF.Identity), reads=[Bwst[s_]], writes=[Bdst_])

        nwin = 4 if stop_after > 5 else 1
        for w in range(nwin):
            c0 = 512 * w
            for j in range(NJ):
                u_ = wcnt["ug"] % 3
                wcnt["ug"] += 1
                k.dma(wug[u_][:, 0, :], WB_d[0, j, :, :], Bwug[u_], writes=[Bwug[u_]])
                k.dma(wug[u_][:, 1, :], WB_d[1, j, :, :], Bwug[u_], writes=[])
                Bwug[u_].w = (Bwug[u_].dsem, Bwug[u_].dsem.n)
                k.pe([lambda e, kc=kc, u_=u_: e.matmul(banks[4], lhsT=wug[u_][:, 0, kc * 128:(kc + 1) * 128], rhs=h2T[:, kc, c0 + 1:c0 + 513],
                                                       start=(kc == 0), stop=(kc == 15)) for kc in range(16)],
                     reads=[Bwug[u_], B_h2T], writes=[Bbank[4]])
                k.pe([lambda e, kc=kc, u_=u_: e.matmul(banks[5], lhsT=wug[u_][:, 1, kc * 128:(kc + 1) * 128], rhs=h2T[:, kc, c0:c0 + 512],
                                                       start=(kc == 0), stop=(kc == 15)) for kc in range(16)],
                     reads=[Bwug[u_], B_h2T], writes=[Bbank[5]])
                k.pe([lambda e, kc=kc, u_=u_: e.matmul(banks[6][:, 0:2], lhsT=wug[u_][:, 1, kc * 128:(kc + 1) * 128], rhs=h2T[:, kc, c0 + 512:c0 + 514],
                                                       start=(kc == 0), stop=(kc == 15)) for kc in range(16)],
                     reads=[Bwug[u_], B_h2T], writes=[Bbank[6]])
                gps = psall[:, 5 * 512: 5 * 512 + 514]
                cw = lambda t_: vecs[:, V_FCW + t_ * 43 + j: V_FCW + t_ * 43 + j + 1]
                k.op(DVE, lambda e: e.tensor_scalar(out=cvt[:], in0=gps[:, 0:512], scalar1=cw(0), scalar2=None, op0=ALU.mult),
                     reads=[Bbank[5], Bbank[6], B_vecs], writes=[B_cvt])
                k.op(DVE, lambda e: e.scalar_tensor_tensor(out=cvt[:], in0=gps[:, 1:513], scalar=cw(1), in1=cvt[:], op0=ALU.mult, op1=ALU.add),
                     reads=[Bbank[5], Bbank[6], B_vecs, B_cvt], writes=[B_cvt])
                k.op(DVE, lambda e: e.scalar_tensor_tensor(out=cvt[:], in0=gps[:, 2:514], scalar=cw(2), in1=cvt[:], op0=ALU.mult, op1=ALU.add),
                     reads=[Bbank[5], Bbank[6], B_vecs, B_cvt], writes=[B_cvt])
                k.op(ACT, lambda e, j=j: e.activation(out=glt[:], in_=cvt[:], func=AF.Gelu_apprx_tanh, bias=vecs[:, V_FCB + j:V_FCB + j + 1]),
                     reads=[B_cvt, B_vecs], writes=[B_glt])
                k.op(DVE, lambda e, j=j: e.tensor_tensor(out=actT[:, j, :], in0=banks[4], in1=glt[:], op=ALU.mult),
                     reads=[Bbank[4], B_glt], writes=[B_act])
            for pair in range(2):
                for j in range(NJ):
                    d_ = wcnt["d"] % 3
                    wcnt["d"] += 1
                    k.dma(wdb[d_][:], WB_d[2, j, :, :], Bwd[d_], writes=[Bwd[d_]])
                    fns = []
                    for t2_ in range(2):
                        ts4 = pair * 2 + t2_
                        for nb in range(4):
                            fns.append(lambda e, j=j, ts4=ts4, nb=nb, t2_=t2_, d_=d_: e.matmul(
                                banks[t2_ * 4 + nb], lhsT=actT[:, j, ts4 * 128:(ts4 + 1) * 128], rhs=wdb[d_][:, nb * 512:(nb + 1) * 512],
                                start=(j == 0), stop=(j == NJ - 1)))
                    k.pe(fns, reads=[B_act, Bwd[d_]], writes=Bbank)
                for t2_ in range(2):
                    ts4 = pair * 2 + t2_
                    row0 = c0 + ts4 * 128
                    k.dma(x1t[:], X1_d[row0:row0 + 128, :], B_x1t, writes=[B_x1t])
                    k.op(DVE, lambda e, t2_=t2_: e.tensor_tensor(out=x2t[:], in0=psall[:, t2_ * 2048:(t2_ + 1) * 2048], in1=g2b[:], op=ALU.mult),
                         reads=Bbank + [B_g2b], writes=[B_x2t])
                    k.op(DVE, lambda e: e.tensor_tensor(out=x2t[:], in0=x2t[:], in1=x1t[:], op=ALU.add), reads=[B_x2t, B_x1t], writes=[B_x2t])
                    k.op(ACT, lambda e: e.activation(out=junk[:, :], in_=x2t[:], func=AF.Square, accum_out=fr[:, 0:1]), reads=[B_x2t], writes=[B_junk, B_fr])
                    k.op(ACT, lambda e: e.activation(out=fr[:, 1:2], in_=fr[:, 0:1], func=AF.Ln, scale=1.0 / D, bias=epsc[:, 0:1]), reads=[B_fr, B_const], writes=[B_fr])
                    k.op(ACT, lambda e: e.activation(out=fr[:, 2:3], in_=fr[:, 1:2], func=AF.Exp, scale=-0.5), reads=[B_fr], writes=[B_fr])
                    k.op(DVE, lambda e: e.scalar_tensor_tensor(out=x1t[:], in0=x2t[:], scalar=fr[:, 2:3], in1=fgb[:], op0=ALU.mult, op1=ALU.mult),
                         reads=[B_x2t, B_fr, B_fgb, B_x1t], writes=[B_x1t])
                    k.dma(out_d[row0:row0 + 128, :], x1t[:], B_x1t, reads=[B_x1t])
        fin = [(B_x1t.dsem, B_x1t.dsem.n)]
        SP.wait(fin)
        bar = []
        for b_ in [B_h2T, B_hTh, B_act, B_x1t, B_x2t, B_g2b, B_fgb, B_cvt, B_glt, B_fr] + Bwst + Bwug + Bwd + Bbank:
            bar.append(b_.w)
            bar += b_.rtoks()
    for E in (PE, ACT, DVE, POOL, SP):
        E.wait(bar)
    return nc


def rope_tables(tok):
    inv = (10000.0 ** (-np.arange(16, dtype=np.float32) / 16)).astype(np.float32)
    tok = np.asarray(tok)
    row = (tok // 64).astype(np.float32)
    col = (tok % 64).astype(np.float32)
    ang = np.stack([row[None, :] * inv[:, None], col[None, :] * inv[:, None]], 0).astype(np.float32)
    cos = np.cos(ang).astype(np.float32)
    sin = np.sin(ang).astype(np.float32)
    C = np.zeros((128, len(tok)), np.float32)
    Sn = np.zeros((128, len(tok)), np.float32)
    for c in range(2):
        for ax in range(2):
            for half in range(2):
                p0 = c * 64 + ax * 32 + half * 16
                C[p0:p0 + 16] = cos[ax]
                Sn[p0:p0 + 16] = sin[ax] * (-1.0 if half == 0 else 1.0)
    return C, Sn


def pcl(v):
    v = np.asarray(v, np.float32)
    return np.ascontiguousarray(v.reshape(-1, 128).T)


def host_inputs(inp):
    f32 = np.float32
    x = np.ascontiguousarray(inp["x"][0], f32)
    ctx = np.ascontiguousarray(inp["ctx"][0], f32)
    shared = {}
    shared["x"] = x
    shared["ctx"] = ctx
    cv = np.stack([pcl(inp["c"][0]), pcl(inp["c_ctx"])], -1)
    shared["cvec"] = np.ascontiguousarray(cv.reshape(128, 32))
    vecs = np.zeros((128, NV), f32)
    vecs[:, V_BADA:V_BADA + 96] = pcl(inp["b_ada"][0])
    vecs[:, V_N1G:V_N1G + 16] = pcl(inp["norm1_g"][0])
    vecs[:, V_N2G:V_N2G + 16] = pcl(inp["norm2_g"][0])
    for j in range(3):
        vecs[:, V_FCW + j * 43:V_FCW + (j + 1) * 43] = pcl(inp["ffn_conv_w"][0, j])
    vecs[:, V_FCB:V_FCB + 43] = pcl(inp["ffn_conv_b"][0])
    for j in range(4):
        vecs[:, V_RCW + j * 8:V_RCW + (j + 1) * 8] = pcl(inp["rec_conv_w"][0, j])
    vecs[:, V_RCB:V_RCB + 8] = pcl(inp["rec_conv_b"][0])
    for d in range(2):
        vecs[:, V_RBA + d * 8:V_RBA + (d + 1) * 8] = pcl(inp["rg_ba"][0, d])
        vecs[:, V_RBI + d * 8:V_RBI + (d + 1) * 8] = pcl(inp["rg_bi"][0, d])
        vecs[:, V_RLAM + d * 8:V_RLAM + (d + 1) * 8] = pcl(inp["rg_lambda"][0, d])
    shared["vecs"] = vecs
    shared["wada"] = np.ascontiguousarray(inp["w_ada"][0], f32)
    shared["win"] = np.ascontiguousarray(inp["w_in"][0], f32)
    shared["wout"] = np.ascontiguousarray(inp["w_out"][0], f32)
    for nm, key in (("wup", "w_up"), ("wgate", "w_gate")):
        w = np.asarray(inp[key][0], f32).reshape(16, 128, NJ, 128)
        shared[nm] = np.ascontiguousarray(w.transpose(2, 1, 0, 3).reshape(NJ, 128, 2048))
    shared["wdown"] = np.ascontiguousarray(np.asarray(inp["w_down"][0], f32).reshape(NJ, 128, 2048))
    rg = np.stack([np.asarray(inp["rg_wa"][0], f32), np.asarray(inp["rg_wi"][0], f32)], 0)
    shared["rgw"] = np.ascontiguousarray(rg.transpose(3, 0, 1, 2, 4).reshape(128, 32 * 128))
    C, Sn = rope_tables(np.arange(S))
    shared["cos"] = C
    shared["sin"] = Sn
    perm = np.zeros((128, 128), f32)
    for p in range(128):
        perm[p ^ 16, p] = 1.0
    shared["perm"] = perm
    shared["identf"] = np.eye(128, dtype=f32)
    shared["identb"] = np.eye(128).astype(ml_dtypes.bfloat16)
    shared["dlam"] = np.ascontiguousarray(np.broadcast_to(np.asarray(inp["diff_lambda"][0], f32).reshape(1, 256), (128, 256)))
    shared["subg"] = np.ascontiguousarray(np.broadcast_to(np.asarray(inp["subln_g"][0], f32).reshape(1, 128), (128, 128)))
    shared["fing"] = np.ascontiguousarray(np.broadcast_to(np.asarray(inp["final_g"], f32).reshape(1, D), (128, D)))
    maps = []
    for c in range(NCORES):
        m = dict(shared)
        xo = np.zeros((17 * 128, D), f32)
        t0 = c * OWN
        xo[0:OWN] = x[t0:t0 + OWN]
        toks = np.zeros(EXT, np.int64)
        toks[0:OWN] = np.arange(t0, t0 + OWN)
        if c > 0:
            xo[OWN] = x[t0 - 1]
            toks[OWN] = t0 - 1
        if c < NCORES - 1:
            xo[OWN + 1] = x[t0 + OWN]
            toks[OWN + 1] = t0 + OWN
        m["xo"] = xo
        Co, So = rope_tables(toks)
        Cp = np.zeros((128, 17 * 128), f32)
        Sp_ = np.zeros((128, 17 * 128), f32)
        Cp[:, :EXT] = Co
        Sp_[:, :EXT] = So
        m["coso"] = Cp
        m["sino"] = Sp_
        mk = np.zeros((128, 32), f32)
        mk[:, c] = 1.0
        if c > 0:
            mk[:, 8 + c - 1] = 1.0
            mk[:, 24] = 1.0
        if c < NCORES - 1:
            mk[:, 16 + c + 1] = 1.0
            mk[:, 25] = 1.0
        m["mk"] = mk
        maps.append(m)
    return maps


STOP_AFTER = 99


def kernel(**inputs):
    maps = host_inputs(inputs)
    nc = build_program(stop_after=STOP_AFTER)
    res = run_bass_kernel_spmd(nc, maps, core_ids=list(range(NCORES)))
    out = np.concatenate([np.asarray(r["out"], np.float32) for r in res.results], 0)
    return out.reshape(1, S, D)
```

```python
import os
from contextlib import ExitStack
import numpy as np
import ml_dtypes
import concourse.bass as bass
import concourse.mybir as mybir
from concourse.bass_utils import run_bass_kernel_spmd

F32 = mybir.dt.float32
BF16 = mybir.dt.bfloat16
AF = mybir.ActivationFunctionType
ALU = mybir.AluOpType

D = 2048
S = 16384
NCTX = 256
DFF = 5504
NJ = 43
NKEY = S + NCTX
NKT = NKEY // 128
OWN = 2048
EXT = 2050
NCORES = 8
EPS = 1e-6
SUBLN_EPS = 1e-5
LAM_INIT = 0.8 - 0.6
TT = 256
SKIP = os.environ.get('P1SKIP', '')

V_BADA = 0
V_N1G = 96
V_N2G = 112
V_FCW = 128
V_FCB = V_FCW + 129
V_RCW = V_FCB + 43
V_RCB = V_RCW + 32
V_RBA = V_RCB + 8
V_RBI = V_RBA + 16
V_RLAM = V_RBI + 16
NV = V_RLAM + 16


class Sem:
    _k = 0

    def __init__(self, nc, name):
        self.h = nc.alloc_semaphore(name)
        self.n = 0
        Sem._k += 1
        self.key = Sem._k


class Eng:
    def __init__(self, nc, eng, name, is_pe=False):
        self.e = eng
        self.sem = Sem(nc, "s_" + name)
        self.seen = {}
        self.is_pe = is_pe

    def wait(self, toks):
        best = {}
        for t in toks:
            if t is None:
                continue
            sem, val = t
            if self.is_pe and sem is self.sem:
                continue
            if self.seen.get(sem.key, 0) >= val:
                continue
            if sem.key not in best or best[sem.key][1] < val:
                best[sem.key] = (sem, val)
        for sem, val in best.values():
            self.e.wait_ge(sem.h, val)
            self.seen[sem.key] = val

    def mark(self, ins):
        self.sem.n += 1
        ins.then_inc(self.sem.h, 1)
        return (self.sem, self.sem.n)


class Buf:
    def __init__(self, name=""):
        self.name = name
        self.w = None
        self.r = {}
        self.dsem = None

    def rtoks(self):
        return list(self.r.values())

    def add_r(self, tok):
        sem, val = tok
        if sem.key not in self.r or self.r[sem.key][1] < val:
            self.r[sem.key] = tok


class K:
    def __init__(self, nc):
        self.nc = nc
        self.PE = Eng(nc, nc.tensor, "pe", is_pe=True)
        self.ACT = Eng(nc, nc.scalar, "act")
        self.DVE = Eng(nc, nc.vector, "dve")
        self.POOL = Eng(nc, nc.gpsimd, "pool")
        self.SP = Eng(nc, nc.sync, "sp")
        self.nsem = 5

    def _deps(self, reads, writes):
        toks = []
        for b in reads:
            toks.append(b.w)
        for b in writes:
            toks.append(b.w)
            toks += b.rtoks()
        return toks

    def _commit(self, tok, reads, writes):
        for b in reads:
            b.add_r(tok)
        for b in writes:
            b.w = tok
            b.r = {}

    def op(self, E, fn, reads=(), writes=()):
        E.wait(self._deps(reads, writes))
        ins = fn(E.e)
        tok = E.mark(ins)
        self._commit(tok, reads, writes)
        return tok

    def pe(self, fns, reads=(), writes=()):
        E = self.PE
        E.wait(self._deps(reads, writes))
        ins = None
        for f in fns:
            ins = f(E.e)
        tok = E.mark(ins)
        self._commit(tok, reads, writes)
        return tok

    def dma(self, out, in_, sbuf, reads=(), writes=(), eng=None):
        E = eng or self.SP
        if sbuf.dsem is None:
            sbuf.dsem = Sem(self.nc, "d_%d" % self.nsem)
            self.nsem += 1
        E.wait(self._deps(reads, writes))
        sbuf.dsem.n += 16
        E.e.dma_start(out=out, in_=in_).then_inc(sbuf.dsem.h, 16)
        tok = (sbuf.dsem, sbuf.dsem.n)
        self._commit(tok, reads, writes)
        return tok


def build_program(debug=False, stop_after=99):
    nc = bass.Bass("TRN2", target_bir_lowering=False)
    k = K(nc)
    PE, ACT, DVE, POOL, SP = k.PE, k.ACT, k.DVE, k.POOL, k.SP

    def din(name, shape, dt=F32):
        return nc.dram_tensor(name, list(shape), dt, kind="ExternalInput").ap()

    x_d = din("x", [S, D])
    ctx_d = din("ctx", [NCTX, D])
    xo_d = din("xo", [17 * 128, D])
    cvec_d = din("cvec", [128, 32])
    vecs_d = din("vecs", [128, NV])
    wada_d = din("wada", [D, 6 * D])
    win_d = din("win", [D, 5120])
    wout_d = din("wout", [D, D])
    wup_d = din("wup", [NJ, 128, 2048])
    wgate_d = din("wgate", [NJ, 128, 2048])
    wdown_d = din("wdown", [NJ, 128, 2048])
    rgw_d = din("rgw", [128, 32 * 128])
    cos_d = din("cos", [128, S])
    sin_d = din("sin", [128, S])
    coso_d = din("coso", [128, 17 * 128])
    sino_d = din("sino", [128, 17 * 128])
    perm_d = din("perm", [128, 128])
    ident_d = din("identf", [128, 128])
    identb_d = din("identb", [128, 128], BF16)
    dlam_d = din("dlam", [128, 256])
    subg_d = din("subg", [128, 128])
    fing_d = din("fing", [128, D])
    mk_d = din("mk", [128, 32])
    out_d = nc.dram_tensor("out", [OWN, D], F32, kind="ExternalOutput").ap()

    KT_d = nc.dram_tensor("KT", [8, 128, NKEY], BF16).ap()
    VV_d = nc.dram_tensor("VV", [8, 128, NKT, 129], BF16).ap()
    RXL_d = nc.dram_tensor("RXL", [8, 128, S], F32).ap()
    RXC_d = nc.dram_tensor("RXC", [8, 128, NCTX], F32).ap()
    dbg = {}
    if debug:
        dbg["mod"] = nc.dram_tensor("dbg_mod", [128, 192], F32, kind="ExternalOutput").ap()
        dbg["KT"] = nc.dram_tensor("dbg_KT", [8, 128, 1024], BF16, kind="ExternalOutput").ap()
        dbg["VV"] = nc.dram_tensor("dbg_VV", [8, 128, 8, 129], BF16, kind="ExternalOutput").ap()
        dbg["RX"] = nc.dram_tensor("dbg_RX", [8, 128, 1024], F32, kind="ExternalOutput").ap()
        dbg["AT"] = nc.dram_tensor("dbg_AT", [128, 17 * 128], BF16, kind="ExternalOutput").ap()
        dbg["QT"] = nc.dram_tensor("dbg_QT", [128, 17 * 128], BF16, kind="ExternalOutput").ap()
        dbg["GZ"] = nc.dram_tensor("dbg_GZ", [128, 17 * 128], BF16, kind="ExternalOutput").ap()

    def sb(name, shape, dt):
        return nc.alloc_sbuf_tensor("sb_" + name, shape, dt)
    pst = nc.alloc_psum_tensor

    vecs = sb("vecs", [128, NV], F32)
    modx = sb("modx", [128, 96], F32)
    modc = sb("modc", [128, 96], F32)
    a1 = sb("a1", [128, 16], F32)
    a1c = sb("a1c", [128, 16], F32)
    a2 = sb("a2", [128, 16], F32)
    modacc = sb("modacc", [128, 192], F32)
    identb = sb("identb", [128, 128], BF16)
    identf = sb("identf", [128, 128], F32)
    permT = sb("permT", [128, 128], F32)
    ones_f = sb("ones_f", [128, 128], F32)
    B_vecs = Buf("vecs")
    B_mod = Buf("mod")
    B_const = Buf("const")

    k.dma(vecs[:], vecs_d[:, :], B_vecs, writes=[B_vecs])
    k.dma(identb[:], identb_d[:, :], B_const, writes=[B_const])
    k.dma(identf[:], ident_d[:, :], B_const, writes=[B_const])
    k.dma(permT[:], perm_d[:, :], B_const, writes=[B_const])
    k.op(k.DVE, lambda e: e.memset(ones_f[:], 1.0), writes=[B_const])

    psall = pst("psall", [128, 4096], F32)
    banks = [psall[:, i * 512:(i + 1) * 512] for i in range(8)]
    Bbank = [Buf("bank%d" % i) for i in range(8)]
    tpbs = [banks[6].bitcast(BF16), banks[7].bitcast(BF16)]
    Btp = [Bbank[6], Bbank[7]]

    with nc.sbuf_tensor("sb_wk0", [128, 6 * D], F32) as wk0, nc.sbuf_tensor("sb_wk1", [128, 6 * D], F32) as wk1, \
            nc.sbuf_tensor("sb_cvec", [128, 32], F32) as cvec, nc.sbuf_tensor("sb_scv", [128, 32], F32) as scv:
        wk = [wk0, wk1]
        Bwk = [Buf("wk0"), Buf("wk1")]
        B_cv = Buf("cvec")
        B_scv = Buf("scv")
        k.dma(cvec[:], cvec_d[:, :], B_cv, writes=[B_cv])
        k.op(ACT, lambda e: e.activation(out=scv[:], in_=cvec[:], func=AF.Silu), reads=[B_cv], writes=[B_scv])
        macc = modacc
        B_macc = Buf("macc")
        for kc in range(16):
            s = kc % 2
            for q in range(4):
                k.dma(wk[s][:, q * 3072:(q + 1) * 3072], wada_d[kc * 128:(kc + 1) * 128, q * 3072:(q + 1) * 3072],
                      Bwk[s], writes=[Bwk[s]] if q == 0 else [])
            Bwk[s].w = (Bwk[s].dsem, Bwk[s].dsem.n)
            psm = banks[s]
            fns = []
            for j in range(96):
                fns.append(lambda e, j=j, s=s, kc=kc, psm=psm: e.matmul(
                    psm[:, 2 * j:2 * j + 2], lhsT=wk[s][:, j * 128:(j + 1) * 128], rhs=scv[:, 2 * kc:2 * kc + 2],
                    start=True, stop=True))
            k.pe(fns, reads=[Bwk[s], B_scv], writes=[Bbank[s]])
            if kc == 0:
                k.op(DVE, lambda e, psm=psm: e.tensor_copy(macc[:], psm[:, 0:192]), reads=[Bbank[s]], writes=[B_macc])
            else:
                k.op(DVE, lambda e, psm=psm: e.tensor_tensor(out=macc[:], in0=macc[:], in1=psm[:, 0:192], op=ALU.add),
                     reads=[Bbank[s], B_macc], writes=[B_macc])
        psv = macc[:].rearrange("p (j t) -> p j t", t=2)
        k.op(DVE, lambda e: e.tensor_tensor(out=modx[:], in0=psv[:, :, 0], in1=vecs[:, V_BADA:V_BADA + 96], op=ALU.add),
             reads=[B_macc, B_vecs], writes=[B_mod])
        k.op(DVE, lambda e: e.tensor_tensor(out=modc[:], in0=psv[:, :, 1], in1=vecs[:, V_BADA:V_BADA + 96], op=ALU.add),
             reads=[B_macc, B_vecs], writes=[B_mod])
        k.op(DVE, lambda e: e.scalar_tensor_tensor(out=a1[:], in0=modx[:, 16:32], scalar=1.0, in1=vecs[:, V_N1G:V_N1G + 16],
                                                   op0=ALU.add, op1=ALU.mult), reads=[B_mod, B_vecs], writes=[B_mod])
        k.op(DVE, lambda e: e.scalar_tensor_tensor(out=a1c[:], in0=modc[:, 16:32], scalar=1.0, in1=vecs[:, V_N1G:V_N1G + 16],
                                                   op0=ALU.add, op1=ALU.mult), reads=[B_mod, B_vecs], writes=[B_mod])
        k.op(DVE, lambda e: e.scalar_tensor_tensor(out=a2[:], in0=modx[:, 64:80], scalar=1.0, in1=vecs[:, V_N2G:V_N2G + 16],
                                                   op0=ALU.add, op1=ALU.mult), reads=[B_mod, B_vecs], writes=[B_mod])
        if debug:
            B_dm = Buf("dbgmod")
            k.dma(dbg["mod"][:, 0:96], modx[:], B_dm, reads=[B_mod])
            k.dma(dbg["mod"][:, 96:192], modc[:], B_dm, reads=[B_mod])
        bar = [Bwk[0].w, Bwk[1].w, Bbank[0].w, Bbank[1].w, B_mod.w] + Bwk[0].rtoks() + Bwk[1].rtoks() + B_scv.rtoks()
    for E in (PE, ACT, DVE, POOL, SP):
        E.wait(bar)
    if debug and stop_after == 0:
        SP.wait([(B_dm.dsem, B_dm.dsem.n)])
        return nc
    if stop_after == 0:
        return nc

    epsc = sb("epsc", [128, 2], F32)
    k.op(DVE, lambda e: e.memset(epsc[:, 0:1], EPS), writes=[B_const])
    k.op(DVE, lambda e: e.memset(epsc[:, 1:2], SUBLN_EPS), writes=[B_const])
    junk = sb("junk", [128, D], BF16)
    B_junk = Buf("junk")
    xn = [sb("xn%d" % i, [128, D], BF16) for i in range(2)]
    Bxn = [Buf("xn%d" % i) for i in range(2)]
    ssq = [sb("ssq%d" % i, [128, 4], F32) for i in range(2)]
    Bssq = [Buf("ssq%d" % i) for i in range(2)]
    cnt = {"nt": 0, "tp": 0, "ev": 0}

    def norm_transpose(src_ap, Bsrc, rows, avec, bvec, dst_fn, Bdst, boff=0):
        i = cnt["nt"] % 2
        cnt["nt"] += 1
        sq = ssq[i]
        k.op(ACT, lambda e: e.activation(out=junk[0:rows, :], in_=src_ap, func=AF.Square, accum_out=sq[0:rows, 0:1]),
             reads=[Bsrc], writes=[B_junk, Bssq[i]])
        k.op(ACT, lambda e: e.activation(out=sq[0:rows, 1:2], in_=sq[0:rows, 0:1], func=AF.Ln, scale=1.0 / D, bias=epsc[0:rows, 0:1]),
             reads=[Bssq[i], B_const], writes=[Bssq[i]])
        k.op(ACT, lambda e: e.activation(out=sq[0:rows, 2:3], in_=sq[0:rows, 1:2], func=AF.Exp, scale=-0.5),
             reads=[Bssq[i]], writes=[Bssq[i]])
        k.op(DVE, lambda e: e.tensor_scalar(out=xn[i][0:rows, :], in0=src_ap, scalar1=sq[0:rows, 2:3], scalar2=None, op0=ALU.mult),
             reads=[Bsrc, Bssq[i]], writes=[Bxn[i]])
        for g in range(2):
            hb = cnt["tp"] % 2
            cnt["tp"] += 1
            tpb = tpbs[hb]
            fns = []
            for q in range(8):
                kc = g * 8 + q
                fns.append(lambda e, kc=kc, q=q, tpb=tpb: e.transpose(tpb[:, q * 128: q * 128 + rows],
                                                                      xn[i][0:rows, kc * 128:(kc + 1) * 128], identb[0:rows, 0:rows]))
            k.pe(fns, reads=[Bxn[i], B_const], writes=[Btp[hb]])
            for q in range(8):
                kc = g * 8 + q
                src = tpb[:, q * 128: q * 128 + rows]
                k.op(ACT, lambda e, kc=kc, src=src: e.activation(out=dst_fn(kc), in_=src, func=AF.Identity,
                                                                scale=avec[:, kc:kc + 1], bias=bvec[:, boff + kc:boff + kc + 1]),
                     reads=[Btp[hb], B_mod], writes=[Bdst])

    with ExitStack() as es:
        def sc(name, shape, dt):
            return es.enter_context(nc.sbuf_tensor("sb_" + name, shape, dt))
        wkvr = sc("wkvr", [128, 16, 3072], BF16)
        wst0 = sc("wst0", [128, 1024], F32); wst1 = sc("wst1", [128, 1024], F32)
        xs0 = sc("xs0", [128, 2, D], F32); xs1 = sc("xs1", [128, 2, D], F32)
        hT0 = sc("hT0", [128, 16, TT], BF16); hT1 = sc("hT1", [128, 16, TT], BF16)
        cs0 = sc("cs0", [128, 2, TT], F32); cs1 = sc("cs1", [128, 2, TT], F32)
        k32a = sc("k32a", [128, TT], F32); k32b = sc("k32b", [128, TT], F32)
        t1a = sc("t1a", [128, TT], F32); t1b = sc("t1b", [128, TT], F32)
        t2a = sc("t2a", [128, TT], F32); t2b = sc("t2b", [128, TT], F32)
        ko = sc("ko", [128, 4, TT], BF16); rxo = sc("rxo", [128, 4, TT], F32)
        vt0 = sc("vt0", [128, 8, 2, 129], BF16); vt1 = sc("vt1", [128, 8, 2, 129], BF16)
        wst = [wst0, wst1]
        Bwst = [Buf(), Buf()]
        B_w = Buf("wkvr")
        n = 0
        for kc in range(16 if 'W' not in SKIP else 0):
            for gi in range(3):
                s_ = n % 2
                n += 1
                k.dma(wst[s_][:], win_d[kc * 128:(kc + 1) * 128, 1024 + gi * 1024: 2048 + gi * 1024], Bwst[s_], writes=[Bwst[s_]])
                k.op(ACT, lambda e, s_=s_, kc=kc, gi=gi: e.activation(out=wkvr[:, kc, gi * 1024:(gi + 1) * 1024], in_=wst[s_][:], func=AF.Identity),
                     reads=[Bwst[s_]], writes=[B_w])
        xs = [xs0, xs1]
        Bxs = [Buf(), Buf()]
        hT = [hT0, hT1]
        BhT = [Buf(), Buf()]
        cs = [cs0, cs1]
        Bcs = [Buf(), Buf()]
        k32 = [k32a, k32b]
        Bk32 = [Buf(), Buf()]
        t1 = [t1a, t1b]
        Bt1 = [Buf(), Buf()]
        t2 = [t2a, t2b]
        Bt2 = [Buf(), Buf()]
        Bko = [Buf() for _ in range(4)]
        Brxo = [Buf() for _ in range(4)]
        vt = [vt0, vt1]
        Bvt = [Buf(), Buf()]
        for v_ in vt:
            k.op(POOL, lambda e, v_=v_: e.memset(v_[:, :, :, 128:129], 1.0), writes=[Bvt[0], Bvt[1]])
        B_scr = Buf("scratch")
        ntiles = 1 + S // TT
        if stop_after == 1 and debug:
            ntiles = 5
        NT_P1 = ntiles
        gslot = {"o": 0, "p": 0, "v": 0, "ko": 0, "rx": 0, "k32": 0}

        def load_tile(ti):
            s_ = ti % 2
            src = ctx_d if ti == 0 else x_d
            r0 = 0 if ti == 0 else (ti - 1) * TT
            k.dma(xs[s_][:], src[r0:r0 + TT, :].rearrange("(s p) d -> p s d", p=128), Bxs[s_], writes=[Bxs[s_]])
            if ti > 0:
                k.dma(cs[s_][:, 0, :], cos_d[:, r0:r0 + TT], Bcs[s_], writes=[Bcs[s_]])
                k.dma(cs[s_][:, 1, :], sin_d[:, r0:r0 + TT], Bcs[s_], writes=[])
                Bcs[s_].w = (Bcs[s_].dsem, Bcs[s_].dsem.n)

        def nt_tile(ti):
            s_ = ti % 2
            av, bv = (a1c, modc) if ti == 0 else (a1, modx)
            for su in range(2):
                norm_transpose(xs[s_][:, su, :], Bxs[s_], 128, av, bv,
                               lambda kc, su=su, s_=s_: hT[s_][:, kc, su * 128:(su + 1) * 128], BhT[s_])

        load_tile(0)
        nt_tile(0)
        for ti in range(ntiles):
            s_ = ti % 2
            if ti + 1 < ntiles:
                load_tile(ti + 1)
            key0 = 0 if ti == 0 else NCTX + (ti - 1) * TT
            pending = []
            for h in range(8):
                oslot = gslot["o"] % 2
                gslot["o"] += 1
                ob = banks[oslot][:, 0:TT]
                Bo = Bbank[oslot]
                k.pe([lambda e, kc=kc, h=h, ob=ob: e.matmul(ob, lhsT=wkvr[:, kc, h * 128:(h + 1) * 128], rhs=hT[s_][:, kc, :],
                                                            start=(kc == 0), stop=(kc == 15)) for kc in range(16)],
                     reads=[B_w, BhT[s_]], writes=[Bo])
                ks = gslot["ko"] % 4
                gslot["ko"] += 1
                if ti == 0:
                    k.op(ACT, lambda e, ob=ob, ks=ks: e.activation(out=ko[:, ks, :], in_=ob, func=AF.Identity), reads=[Bo], writes=[Bko[ks]])
                    k.dma(KT_d[h, :, key0:key0 + TT], ko[:, ks, :], Bko[ks], reads=[Bko[ks]])
                    continue
                q_ = gslot["k32"] % 2
                gslot["k32"] += 1
                k.op(ACT, lambda e, ob=ob, q_=q_: e.activation(out=k32[q_][:], in_=ob, func=AF.Identity), reads=[Bo], writes=[Bk32[q_]])

                def post(h=h, q_=q_, ks=ks):
                    pb = banks[2][:, 0:TT]
                    k.pe([lambda e, pb=pb, q_=q_: e.matmul(pb, lhsT=permT[:], rhs=k32[q_][:], start=True, stop=True)],
                         reads=[Bk32[q_], B_const], writes=[Bbank[2]])
                    k.op(DVE, lambda e, pb=pb, q_=q_: e.tensor_tensor(out=t1[q_][:], in0=pb, in1=cs[s_][:, 1, :], op=ALU.mult),
                         reads=[Bbank[2], Bcs[s_]], writes=[Bt1[q_]])
                    k.op(POOL, lambda e, q_=q_: e.tensor_tensor(out=t2[q_][:], in0=k32[q_][:], in1=cs[s_][:, 0, :], op=ALU.mult),
                         reads=[Bk32[q_], Bcs[s_]], writes=[Bt2[q_]])
                    k.op(POOL, lambda e, q_=q_, ks=ks: e.tensor_tensor(out=ko[:, ks, :], in0=t1[q_][:], in1=t2[q_][:], op=ALU.add),
                         reads=[Bt1[q_], Bt2[q_]], writes=[Bko[ks]])
                    k.dma(KT_d[h, :, key0:key0 + TT], ko[:, ks, :], Bko[ks], reads=[Bko[ks]])
                if pending:
                    pending.pop(0)()
                pending.append(post)
            if ti + 1 < ntiles:
                nt_tile(ti + 1)
            while pending:
                pending.pop(0)()
            for b in range(0 if 'R' not in SKIP else 8, 8):
                oslot = gslot["o"] % 2
                gslot["o"] += 1
                ob = banks[oslot][:, 0:TT]
                Bo = Bbank[oslot]
                k.pe([lambda e, kc=kc, b=b, ob=ob: e.matmul(ob, lhsT=wkvr[:, kc, 2048 + b * 128:2048 + (b + 1) * 128], rhs=hT[s_][:, kc, :],
                                                            start=(kc == 0), stop=(kc == 15)) for kc in range(16)],
                     reads=[B_w, BhT[s_]], writes=[Bo])
                rs = gslot["rx"] % 4
                gslot["rx"] += 1
                k.op(ACT, lambda e, ob=ob, rs=rs: e.activation(out=rxo[:, rs, :], in_=ob, func=AF.Identity), reads=[Bo], writes=[Brxo[rs]])
                dst = RXC_d[b, :, :] if ti == 0 else RXL_d[b, :, (ti - 1) * TT: ti * TT]
                k.dma(dst, rxo[:, rs, :], Brxo[rs], reads=[Brxo[rs]])
            vs_ = ti % 2
            for su in range(0 if 'V' not in SKIP else 2, 2):
                for half in range(2):
                    vb_i = 3 + gslot["v"] % 3
                    gslot["v"] += 1
                    vb = banks[vb_i]
                    k.pe([lambda e, kc=kc, su=su, half=half, vb=vb: e.matmul(
                        vb, lhsT=hT[s_][:, kc, su * 128:(su + 1) * 128], rhs=wkvr[:, kc, 1024 + half * 512:1024 + (half + 1) * 512],
                        start=(kc == 0), stop=(kc == 15)) for kc in range(16)],
                        reads=[B_w, BhT[s_]], writes=[Bbank[vb_i]])
                    k.op(DVE, lambda e, su=su, half=half, vb=vb, vs_=vs_: e.tensor_copy(
                        vt[vs_][:, half * 4:(half + 1) * 4, su, 0:128], vb.rearrange("p (h d) -> p h d", h=4)),
                        reads=[Bbank[vb_i]], writes=[Bvt[vs_]])
            kt0 = key0 // 128
            for h in range(0 if 'V' not in SKIP else 8, 8):
                k.dma(VV_d[h, :, kt0:kt0 + 2, :], vt[vs_][:, h, :, :], Bvt[vs_], reads=[Bvt[vs_]])
        bar = []
        for b_ in Bko + Brxo + Bvt + Bxs + Bcs + BhT + Bk32 + Bt1 + Bt2 + Bwst + [B_w] + Bbank + Btp + Bxn + Bssq + [B_junk]:
            bar.append(b_.w)
            bar += b_.rtoks()
            if b_.dsem is not None:
                bar.append((b_.dsem, b_.dsem.n))
    for E in (PE, ACT, DVE, POOL, SP):
        E.wait(bar)

    if debug and stop_after == 1 and 'D' in SKIP:
        return nc
    if debug and stop_after == 1:
        with nc.sbuf_tensor("sb_dbt", [128, 8, 129 * 8], BF16) as dbt, nc.sbuf_tensor("sb_dbf", [128, 8, 1024], F32) as dbf:
            Bd = Buf()
            k.dma(dbt[:, :, 0:1024], KT_d[:, :, 0:1024].rearrange("h p n -> p h n"), Bd, writes=[Bd])
            k.dma(dbg["KT"].rearrange("h p n -> p h n"), dbt[:, :, 0:1024], Bd, reads=[Bd])
            Bd2 = Buf()
            k.dma(dbf[:], RXL_d[:, :, 0:1024].rearrange("h p n -> p h n"), Bd2, writes=[Bd2])
            k.dma(dbg["RX"].rearrange("h p n -> p h n"), dbf[:], Bd2, reads=[Bd2])
            Bd3 = Buf()
            k.dma(dbt[:].rearrange("p h (t d) -> p h t d", d=129), VV_d[:, :, 0:8, :].rearrange("h p t d -> p h t d"), Bd3,
                  reads=[Bd], writes=[Bd3])
            k.dma(dbg["VV"].rearrange("h p t d -> p h t d"), dbt[:].rearrange("p h (t d) -> p h t d", d=129), Bd3, reads=[Bd3])
            fin = [(b_.dsem, b_.dsem.n) for b_ in (Bd, Bd2, Bd3)]
            SP.wait(fin)
        return nc


    EXTP = 17 * 128
    B_QT, B_GZ, B_AT = Buf("QT"), Buf("GZ"), Buf("AT")
    dl = sb("dl", [128, 256], F32)
    subg8 = sb("subg8", [128, 128], F32)
    lamv = sb("lamv", [128, 8], F32)
    mk = sb("mk", [128, 32], F32)
    B_misc = Buf("misc")
    k.dma(dl[:], dlam_d[:, :], B_misc, writes=[B_misc])
    k.dma(subg8[:], subg_d[:, :], B_misc, writes=[B_misc])
    k.dma(mk[:], mk_d[:, :], B_misc, writes=[B_misc])
    k.op(DVE, lambda e: e.tensor_tensor(out=dl[:, 0:64], in0=dl[:, 0:64], in1=dl[:, 64:128], op=ALU.mult), reads=[B_misc], writes=[B_misc])
    k.op(DVE, lambda e: e.tensor_tensor(out=dl[:, 128:192], in0=dl[:, 128:192], in1=dl[:, 192:256], op=ALU.mult), reads=[B_misc], writes=[B_misc])
    k.op(ACT, lambda e: e.activation(out=dl[:, 64:128], in_=dl[:, 0:64], func=AF.Identity, accum_out=lamv[:, 0:1]), reads=[B_misc], writes=[B_misc])
    k.op(ACT, lambda e: e.activation(out=dl[:, 192:256], in_=dl[:, 128:192], func=AF.Identity, accum_out=lamv[:, 1:2]), reads=[B_misc], writes=[B_misc])
    k.op(ACT, lambda e: e.activation(out=lamv[:, 2:4], in_=lamv[:, 0:2], func=AF.Exp), reads=[B_misc], writes=[B_misc])
    k.op(DVE, lambda e: e.tensor_tensor(out=lamv[:, 4:5], in0=lamv[:, 3:4], in1=lamv[:, 2:3], op=ALU.subtract), reads=[B_misc], writes=[B_misc])
    k.op(DVE, lambda e: e.tensor_scalar(out=lamv[:, 4:5], in0=lamv[:, 4:5], scalar1=-LAM_INIT, scalar2=None, op0=ALU.add), reads=[B_misc], writes=[B_misc])
    k.op(DVE, lambda e: e.tensor_scalar(out=subg8[:], in0=subg8[:], scalar1=1.0 - LAM_INIT, scalar2=None, op0=ALU.mult), reads=[B_misc], writes=[B_misc])

    X1_d = nc.dram_tensor("X1s", [17 * 128, D], F32).ap()
    XR_d = nc.dram_tensor("XRs", [128, NKEY], F32).ap()
    WB_d = nc.dram_tensor("WBs", [3, NJ, 128, 2048], BF16).ap()
    es_mix = ExitStack()
    QT = es_mix.enter_context(nc.sbuf_tensor("sb_QT", [128, 8, EXTP], BF16))
    GZ = es_mix.enter_context(nc.sbuf_tensor("sb_GZ", [128, 8, EXTP], BF16))
    AT = es_mix.enter_context(nc.sbuf_tensor("sb_AT", [128, 8, EXTP], BF16))
    with ExitStack() as es:
        def sc(name, shape, dt):
            return es.enter_context(nc.sbuf_tensor("sb_" + name, shape, dt))
        wqz = sc("wqz", [128, 16, 1024], BF16)
        wst0_ = sc("b_wst0", [128, 1024], F32)
        wst = [wst0_, wst0_]
        xs0_ = sc("b_xs0", [128, 2, D], F32)
        xs = [xs0_, xs0_]
        hT = [sc("b_hT0", [128, 16, TT], BF16), sc("b_hT1", [128, 16, TT], BF16)]
        cs = [sc("b_cs0", [128, 2, TT], F32), sc("b_cs1", [128, 2, TT], F32)]
        k32 = [sc("b_k32a", [128, TT], F32), sc("b_k32b", [128, TT], F32)]
        t1 = [sc("b_t1a", [128, TT], F32), sc("b_t1b", [128, TT], F32)]
        t2 = [sc("b_t2a", [128, TT], F32), sc("b_t2b", [128, TT], F32)]
        Bwst, Bxs, BhT, Bcs, Bk32, Bt1, Bt2 = ([Buf(), Buf()] for _ in range(7))
        Bxs[1] = Bxs[0]
        Bwst[1] = Bwst[0]
        B_w = Buf("wqz")
        NOT = 9

        def load_own(ti):
            s_ = ti % 2
            nsub = 2 if ti < 8 else 1
            ncol = nsub * 128
            k.dma(xs[s_][:, 0:nsub, :], xo_d[ti * TT: ti * TT + ncol, :].rearrange("(s p) d -> p s d", p=128), Bxs[s_], writes=[Bxs[s_]])
            k.dma(cs[s_][:, 0, 0:ncol], coso_d[:, ti * TT: ti * TT + ncol], Bcs[s_], writes=[Bcs[s_]])
            k.dma(cs[s_][:, 1, 0:ncol], sino_d[:, ti * TT: ti * TT + ncol], Bcs[s_], writes=[])
            Bcs[s_].w = (Bcs[s_].dsem, Bcs[s_].dsem.n)

        go = 0
        for pas in range(2):
            c0w = 0 if pas == 0 else 4096
            for kc in range(16):
                k.dma(wst[0][:], win_d[kc * 128:(kc + 1) * 128, c0w:c0w + 1024], Bwst[0], writes=[Bwst[0]])
                k.op(ACT, lambda e, kc=kc: e.activation(out=wqz[:, kc, :], in_=wst[0][:], func=AF.Identity), reads=[Bwst[0]], writes=[B_w])
            load_own(0)
            for ti in range(NOT):
                s_ = ti % 2
                nsub = 2 if ti < 8 else 1
                ncol = nsub * 128
                col0 = ti * TT
                for su in range(nsub):
                    norm_transpose(xs[s_][:, su, :], Bxs[s_], 128, a1, modx,
                                   lambda kc, su=su, s_=s_: hT[s_][:, kc, su * 128:(su + 1) * 128], BhT[s_])
                if ti + 1 < NOT:
                    load_own(ti + 1)
                for h in range(8 if pas == 0 else 0):
                    oslot = go % 2
                    go += 1
                    ob = banks[oslot][:, 0:ncol]
                    Bo = Bbank[oslot]
                    k.pe([lambda e, kc=kc, h=h, ob=ob: e.matmul(ob, lhsT=wqz[:, kc, h * 128:(h + 1) * 128], rhs=hT[s_][:, kc, 0:ncol],
                                                                start=(kc == 0), stop=(kc == 15)) for kc in range(16)],
                         reads=[B_w, BhT[s_]], writes=[Bo])
                    q_ = h % 2
                    k.op(ACT, lambda e, ob=ob, q_=q_: e.activation(out=k32[q_][:, 0:ncol], in_=ob, func=AF.Identity), reads=[Bo], writes=[Bk32[q_]])
                    pb = banks[2][:, 0:ncol]
                    k.pe([lambda e, pb=pb, q_=q_: e.matmul(pb, lhsT=permT[:], rhs=k32[q_][:, 0:ncol], start=True, stop=True)],
                         reads=[Bk32[q_], B_const], writes=[Bbank[2]])
                    k.op(DVE, lambda e, pb=pb, q_=q_: e.tensor_tensor(out=t1[q_][:, 0:ncol], in0=pb, in1=cs[s_][:, 1, 0:ncol], op=ALU.mult),
                         reads=[Bbank[2], Bcs[s_]], writes=[Bt1[q_]])
                    k.op(POOL, lambda e, q_=q_: e.tensor_tensor(out=t2[q_][:, 0:ncol], in0=k32[q_][:, 0:ncol], in1=cs[s_][:, 0, 0:ncol], op=ALU.mult),
                         reads=[Bk32[q_], Bcs[s_]], writes=[Bt2[q_]])
                    k.op(POOL, lambda e, q_=q_, h=h: e.tensor_tensor(out=QT[:, h, col0:col0 + ncol], in0=t1[q_][:, 0:ncol], in1=t2[q_][:, 0:ncol], op=ALU.add),
                         reads=[Bt1[q_], Bt2[q_]], writes=[B_QT])
                for b in range(8 if pas == 1 else 0):
                    oslot = go % 2
                    go += 1
                    ob = banks[oslot][:, 0:ncol]
                    Bo = Bbank[oslot]
                    k.pe([lambda e, kc=kc, b=b, ob=ob: e.matmul(ob, lhsT=wqz[:, kc, b * 128:(b + 1) * 128], rhs=hT[s_][:, kc, 0:ncol],
                                                                start=(kc == 0), stop=(kc == 15)) for kc in range(16)],
                         reads=[B_w, BhT[s_]], writes=[Bo])
                    k.op(ACT, lambda e, ob=ob, b=b: e.activation(out=GZ[:, b, col0:col0 + ncol], in_=ob, func=AF.Gelu_apprx_tanh),
                         reads=[Bo], writes=[B_GZ])
        bar = []
        for b_ in Bwst + Bxs + BhT + Bcs + Bk32 + Bt1 + Bt2 + [B_w, B_QT, B_GZ] + Bbank + Bxn + Bssq + [B_junk]:
            bar.append(b_.w)
            bar += b_.rtoks()
    for E in (PE, ACT, DVE, POOL, SP):
        E.wait(bar)

    with ExitStack() as es:
        def sc(name, shape, dt):
            return es.enter_context(nc.sbuf_tensor("sb_" + name, shape, dt))
        Kh = sc("Kh", [128, NKEY], BF16)
        Vh = sc("Vh", [128, NKT, 129], BF16)
        pt = [sc("pt%d" % i, [128, 2, 512], BF16) for i in range(3)]
        Bpt = [Buf() for _ in range(3)]
        o32 = [sc("o32_%d" % i, [128, 128], F32) for i in range(2)]
        Bo32 = [Buf(), Buf()]
        onb = [sc("onb%d" % i, [128, 128], BF16) for i in range(2)]
        Bonb = [Buf(), Buf()]
        r4 = [sc("r4_%d" % i, [128, 8], F32) for i in range(2)]
        Br4 = [Buf(), Buf()]
        B_K, B_V = Buf("Kh"), Buf("Vh")
        pc_f = sc("pc_f", [128, 1024], F32)
        pc_b = sc("pc_b", [128, 1024], BF16)
        B_pcf, B_pcb = Buf(), Buf()
        pc = {"i": 0}
        wsrc = [wup_d, wgate_d, wdown_d]

        def precast_step():
            i_ = pc["i"]
            if i_ >= 3 * NJ * 2:
                return
            pc["i"] += 1
            wi_, j_, hf_ = i_ // (NJ * 2), (i_ // 2) % NJ, i_ % 2
            k.dma(pc_f[:], wsrc[wi_][j_, :, hf_ * 1024:(hf_ + 1) * 1024], B_pcf, writes=[B_pcf])
            k.op(POOL, lambda e: e.tensor_copy(pc_b[:], pc_f[:]), reads=[B_pcf], writes=[B_pcb])
            k.dma(WB_d[wi_, j_, :, hf_ * 1024:(hf_ + 1) * 1024], pc_b[:], B_pcb, reads=[B_pcb])
        nheads = 8 if stop_after > 3 else 1
        qblocks = [(q0, 512) for q0 in range(0, 2048, 512)] + [(2048, 2)]
        if stop_after == 3 and debug:
            qblocks = [(0, 512), (2048, 2)]
        k.op(DVE, lambda e: e.memset(AT[:, :, EXT:EXTP], 0.0), writes=[B_AT])
        it = 0
        fz = 0
        for h in range(nheads):
            k.dma(Kh[:], KT_d[h, :, :], B_K, writes=[B_K])
            k.dma(Vh[:], VV_d[h, :, :, :], B_V, writes=[B_V])
            for (q0, nq) in qblocks:
                nqs = (nq + 127) // 128
                rows = min(128, nq)
                for _ in range(7):
                    precast_step()
                def emit_s(kt, it_):
                    sbuf_i = it_ % 2
                    r = it_ % 3
                    b0 = 2 * sbuf_i
                    k.pe([lambda e, c=c, kt=kt, b0=b0: e.matmul(banks[b0 + c][:, 0:nq], lhsT=Kh[c * 64:(c + 1) * 64, kt * 128:(kt + 1) * 128],
                                                               rhs=QT[c * 64:(c + 1) * 64, h, q0:q0 + nq], start=True, stop=True) for c in range(2)],
                         reads=[B_K, B_QT], writes=[Bbank[b0], Bbank[b0 + 1]])
                    sview = psall[:, b0 * 512:(b0 + 2) * 512].rearrange("p (c n) -> p c n", c=2)[:, :, 0:nq]
                    k.op(ACT, lambda e, sview=sview, r=r: e.activation(out=pt[r][:, :, 0:nq], in_=sview, func=AF.Exp, scale=0.125),
                         reads=[Bbank[b0], Bbank[b0 + 1]], writes=[Bpt[r]])

                def emit_pv(kt, it_):
                    r = it_ % 3
                    fns = []
                    accs = []
                    for c in range(2):
                        for qs in range(nqs):
                            ab = 4 + 2 * c + qs // 2
                            co = (qs % 2) * 256
                            if Bbank[ab] not in accs:
                                accs.append(Bbank[ab])
                            fns.append(lambda e, c=c, qs=qs, ab=ab, co=co, kt=kt, r=r: e.matmul(
                                banks[ab][0:rows, co:co + 129], lhsT=pt[r][:, c, qs * 128:qs * 128 + rows], rhs=Vh[:, kt, :],
                                start=(kt == 0 and qs % 2 == 0), stop=(kt == NKT - 1)))
                    k.pe(fns, reads=[Bpt[r], B_V], writes=accs)

                it0 = it
                for kt in range(NKT):
                    emit_s(kt, it0 + kt)
                    if kt >= 1:
                        emit_pv(kt - 1, it0 + kt - 1)
                emit_pv(NKT - 1, it0 + NKT - 1)
                it = it0 + NKT
                R_ = rows
                for qs in range(nqs):
                    f = fz % 2
                    fz += 1
                    a0 = banks[4 + qs // 2][0:R_, (qs % 2) * 256:(qs % 2) * 256 + 129]
                    a1_ = banks[6 + qs // 2][0:R_, (qs % 2) * 256:(qs % 2) * 256 + 129]
                    k.op(DVE, lambda e, f=f, a0=a0: e.reciprocal(out=r4[f][0:R_, 0:1], in_=a0[:, 128:129]), reads=[Bbank[4 + qs // 2]], writes=[Br4[f]])
                    k.op(DVE, lambda e, f=f, a1_=a1_: e.reciprocal(out=r4[f][0:R_, 1:2], in_=a1_[:, 128:129]), reads=[Bbank[6 + qs // 2]], writes=[Br4[f]])
                    k.op(DVE, lambda e, f=f: e.tensor_tensor(out=r4[f][0:R_, 2:3], in0=r4[f][0:R_, 1:2], in1=lamv[0:R_, 4:5], op=ALU.mult),
                         reads=[Br4[f], B_misc], writes=[Br4[f]])
                    k.op(DVE, lambda e, f=f, a0=a0: e.tensor_scalar(out=o32[f][0:R_, :], in0=a0[:, 0:128], scalar1=r4[f][0:R_, 0:1], scalar2=None, op0=ALU.mult),
                         reads=[Bbank[4 + qs // 2], Br4[f]], writes=[Bo32[f]])
                    k.op(DVE, lambda e, f=f, a1_=a1_: e.scalar_tensor_tensor(out=o32[f][0:R_, :], in0=a1_[:, 0:128], scalar=r4[f][0:R_, 2:3], in1=o32[f][0:R_, :],
                                                                           op0=ALU.mult, op1=ALU.add),
                         reads=[Bbank[6 + qs // 2], Br4[f], Bo32[f]], writes=[Bo32[f]])
                    k.op(ACT, lambda e, f=f: e.activation(out=junk[0:R_, 0:128], in_=o32[f][0:R_, :], func=AF.Square, accum_out=r4[f][0:R_, 3:4]),
                         reads=[Bo32[f]], writes=[B_junk, Br4[f]])
                    k.op(ACT, lambda e, f=f: e.activation(out=r4[f][0:R_, 4:5], in_=r4[f][0:R_, 3:4], func=AF.Ln, scale=1.0 / 128, bias=epsc[0:R_, 1:2]),
                         reads=[Br4[f], B_const], writes=[Br4[f]])
                    k.op(ACT, lambda e, f=f: e.activation(out=r4[f][0:R_, 5:6], in_=r4[f][0:R_, 4:5], func=AF.Exp, scale=-0.5),
                         reads=[Br4[f]], writes=[Br4[f]])
                    k.op(DVE, lambda e, f=f: e.scalar_tensor_tensor(out=onb[f][0:R_, :], in0=o32[f][0:R_, :], scalar=r4[f][0:R_, 5:6], in1=subg8[0:R_, :],
                                                                   op0=ALU.mult, op1=ALU.mult),
                         reads=[Bo32[f], Br4[f], B_misc], writes=[Bonb[f]])
                    tb = banks[0].bitcast(BF16)
                    k.pe([lambda e, f=f, tb=tb: e.transpose(tb[:, 0:R_], onb[f][0:R_, :], identb[0:R_, 0:R_])], reads=[Bonb[f], B_const], writes=[Bbank[0]])
                    k.op(ACT, lambda e, tb=tb, qs=qs: e.activation(out=AT[:, h, q0 + qs * 128: q0 + qs * 128 + R_], in_=tb[:, 0:R_], func=AF.Identity),
                         reads=[Bbank[0]], writes=[B_AT])
        while pc["i"] < 3 * NJ * 2:
            precast_step()
        bar = []
        for b_ in [B_K, B_V, B_QT, B_AT, B_pcf, B_pcb] + Bpt + Bo32 + Bonb + Br4 + Bbank + [B_junk]:
            bar.append(b_.w)
            bar += b_.rtoks()
            if b_.dsem is not None:
                bar.append((b_.dsem, b_.dsem.n))
    for E in (PE, ACT, DVE, POOL, SP):
        E.wait(bar)
    if debug and stop_after == 3:
        Bd = Buf()
        k.dma(dbg["AT"][:, :], AT[:, 0, :], Bd, reads=[B_AT])
        k.dma(dbg["QT"][:, :], QT[:, 0, :], Bd, reads=[B_QT])
        k.dma(dbg["GZ"][:, :], GZ[:, 0, :], Bd, reads=[B_GZ])
        SP.wait([(Bd.dsem, Bd.dsem.n)])
        return nc


    RT, B_RT = QT, B_QT
    k.op(POOL, lambda e: e.memset(RT[:, :, EXT:EXTP], 0.0), writes=[B_RT])
    TN = 1024
    with ExitStack() as es:
        def sc(name, shape, dt):
            return es.enter_context(nc.sbuf_tensor("sb_" + name, shape, dt))
        rgwb = sc("rgwb", [128, 32 * 128], BF16)
        rgst = sc("rgst", [128, 1024], F32)
        cst = sc("cst", [128, 72], F32)
        xt2 = [sc("r_xt%d" % i, [128, TN + 4], F32) for i in range(2)]
        xr2 = [sc("r_xr%d" % i, [128, TN], F32) for i in range(2)]
        xrb2 = [sc("r_xrb%d" % i, [128, TN], BF16) for i in range(2)]
        EA2 = [sc("r_EA%d" % i, [128, TN], F32) for i in range(2)]
        EI2 = [sc("r_EI%d" % i, [128, TN], F32) for i in range(2)]
        A_2 = [sc("r_A%d" % i, [128, TN], F32) for i in range(2)]
        A22 = [sc("r_A2%d" % i, [128, TN], F32) for i in range(2)]
        H2 = [sc("r_H%d" % i, [128, TN], F32) for i in range(2)]
        Hacc = sc("r_Hacc", [128, 2052], F32)
        hcar = sc("r_hcar", [128, 2], F32)
        B_rgw, B_rgst, B_cst, B_Hacc, B_hcar = (Buf() for _ in range(5))
        Bxt2, Bxr2, Bxrb2, BEA2, BEI2, BA2, BA22, BH2 = ([Buf(), Buf()] for _ in range(8))
        for q in range(4):
            k.dma(rgst[:], rgw_d[:, q * 1024:(q + 1) * 1024], B_rgst, writes=[B_rgst])
            k.op(ACT, lambda e, q=q: e.activation(out=rgwb[:, q * 1024:(q + 1) * 1024], in_=rgst[:], func=AF.Identity), reads=[B_rgst], writes=[B_rgw])
        k.op(DVE, lambda e: e.memset(cst[:, 64:65], 1.0), writes=[B_cst])
        k.op(ACT, lambda e: e.activation(out=cst[:, 0:16], in_=vecs[:, V_RLAM:V_RLAM + 16], func=AF.Exp, scale=-1.0), reads=[B_vecs], writes=[B_cst])
        k.op(ACT, lambda e: e.activation(out=cst[:, 0:16], in_=cst[:, 0:16], func=AF.Ln, scale=1.0, bias=cst[:, 64:65]), reads=[B_cst], writes=[B_cst])
        k.op(DVE, lambda e: e.tensor_scalar(out=cst[:, 16:32], in0=cst[:, 0:16], scalar1=-16.0, scalar2=None, op0=ALU.mult), reads=[B_cst], writes=[B_cst])
        k.op(DVE, lambda e: e.tensor_scalar(out=cst[:, 0:16], in0=cst[:, 0:16], scalar1=-8.0, scalar2=None, op0=ALU.mult), reads=[B_cst], writes=[B_cst])
        k.op(DVE, lambda e: e.tensor_scalar(out=cst[:, 32:48], in0=vecs[:, V_RBA:V_RBA + 16], scalar1=-1.0, scalar2=None, op0=ALU.mult), reads=[B_vecs], writes=[B_cst])
        k.op(DVE, lambda e: e.tensor_scalar(out=cst[:, 48:64], in0=vecs[:, V_RBI:V_RBI + 16], scalar1=-1.0, scalar2=None, op0=ALU.mult), reads=[B_vecs], writes=[B_cst])
        nblocks = 8 if stop_after > 2 else 1
        gs = 0
        tc_ = 0
        B_XRd = [Buf() for _ in range(1 + S // TN)]
        for b in range(nblocks):
            k.op(DVE, lambda e: e.memset(Hacc[:], 0.0), writes=[B_Hacc])
            for d in range(2):
                idx = d * 8 + b
                wa = rgwb[:, ((0 * 2 + d) * 8 + b) * 128:((0 * 2 + d) * 8 + b + 1) * 128]
                wi = rgwb[:, ((1 * 2 + d) * 8 + b) * 128:((1 * 2 + d) * 8 + b + 1) * 128]
                k.op(DVE, lambda e: e.memset(hcar[:, 0:1], 0.0), writes=[B_hcar])
                nlt = S // TN
                tiles = [("c", 0)] + [("l", t) for t in (range(nlt) if d == 0 else range(nlt - 1, -1, -1))]
                def tile_geom(kind, t):
                    n = 256 if kind == "c" else TN
                    seqlen = NCTX if kind == "c" else S
                    src = RXC_d if kind == "c" else RXL_d
                    t0 = t * TN
                    lo, hi = max(0, t0 - 2), min(seqlen, t0 + n + 1)
                    xslot = 0 if kind == "c" else 1 + t
                    xoff = 0 if kind == "c" else NCTX + t0
                    return n, src, t0, lo, hi, xslot, xoff

                def emit_load(kind, t, u):
                    n, src, t0, lo, hi, xslot, xoff = tile_geom(kind, t)
                    if d == 0:
                        xt = xt2[u]
                        k.op(DVE, lambda e, xt=xt: e.memset(xt[:, 0:2], 0.0), writes=[Bxt2[u]])
                        k.op(DVE, lambda e, xt=xt, n=n: e.memset(xt[:, n + 2:n + 3], 0.0), writes=[Bxt2[u]])
                        k.dma(xt[:, lo - t0 + 2: hi - t0 + 2], src[b, :, lo:hi], Bxt2[u], writes=[Bxt2[u]])
                    else:
                        k.dma(xr2[u][:, 0:n], XR_d[:, xoff:xoff + n], Bxr2[u], reads=[B_XRd[xslot]], writes=[Bxr2[u]])

                def stage_a(kind, t, u):
                    emit_load(kind, t, u)
                    n, src, t0, lo, hi, xslot, xoff = tile_geom(kind, t)
                    if d == 0:
                        xt, xr = xt2[u], xr2[u]
                        w_ = lambda j: vecs[:, V_RCW + j * 8 + b: V_RCW + j * 8 + b + 1]
                        k.op(ACT, lambda e, n=n, xt=xt, xr=xr: e.activation(out=xr[:, 0:n], in_=xt[:, 0:n], func=AF.Identity, scale=w_(0),
                                                                           bias=vecs[:, V_RCB + b:V_RCB + b + 1]),
                             reads=[Bxt2[u], B_vecs], writes=[Bxr2[u]])
                        for j in range(1, 4):
                            k.op(DVE, lambda e, n=n, j=j, xt=xt, xr=xr: e.scalar_tensor_tensor(out=xr[:, 0:n], in0=xt[:, j:j + n], scalar=w_(j), in1=xr[:, 0:n],
                                                                                              op0=ALU.mult, op1=ALU.add),
                                 reads=[Bxt2[u], B_vecs, Bxr2[u]], writes=[Bxr2[u]])

                stage_a(tiles[0][0], tiles[0][1], tc_ % 2)
                for ti_, (kind, t) in enumerate(tiles):
                    u = tc_ % 2
                    tc_ += 1
                    xt, xr, xrb, EA, EI, A_, A2, H = xt2[u], xr2[u], xrb2[u], EA2[u], EI2[u], A_2[u], A22[u], H2[u]
                    B_xt, B_xr, B_xrb, B_EA, B_EI, B_A, B_A2, B_H = Bxt2[u], Bxr2[u], Bxrb2[u], BEA2[u], BEI2[u], BA2[u], BA22[u], BH2[u]
                    n, src, t0, lo, hi, xslot, xoff = tile_geom(kind, t)
                    if ti_ + 1 < len(tiles):
                        stage_a(tiles[ti_ + 1][0], tiles[ti_ + 1][1], tc_ % 2)
                    if d == 0:
                        k.dma(XR_d[:, xoff:xoff + n], xr[:, 0:n], B_xr, reads=[B_xr], writes=[B_XRd[xslot]])
                    k.op(ACT, lambda e, n=n, xr=xr, xrb=xrb: e.activation(out=xrb[:, 0:n], in_=xr[:, 0:n], func=AF.Identity), reads=[B_xr], writes=[B_xrb])
                    base = 4 * (gs % 2)
                    gs += 1
                    for (w_g, off, dstE, Bd_, nb_) in ((wa, 0, EA, B_EA, cst[:, 32 + idx:33 + idx]), (wi, 2, EI, B_EI, cst[:, 48 + idx:49 + idx])):
                        fns = []
                        for m0 in range(0, n, 512):
                            mw = min(512, n - m0)
                            fns.append(lambda e, w_g=w_g, m0=m0, mw=mw, off=off, base=base, xrb=xrb: e.matmul(
                                psall[:, (base + off) * 512 + m0:(base + off) * 512 + m0 + mw], lhsT=w_g, rhs=xrb[:, m0:m0 + mw],
                                start=True, stop=True))
                        k.pe(fns, reads=[B_rgw, B_xrb], writes=[Bbank[base + off], Bbank[base + off + 1]])
                        k.op(ACT, lambda e, off=off, base=base, n=n, dstE=dstE, nb_=nb_: e.activation(
                            out=dstE[:, 0:n], in_=psall[:, (base + off) * 512:(base + off) * 512 + n], func=AF.Exp, scale=-1.0, bias=nb_),
                            reads=[Bbank[base + off], Bbank[base + off + 1], B_cst], writes=[Bd_])
                    k.op(ACT, lambda e, n=n, EA=EA: e.activation(out=EA[:, 0:n], in_=EA[:, 0:n], func=AF.Ln, scale=1.0, bias=cst[:, 64:65]), reads=[B_EA, B_cst], writes=[B_EA])
                    k.op(ACT, lambda e, n=n, EA=EA: e.activation(out=EA[:, 0:n], in_=EA[:, 0:n], func=AF.Exp, scale=-1.0), reads=[B_EA], writes=[B_EA])
                    k.op(ACT, lambda e, n=n, EI=EI: e.activation(out=EI[:, 0:n], in_=EI[:, 0:n], func=AF.Ln, scale=1.0, bias=cst[:, 64:65]), reads=[B_EI, B_cst], writes=[B_EI])
                    k.op(ACT, lambda e, n=n, EI=EI: e.activation(out=EI[:, 0:n], in_=EI[:, 0:n], func=AF.Exp, scale=-1.0), reads=[B_EI], writes=[B_EI])
                    k.op(ACT, lambda e, n=n, A_=A_, EA=EA: e.activation(out=A_[:, 0:n], in_=EA[:, 0:n], func=AF.Exp, scale=cst[:, idx:idx + 1]), reads=[B_EA, B_cst], writes=[B_A])
                    k.op(ACT, lambda e, n=n, A2=A2, EA=EA: e.activation(out=A2[:, 0:n], in_=EA[:, 0:n], func=AF.Exp, scale=cst[:, 16 + idx:17 + idx]), reads=[B_EA, B_cst], writes=[B_A2])
                    k.op(ACT, lambda e, n=n, A2=A2: e.activation(out=A2[:, 0:n], in_=A2[:, 0:n], func=AF.Ln, scale=-1.0, bias=cst[:, 64:65]), reads=[B_A2, B_cst], writes=[B_A2])
                    k.op(ACT, lambda e, n=n, A2=A2: e.activation(out=A2[:, 0:n], in_=A2[:, 0:n], func=AF.Exp, scale=0.5), reads=[B_A2], writes=[B_A2])
                    k.op(DVE, lambda e, n=n, EI=EI, xr=xr: e.tensor_tensor(out=EI[:, 0:n], in0=EI[:, 0:n], in1=xr[:, 0:n], op=ALU.mult), reads=[B_EI, B_xr], writes=[B_EI])
                    k.op(DVE, lambda e, n=n, EI=EI, A2=A2: e.tensor_tensor(out=EI[:, 0:n], in0=EI[:, 0:n], in1=A2[:, 0:n], op=ALU.mult), reads=[B_EI, B_A2], writes=[B_EI])
                    if d == 0:
                        k.op(DVE, lambda e, n=n, H=H, A_=A_, EI=EI: e.tensor_tensor_scan(out=H[:, 0:n], data0=A_[:, 0:n], data1=EI[:, 0:n], initial=hcar[:, 0:1],
                                                                                      op0=ALU.mult, op1=ALU.add), reads=[B_A, B_EI, B_hcar], writes=[B_H])
                        k.op(DVE, lambda e, n=n, H=H: e.tensor_copy(hcar[:, 0:1], H[:, n - 1:n]), reads=[B_H], writes=[B_hcar])
                    else:
                        k.op(DVE, lambda e, n=n, H=H, A_=A_, EI=EI: e.tensor_tensor_scan(out=H[:, 0:n][:, ::-1], data0=A_[:, 0:n][:, ::-1], data1=EI[:, 0:n][:, ::-1],
                                                                                      initial=hcar[:, 0:1], op0=ALU.mult, op1=ALU.add), reads=[B_A, B_EI, B_hcar], writes=[B_H])
                        k.op(DVE, lambda e, H=H: e.tensor_copy(hcar[:, 0:1], H[:, 0:1]), reads=[B_H], writes=[B_hcar])
                    if kind == "l":
                        per = 2048 // TN
                        tb_, hf_ = t // per, t % per
                        k.op(DVE, lambda e, tb_=tb_, hf_=hf_, H=H: e.scalar_tensor_tensor(out=Hacc[:, hf_ * TN:(hf_ + 1) * TN], in0=H[:, 0:TN], scalar=mk[:, tb_:tb_ + 1],
                                                                                         in1=Hacc[:, hf_ * TN:(hf_ + 1) * TN], op0=ALU.mult, op1=ALU.add),
                             reads=[B_H, B_misc, B_Hacc], writes=[B_Hacc])
                        if hf_ == per - 1:
                            k.op(DVE, lambda e, tb_=tb_, H=H: e.scalar_tensor_tensor(out=Hacc[:, 2048:2049], in0=H[:, TN - 1:TN], scalar=mk[:, 8 + tb_:9 + tb_], in1=Hacc[:, 2048:2049],
                                                                                   op0=ALU.mult, op1=ALU.add), reads=[B_H, B_misc, B_Hacc], writes=[B_Hacc])
                        if hf_ == 0:
                            k.op(DVE, lambda e, tb_=tb_, H=H: e.scalar_tensor_tensor(out=Hacc[:, 2049:2050], in0=H[:, 0:1], scalar=mk[:, 16 + tb_:17 + tb_], in1=Hacc[:, 2049:2050],
                                                                                   op0=ALU.mult, op1=ALU.add), reads=[B_H, B_misc, B_Hacc], writes=[B_Hacc])
            k.op(DVE, lambda e, b=b: e.tensor_tensor(out=RT[:, b, 0:EXT], in0=Hacc[:, 0:EXT], in1=GZ[:, b, 0:EXT], op=ALU.mult),
                 reads=[B_Hacc, B_GZ], writes=[B_RT])
        bar = []
        for b_ in [B_rgw, B_rgst, B_cst, B_Hacc, B_hcar, B_RT, B_GZ] + Bxt2 + Bxr2 + Bxrb2 + BEA2 + BEI2 + BA2 + BA22 + BH2 + Bbank:
            bar.append(b_.w)
            bar += b_.rtoks()
            if b_.dsem is not None:
                bar.append((b_.dsem, b_.dsem.n))
    for E in (PE, ACT, DVE, POOL, SP):
        E.wait(bar)
    if debug and stop_after == 2:
        Bd = Buf()
        k.dma(dbg["QT"][:, :], RT[:, 0, :], Bd, reads=[B_RT])
        SP.wait([(Bd.dsem, Bd.dsem.n)])
        return nc


    gz32 = GZ[:].rearrange("p a b -> p (a b)").bitcast(F32)
    x1t = gz32[:, 0:2048]
    xst = gz32[:, 2048:4096]
    g1b = gz32[:, 4096:6144]
    dgt = gz32[:, 6144:6272]
    B_x1t, B_xst, B_g1b, B_dg, B_X1 = Buf(), Buf(), Buf(), Buf(), Buf()

    def row_broadcast(dst, Bdst, col0):
        for q in range(4):
            for kc4 in range(4):
                kc = q * 4 + kc4
                k.op(DVE, lambda e, kc=kc: e.tensor_scalar(out=dgt, in0=identf[:], scalar1=modx[:, col0 + kc:col0 + kc + 1], scalar2=None, op0=ALU.mult),
                     reads=[B_const, B_mod], writes=[B_dg])
                k.pe([lambda e, kc4=kc4: e.matmul(banks[0][:, kc4 * 128:(kc4 + 1) * 128], lhsT=ones_f[:], rhs=dgt, start=True, stop=True)],
                     reads=[B_dg, B_const], writes=[Bbank[0]])
            k.op(ACT, lambda e, q=q: e.activation(out=dst[:, q * 512:(q + 1) * 512], in_=banks[0], func=AF.Identity), reads=[Bbank[0]], writes=[Bdst])

    row_broadcast(g1b, B_g1b, 32)
    with ExitStack() as es:
        def sc(name, shape, dt):
            return es.enter_context(nc.sbuf_tensor("sb_" + name, shape, dt))
        wo = sc("wo", [128, 16, D], BF16)
        B_wo = Buf()
        for ch in range(16):
            for hf in range(2):
                k.dma(xst[:, 0:1024], wout_d[ch * 128:(ch + 1) * 128, hf * 1024:(hf + 1) * 1024], B_xst, writes=[B_xst])
                k.op(ACT, lambda e, ch=ch, hf=hf: e.activation(out=wo[:, ch, hf * 1024:(hf + 1) * 1024], in_=xst[:, 0:1024], func=AF.Identity), reads=[B_xst], writes=[B_wo])
        for ts in range(17):
            k.dma(xst, xo_d[ts * 128:(ts + 1) * 128, :], B_xst, writes=[B_xst])
            for nb in range(4):
                k.pe([lambda e, ch=ch, nb=nb, ts=ts: e.matmul(banks[nb], lhsT=(AT if ch < 8 else RT)[:, ch % 8, ts * 128:(ts + 1) * 128],
                                                              rhs=wo[:, ch, nb * 512:(nb + 1) * 512], start=(ch == 0), stop=(ch == 15)) for ch in range(16)],
                     reads=[B_AT, B_RT, B_wo], writes=[Bbank[nb]])
            k.op(DVE, lambda e: e.tensor_tensor(out=x1t, in0=psall[:, 0:2048], in1=g1b, op=ALU.mult),
                 reads=[Bbank[0], Bbank[1], Bbank[2], Bbank[3], B_g1b], writes=[B_x1t])
            k.op(DVE, lambda e: e.tensor_tensor(out=x1t, in0=x1t, in1=xst, op=ALU.add), reads=[B_x1t, B_xst], writes=[B_x1t])
            k.dma(X1_d[ts * 128:(ts + 1) * 128, :], x1t, B_x1t, reads=[B_x1t], writes=[B_X1])
        bar = []
        for b_ in [B_wo, B_x1t, B_xst, B_g1b, B_dg, B_AT, B_RT, B_X1] + Bbank:
            bar.append(b_.w)
            bar += b_.rtoks()
            if b_.dsem is not None:
                bar.append((b_.dsem, b_.dsem.n))
    for E in (PE, ACT, DVE, POOL, SP):
        E.wait(bar)
    es_mix.close()
    if debug and stop_after == 4:
        return nc

    with ExitStack() as es:
        def sc(name, shape, dt):
            return es.enter_context(nc.sbuf_tensor("sb_" + name, shape, dt))
        h2T = sc("h2T", [128, 16, EXT], BF16)
        actT = sc("actT", [128, NJ, 512], BF16)
        hTh = actT[:, 0:4, :].rearrange("p a (b c) -> p (a b) c", c=128)
        x1t = sc("f_x1t", [128, D], F32)
        x2t = sc("f_x2t", [128, D], F32)
        g2b = sc("f_g2b", [128, D], F32)
        fgb = sc("f_fgb", [128, D], F32)
        wug = [sc("f_wug%d" % i, [128, 2, D], BF16) for i in range(3)]
        wdb = [sc("f_wd%d" % i, [128, D], BF16) for i in range(3)]
        cvt = sc("f_cvt", [128, 512], F32)
        glt = sc("f_glt", [128, 512], F32)
        fr = sc("f_fr", [128, 4], F32)
        B_h2T, B_hTh, B_act, B_x1t, B_x2t, B_g2b, B_fgb, B_cvt, B_glt, B_fr = (Buf() for _ in range(10))
        B_hTh = B_act
        Bwst = []
        Bwug = [Buf(), Buf(), Buf()]
        Bwd = [Buf(), Buf(), Buf()]
        dgt = x2t[:, 0:128]
        B_dg = B_x2t

        def row_broadcast2(dst, Bdst, col0):
            for q in range(4):
                for kc4 in range(4):
                    kc = q * 4 + kc4
                    k.op(DVE, lambda e, kc=kc: e.tensor_scalar(out=dgt, in0=identf[:], scalar1=modx[:, col0 + kc:col0 + kc + 1], scalar2=None, op0=ALU.mult),
                         reads=[B_const, B_mod], writes=[B_dg])
                    k.pe([lambda e, kc4=kc4: e.matmul(banks[0][:, kc4 * 128:(kc4 + 1) * 128], lhsT=ones_f[:], rhs=dgt, start=True, stop=True)],
                         reads=[B_dg, B_const], writes=[Bbank[0]])
                k.op(ACT, lambda e, q=q: e.activation(out=dst[:, q * 512:(q + 1) * 512], in_=banks[0], func=AF.Identity), reads=[Bbank[0]], writes=[Bdst])

        row_broadcast2(g2b, B_g2b, 80)
        k.dma(fgb[:], fing_d[:, :], B_fgb, writes=[B_fgb])
        for ts in range(17):
            k.dma(x1t[:], X1_d[ts * 128:(ts + 1) * 128, :], B_x1t, writes=[B_x1t])
            if ts < 16:
                norm_transpose(x1t[:], B_x1t, 128, a2, modx, lambda kc, ts=ts: h2T[:, kc, 1 + ts * 128: 1 + (ts + 1) * 128], B_h2T, boff=48)
            else:
                norm_transpose(x1t[:], B_x1t, 128, a2, modx, lambda kc: hTh[:, kc, :], B_hTh, boff=48)
                k.op(POOL, lambda e: e.tensor_scalar(out=h2T[:, :, 0:1], in0=hTh[:, :, 0:1], scalar1=mk[:, 24:25], scalar2=None, op0=ALU.mult),
                     reads=[B_hTh, B_misc], writes=[B_h2T])
                k.op(POOL, lambda e: e.tensor_scalar(out=h2T[:, :, 2049:2050], in0=hTh[:, :, 1:2], scalar1=mk[:, 25:26], scalar2=None, op0=ALU.mult),
                     reads=[B_hTh, B_misc], writes=[B_h2T])
        wcnt = {"s": 0, "ug": 0, "d": 0}

        def load_cast(src_ap, dst_ap, Bdst_):
            s_ = wcnt["s"] % 2
            wcnt["s"] += 1
            k.dma(wst[s_][:], src_ap, Bwst[s_], writes=[Bwst[s_]])
            k.op(ACT, lambda e, s_=s_: e.activation(out=dst_ap, in_=wst[s_][:], func=AF.Identity), reads=[Bwst[s_]], writes=[Bdst_])

        nwin = 4 if stop_after > 5 else 1
        for w in range(nwin):
            c0 = 512 * w
            for j in range(NJ):
                u_ = wcnt["ug"] % 3
                wcnt["ug"] += 1
                k.dma(wug[u_][:, 0, :], WB_d[0, j, :, :], Bwug[u_], writes=[Bwug[u_]])
                k.dma(wug[u_][:, 1, :], WB_d[1, j, :, :], Bwug[u_], writes=[])
                Bwug[u_].w = (Bwug[u_].dsem, Bwug[u_].dsem.n)
                ub = 4 if j % 2 == 0 else 0
                gb = ub + 1
                k.pe([lambda e, kc=kc, u_=u_, ub=ub: e.matmul(banks[ub], lhsT=wug[u_][:, 0, kc * 128:(kc + 1) * 128], rhs=h2T[:, kc, c0 + 1:c0 + 513],
                                                              start=(kc == 0), stop=(kc == 15)) for kc in range(16)],
                     reads=[Bwug[u_], B_h2T], writes=[Bbank[ub]])
                k.pe([lambda e, kc=kc, u_=u_, gb=gb: e.matmul(banks[gb], lhsT=wug[u_][:, 1, kc * 128:(kc + 1) * 128], rhs=h2T[:, kc, c0:c0 + 512],
                                                              start=(kc == 0), stop=(kc == 15)) for kc in range(16)],
                     reads=[Bwug[u_], B_h2T], writes=[Bbank[gb]])
                k.pe([lambda e, kc=kc, u_=u_, gb=gb: e.matmul(banks[gb + 1][:, 0:2], lhsT=wug[u_][:, 1, kc * 128:(kc + 1) * 128], rhs=h2T[:, kc, c0 + 512:c0 + 514],
                                                              start=(kc == 0), stop=(kc == 15)) for kc in range(16)],
                     reads=[Bwug[u_], B_h2T], writes=[Bbank[gb + 1]])
                gps = psall[:, gb * 512: gb * 512 + 514]
                cw = lambda t_: vecs[:, V_FCW + t_ * 43 + j: V_FCW + t_ * 43 + j + 1]
                k.op(DVE, lambda e, gps=gps: e.tensor_scalar(out=cvt[:], in0=gps[:, 0:512], scalar1=cw(0), scalar2=None, op0=ALU.mult),
                     reads=[Bbank[gb], Bbank[gb + 1], B_vecs], writes=[B_cvt])
                k.op(DVE, lambda e, gps=gps: e.scalar_tensor_tensor(out=cvt[:], in0=gps[:, 1:513], scalar=cw(1), in1=cvt[:], op0=ALU.mult, op1=ALU.add),
                     reads=[Bbank[gb], Bbank[gb + 1], B_vecs, B_cvt], writes=[B_cvt])
                k.op(DVE, lambda e, gps=gps: e.scalar_tensor_tensor(out=cvt[:], in0=gps[:, 2:514], scalar=cw(2), in1=cvt[:], op0=ALU.mult, op1=ALU.add),
                     reads=[Bbank[gb], Bbank[gb + 1], B_vecs, B_cvt], writes=[B_cvt])
                k.op(ACT, lambda e, j=j: e.activation(out=glt[:], in_=cvt[:], func=AF.Gelu_apprx_tanh, bias=vecs[:, V_FCB + j:V_FCB + j + 1]),
                     reads=[B_cvt, B_vecs], writes=[B_glt])
                k.op(DVE, lambda e, j=j, ub=ub: e.tensor_tensor(out=actT[:, j, :], in0=banks[ub], in1=glt[:], op=ALU.mult),
                     reads=[Bbank[ub], B_glt], writes=[B_act])
            for pair in range(2):
                for j in range(NJ):
                    d_ = wcnt["d"] % 3
                    wcnt["d"] += 1
                    k.dma(wdb[d_][:], WB_d[2, j, :, :], Bwd[d_], writes=[Bwd[d_]])
                    fns = []
                    for t2_ in range(2):
                        ts4 = pair * 2 + t2_
                        for nb in range(4):
                            fns.append(lambda e, j=j, ts4=ts4, nb=nb, t2_=t2_, d_=d_: e.matmul(
                                banks[t2_ * 4 + nb], lhsT=actT[:, j, ts4 * 128:(ts4 + 1) * 128], rhs=wdb[d_][:, nb * 512:(nb + 1) * 512],
                                start=(j == 0), stop=(j == NJ - 1)))
                    k.pe(fns, reads=[B_act, Bwd[d_]], writes=Bbank)
                for t2_ in range(2):
                    ts4 = pair * 2 + t2_
                    row0 = c0 + ts4 * 128
                    k.dma(x1t[:], X1_d[row0:row0 + 128, :], B_x1t, writes=[B_x1t])
                    k.op(DVE, lambda e, t2_=t2_: e.tensor_tensor(out=x2t[:], in0=psall[:, t2_ * 2048:(t2_ + 1) * 2048], in1=g2b[:], op=ALU.mult),
                         reads=Bbank + [B_g2b], writes=[B_x2t])
                    k.op(DVE, lambda e: e.tensor_tensor(out=x2t[:], in0=x2t[:], in1=x1t[:], op=ALU.add), reads=[B_x2t, B_x1t], writes=[B_x2t])
                    k.op(ACT, lambda e: e.activation(out=junk[:, :], in_=x2t[:], func=AF.Square, accum_out=fr[:, 0:1]), reads=[B_x2t], writes=[B_junk, B_fr])
                    k.op(ACT, lambda e: e.activation(out=fr[:, 1:2], in_=fr[:, 0:1], func=AF.Ln, scale=1.0 / D, bias=epsc[:, 0:1]), reads=[B_fr, B_const], writes=[B_fr])
                    k.op(ACT, lambda e: e.activation(out=fr[:, 2:3], in_=fr[:, 1:2], func=AF.Exp, scale=-0.5), reads=[B_fr], writes=[B_fr])
                    k.op(DVE, lambda e: e.scalar_tensor_tensor(out=x1t[:], in0=x2t[:], scalar=fr[:, 2:3], in1=fgb[:], op0=ALU.mult, op1=ALU.mult),
                         reads=[B_x2t, B_fr, B_fgb, B_x1t], writes=[B_x1t])
                    k.dma(out_d[row0:row0 + 128, :], x1t[:], B_x1t, reads=[B_x1t])
        fin = [(B_x1t.dsem, B_x1t.dsem.n)]
        SP.wait(fin)
        bar = []
        for b_ in [B_h2T, B_hTh, B_act, B_x1t, B_x2t, B_g2b, B_fgb, B_cvt, B_glt, B_fr] + Bwst + Bwug + Bwd + Bbank:
            bar.append(b_.w)
            bar += b_.rtoks()
    for E in (PE, ACT, DVE, POOL, SP):
        E.wait(bar)
    return nc


def rope_tables(tok):
    inv = (10000.0 ** (-np.arange(16, dtype=np.float32) / 16)).astype(np.float32)
    tok = np.asarray(tok)
    row = (tok // 64).astype(np.float32)
    col = (tok % 64).astype(np.float32)
    ang = np.stack([row[None, :] * inv[:, None], col[None, :] * inv[:, None]], 0).astype(np.float32)
    cos = np.cos(ang).astype(np.float32)
    sin = np.sin(ang).astype(np.float32)
    C = np.zeros((128, len(tok)), np.float32)
    Sn = np.zeros((128, len(tok)), np.float32)
    for c in range(2):
        for ax in range(2):
            for half in range(2):
                p0 = c * 64 + ax * 32 + half * 16
                C[p0:p0 + 16] = cos[ax]
                Sn[p0:p0 + 16] = sin[ax] * (-1.0 if half == 0 else 1.0)
    return C, Sn


def pcl(v):
    v = np.asarray(v, np.float32)
    return np.ascontiguousarray(v.reshape(-1, 128).T)


def host_inputs(inp):
    f32 = np.float32
    x = np.ascontiguousarray(inp["x"][0], f32)
    ctx = np.ascontiguousarray(inp["ctx"][0], f32)
    shared = {}
    shared["x"] = x
    shared["ctx"] = ctx
    cv = np.stack([pcl(inp["c"][0]), pcl(inp["c_ctx"])], -1)
    shared["cvec"] = np.ascontiguousarray(cv.reshape(128, 32))
    vecs = np.zeros((128, NV), f32)
    vecs[:, V_BADA:V_BADA + 96] = pcl(inp["b_ada"][0])
    vecs[:, V_N1G:V_N1G + 16] = pcl(inp["norm1_g"][0])
    vecs[:, V_N2G:V_N2G + 16] = pcl(inp["norm2_g"][0])
    for j in range(3):
        vecs[:, V_FCW + j * 43:V_FCW + (j + 1) * 43] = pcl(inp["ffn_conv_w"][0, j])
    vecs[:, V_FCB:V_FCB + 43] = pcl(inp["ffn_conv_b"][0])
    for j in range(4):
        vecs[:, V_RCW + j * 8:V_RCW + (j + 1) * 8] = pcl(inp["rec_conv_w"][0, j])
    vecs[:, V_RCB:V_RCB + 8] = pcl(inp["rec_conv_b"][0])
    for d in range(2):
        vecs[:, V_RBA + d * 8:V_RBA + (d + 1) * 8] = pcl(inp["rg_ba"][0, d])
        vecs[:, V_RBI + d * 8:V_RBI + (d + 1) * 8] = pcl(inp["rg_bi"][0, d])
        vecs[:, V_RLAM + d * 8:V_RLAM + (d + 1) * 8] = pcl(inp["rg_lambda"][0, d])
    shared["vecs"] = vecs
    shared["wada"] = np.ascontiguousarray(inp["w_ada"][0], f32)
    shared["win"] = np.ascontiguousarray(inp["w_in"][0], f32)
    shared["wout"] = np.ascontiguousarray(inp["w_out"][0], f32)
    for nm, key in (("wup", "w_up"), ("wgate", "w_gate")):
        w = np.asarray(inp[key][0], f32).reshape(16, 128, NJ, 128)
        shared[nm] = np.ascontiguousarray(w.transpose(2, 1, 0, 3).reshape(NJ, 128, 2048))
    shared["wdown"] = np.ascontiguousarray(np.asarray(inp["w_down"][0], f32).reshape(NJ, 128, 2048))
    rg = np.stack([np.asarray(inp["rg_wa"][0], f32), np.asarray(inp["rg_wi"][0], f32)], 0)
    shared["rgw"] = np.ascontiguousarray(rg.transpose(3, 0, 1, 2, 4).reshape(128, 32 * 128))
    C, Sn = rope_tables(np.arange(S))
    shared["cos"] = C
    shared["sin"] = Sn
    perm = np.zeros((128, 128), f32)
    for p in range(128):
        perm[p ^ 16, p] = 1.0
    shared["perm"] = perm
    shared["identf"] = np.eye(128, dtype=f32)
    shared["identb"] = np.eye(128).astype(ml_dtypes.bfloat16)
    shared["dlam"] = np.ascontiguousarray(np.broadcast_to(np.asarray(inp["diff_lambda"][0], f32).reshape(1, 256), (128, 256)))
    shared["subg"] = np.ascontiguousarray(np.broadcast_to(np.asarray(inp["subln_g"][0], f32).reshape(1, 128), (128, 128)))
    shared["fing"] = np.ascontiguousarray(np.broadcast_to(np.asarray(inp["final_g"], f32).reshape(1, D), (128, D)))
    maps = []
    for c in range(NCORES):
        m = dict(shared)
        xo = np.zeros((17 * 128, D), f32)
        t0 = c * OWN
        xo[0:OWN] = x[t0:t0 + OWN]
        toks = np.zeros(EXT, np.int64)
        toks[0:OWN] = np.arange(t0, t0 + OWN)
        if c > 0:
            xo[OWN] = x[t0 - 1]
            toks[OWN] = t0 - 1
        if c < NCORES - 1:
            xo[OWN + 1] = x[t0 + OWN]
            toks[OWN + 1] = t0 + OWN
        m["xo"] = xo
        Co, So = rope_tables(toks)
        Cp = np.zeros((128, 17 * 128), f32)
        Sp_ = np.zeros((128, 17 * 128), f32)
        Cp[:, :EXT] = Co
        Sp_[:, :EXT] = So
        m["coso"] = Cp
        m["sino"] = Sp_
        mk = np.zeros((128, 32), f32)
        mk[:, c] = 1.0
        if c > 0:
            mk[:, 8 + c - 1] = 1.0
            mk[:, 24] = 1.0
        if c < NCORES - 1:
            mk[:, 16 + c + 1] = 1.0
            mk[:, 25] = 1.0
        m["mk"] = mk
        maps.append(m)
    return maps


STOP_AFTER = 99


def kernel(**inputs):
    maps = host_inputs(inputs)
    nc = build_program(stop_after=STOP_AFTER)
    res = run_bass_kernel_spmd(nc, maps, core_ids=list(range(NCORES)))
    out = np.concatenate([np.asarray(r["out"], np.float32) for r in res.results], 0)
    return out.reshape(1, S, D)
```

```python
import os
from contextlib import ExitStack
import numpy as np
import ml_dtypes
import concourse.bass as bass
import concourse.mybir as mybir
from concourse.bass_utils import run_bass_kernel_spmd

F32 = mybir.dt.float32
BF16 = mybir.dt.bfloat16
AF = mybir.ActivationFunctionType
ALU = mybir.AluOpType

D = 2048
S = 16384
NCTX = 256
DFF = 5504
NJ = 43
NKEY = S + NCTX
NKT = NKEY // 128
OWN = 2048
EXT = 2050
NCORES = 8
EPS = 1e-6
SUBLN_EPS = 1e-5
LAM_INIT = 0.8 - 0.6
TT = 256
SKIP = os.environ.get('P1SKIP', '')

V_BADA = 0
V_N1G = 96
V_N2G = 112
V_FCW = 128
V_FCB = V_FCW + 129
V_RCW = V_FCB + 43
V_RCB = V_RCW + 32
V_RBA = V_RCB + 8
V_RBI = V_RBA + 16
V_RLAM = V_RBI + 16
NV = V_RLAM + 16


class Sem:
    _k = 0

    def __init__(self, nc, name):
        self.h = nc.alloc_semaphore(name)
        self.n = 0
        Sem._k += 1
        self.key = Sem._k


class Eng:
    def __init__(self, nc, eng, name, is_pe=False):
        self.e = eng
        self.sem = Sem(nc, "s_" + name)
        self.seen = {}
        self.is_pe = is_pe

    def wait(self, toks):
        best = {}
        for t in toks:
            if t is None:
                continue
            sem, val = t
            if self.is_pe and sem is self.sem:
                continue
            if self.seen.get(sem.key, 0) >= val:
                continue
            if sem.key not in best or best[sem.key][1] < val:
                best[sem.key] = (sem, val)
        for sem, val in best.values():
            self.e.wait_ge(sem.h, val)
            self.seen[sem.key] = val

    def mark(self, ins):
        self.sem.n += 1
        ins.then_inc(self.sem.h, 1)
        return (self.sem, self.sem.n)


class Buf:
    def __init__(self, name=""):
        self.name = name
        self.w = None
        self.r = {}
        self.dsem = None

    def rtoks(self):
        return list(self.r.values())

    def add_r(self, tok):
        sem, val = tok
        if sem.key not in self.r or self.r[sem.key][1] < val:
            self.r[sem.key] = tok


class K:
    def __init__(self, nc):
        self.nc = nc
        self.PE = Eng(nc, nc.tensor, "pe", is_pe=True)
        self.ACT = Eng(nc, nc.scalar, "act")
        self.DVE = Eng(nc, nc.vector, "dve")
        self.POOL = Eng(nc, nc.gpsimd, "pool")
        self.SP = Eng(nc, nc.sync, "sp")
        self.nsem = 5

    def _deps(self, reads, writes):
        toks = []
        for b in reads:
            toks.append(b.w)
        for b in writes:
            toks.append(b.w)
            toks += b.rtoks()
        return toks

    def _commit(self, tok, reads, writes):
        for b in reads:
            b.add_r(tok)
        for b in writes:
            b.w = tok
            b.r = {}

    def op(self, E, fn, reads=(), writes=()):
        E.wait(self._deps(reads, writes))
        ins = fn(E.e)
        tok = E.mark(ins)
        self._commit(tok, reads, writes)
        return tok

    def pe(self, fns, reads=(), writes=()):
        E = self.PE
        E.wait(self._deps(reads, writes))
        ins = None
        for f in fns:
            ins = f(E.e)
        tok = E.mark(ins)
        self._commit(tok, reads, writes)
        return tok

    def dma(self, out, in_, sbuf, reads=(), writes=(), eng=None):
        E = eng or self.SP
        if sbuf.dsem is None:
            sbuf.dsem = Sem(self.nc, "d_%d" % self.nsem)
            self.nsem += 1
        E.wait(self._deps(reads, writes))
        sbuf.dsem.n += 16
        E.e.dma_start(out=out, in_=in_).then_inc(sbuf.dsem.h, 16)
        tok = (sbuf.dsem, sbuf.dsem.n)
        self._commit(tok, reads, writes)
        return tok


def build_program(debug=False, stop_after=99):
    nc = bass.Bass("TRN2", target_bir_lowering=False)
    k = K(nc)
    PE, ACT, DVE, POOL, SP = k.PE, k.ACT, k.DVE, k.POOL, k.SP

    def din(name, shape, dt=F32):
        return nc.dram_tensor(name, list(shape), dt, kind="ExternalInput").ap()

    x_d = din("x", [S, D])
    ctx_d = din("ctx", [NCTX, D])
    xo_d = din("xo", [17 * 128, D])
    cvec_d = din("cvec", [128, 32])
    vecs_d = din("vecs", [128, NV])
    wada_d = din("wada", [D, 6 * D])
    win_d = din("win", [D, 5120])
    wout_d = din("wout", [D, D])
    wup_d = din("wup", [NJ, 128, 2048])
    wgate_d = din("wgate", [NJ, 128, 2048])
    wdown_d = din("wdown", [NJ, 128, 2048])
    rgw_d = din("rgw", [128, 32 * 128])
    cos_d = din("cos", [128, S])
    sin_d = din("sin", [128, S])
    coso_d = din("coso", [128, 17 * 128])
    sino_d = din("sino", [128, 17 * 128])
    perm_d = din("perm", [128, 128])
    ident_d = din("identf", [128, 128])
    identb_d = din("identb", [128, 128], BF16)
    dlam_d = din("dlam", [128, 256])
    subg_d = din("subg", [128, 128])
    fing_d = din("fing", [128, D])
    mk_d = din("mk", [128, 32])
    out_d = nc.dram_tensor("out", [OWN, D], F32, kind="ExternalOutput").ap()

    KT_d = nc.dram_tensor("KT", [8, 128, NKEY], BF16).ap()
    VV_d = nc.dram_tensor("VV", [8, 128, NKT, 129], BF16).ap()
    RXL_d = nc.dram_tensor("RXL", [8, 128, S], F32).ap()
    RXC_d = nc.dram_tensor("RXC", [8, 128, NCTX], F32).ap()
    dbg = {}
    if debug:
        dbg["mod"] = nc.dram_tensor("dbg_mod", [128, 192], F32, kind="ExternalOutput").ap()
        dbg["KT"] = nc.dram_tensor("dbg_KT", [8, 128, 1024], BF16, kind="ExternalOutput").ap()
        dbg["VV"] = nc.dram_tensor("dbg_VV", [8, 128, 8, 129], BF16, kind="ExternalOutput").ap()
        dbg["RX"] = nc.dram_tensor("dbg_RX", [8, 128, 1024], F32, kind="ExternalOutput").ap()
        dbg["AT"] = nc.dram_tensor("dbg_AT", [128, 17 * 128], BF16, kind="ExternalOutput").ap()
        dbg["QT"] = nc.dram_tensor("dbg_QT", [128, 17 * 128], BF16, kind="ExternalOutput").ap()
        dbg["GZ"] = nc.dram_tensor("dbg_GZ", [128, 17 * 128], BF16, kind="ExternalOutput").ap()

    def sb(name, shape, dt):
        return nc.alloc_sbuf_tensor("sb_" + name, shape, dt)
    pst = nc.alloc_psum_tensor

    vecs = sb("vecs", [128, NV], F32)
    modx = sb("modx", [128, 96], F32)
    modc = sb("modc", [128, 96], F32)
    a1 = sb("a1", [128, 16], F32)
    a1c = sb("a1c", [128, 16], F32)
    a2 = sb("a2", [128, 16], F32)
    modacc = sb("modacc", [128, 192], F32)
    identb = sb("identb", [128, 128], BF16)
    identf = sb("identf", [128, 128], F32)
    permT = sb("permT", [128, 128], F32)
    ones_f = sb("ones_f", [128, 128], F32)
    B_vecs = Buf("vecs")
    B_mod = Buf("mod")
    B_const = Buf("const")

    k.dma(vecs[:], vecs_d[:, :], B_vecs, writes=[B_vecs])
    k.dma(identb[:], identb_d[:, :], B_const, writes=[B_const])
    k.dma(identf[:], ident_d[:, :], B_const, writes=[B_const])
    k.dma(permT[:], perm_d[:, :], B_const, writes=[B_const])
    k.op(k.DVE, lambda e: e.memset(ones_f[:], 1.0), writes=[B_const])

    psall = pst("psall", [128, 4096], F32)
    banks = [psall[:, i * 512:(i + 1) * 512] for i in range(8)]
    Bbank = [Buf("bank%d" % i) for i in range(8)]
    tpbs = [banks[6].bitcast(BF16), banks[7].bitcast(BF16)]
    Btp = [Bbank[6], Bbank[7]]

    with nc.sbuf_tensor("sb_wk0", [128, 6 * D], F32) as wk0, nc.sbuf_tensor("sb_wk1", [128, 6 * D], F32) as wk1, \
            nc.sbuf_tensor("sb_cvec", [128, 32], F32) as cvec, nc.sbuf_tensor("sb_scv", [128, 32], F32) as scv:
        wk = [wk0, wk1]
        Bwk = [Buf("wk0"), Buf("wk1")]
        B_cv = Buf("cvec")
        B_scv = Buf("scv")
        k.dma(cvec[:], cvec_d[:, :], B_cv, writes=[B_cv])
        k.op(ACT, lambda e: e.activation(out=scv[:], in_=cvec[:], func=AF.Silu), reads=[B_cv], writes=[B_scv])
        macc = modacc
        B_macc = Buf("macc")
        for kc in range(16):
            s = kc % 2
            for q in range(4):
                k.dma(wk[s][:, q * 3072:(q + 1) * 3072], wada_d[kc * 128:(kc + 1) * 128, q * 3072:(q + 1) * 3072],
                      Bwk[s], writes=[Bwk[s]] if q == 0 else [])
            Bwk[s].w = (Bwk[s].dsem, Bwk[s].dsem.n)
            psm = banks[s]
            fns = []
            for j in range(96):
                fns.append(lambda e, j=j, s=s, kc=kc, psm=psm: e.matmul(
                    psm[:, 2 * j:2 * j + 2], lhsT=wk[s][:, j * 128:(j + 1) * 128], rhs=scv[:, 2 * kc:2 * kc + 2],
                    start=True, stop=True))
            k.pe(fns, reads=[Bwk[s], B_scv], writes=[Bbank[s]])
            if kc == 0:
                k.op(DVE, lambda e, psm=psm: e.tensor_copy(macc[:], psm[:, 0:192]), reads=[Bbank[s]], writes=[B_macc])
            else:
                k.op(DVE, lambda e, psm=psm: e.tensor_tensor(out=macc[:], in0=macc[:], in1=psm[:, 0:192], op=ALU.add),
                     reads=[Bbank[s], B_macc], writes=[B_macc])
        psv = macc[:].rearrange("p (j t) -> p j t", t=2)
        k.op(DVE, lambda e: e.tensor_tensor(out=modx[:], in0=psv[:, :, 0], in1=vecs[:, V_BADA:V_BADA + 96], op=ALU.add),
             reads=[B_macc, B_vecs], writes=[B_mod])
        k.op(DVE, lambda e: e.tensor_tensor(out=modc[:], in0=psv[:, :, 1], in1=vecs[:, V_BADA:V_BADA + 96], op=ALU.add),
             reads=[B_macc, B_vecs], writes=[B_mod])
        k.op(DVE, lambda e: e.scalar_tensor_tensor(out=a1[:], in0=modx[:, 16:32], scalar=1.0, in1=vecs[:, V_N1G:V_N1G + 16],
                                                   op0=ALU.add, op1=ALU.mult), reads=[B_mod, B_vecs], writes=[B_mod])
        k.op(DVE, lambda e: e.scalar_tensor_tensor(out=a1c[:], in0=modc[:, 16:32], scalar=1.0, in1=vecs[:, V_N1G:V_N1G + 16],
                                                   op0=ALU.add, op1=ALU.mult), reads=[B_mod, B_vecs], writes=[B_mod])
        k.op(DVE, lambda e: e.scalar_tensor_tensor(out=a2[:], in0=modx[:, 64:80], scalar=1.0, in1=vecs[:, V_N2G:V_N2G + 16],
                                                   op0=ALU.add, op1=ALU.mult), reads=[B_mod, B_vecs], writes=[B_mod])
        if debug:
            B_dm = Buf("dbgmod")
            k.dma(dbg["mod"][:, 0:96], modx[:], B_dm, reads=[B_mod])
            k.dma(dbg["mod"][:, 96:192], modc[:], B_dm, reads=[B_mod])
        bar = [Bwk[0].w, Bwk[1].w, Bbank[0].w, Bbank[1].w, B_mod.w] + Bwk[0].rtoks() + Bwk[1].rtoks() + B_scv.rtoks()
    for E in (PE, ACT, DVE, POOL, SP):
        E.wait(bar)
    if debug and stop_after == 0:
        SP.wait([(B_dm.dsem, B_dm.dsem.n)])
        return nc
    if stop_after == 0:
        return nc

    epsc = sb("epsc", [128, 2], F32)
    k.op(DVE, lambda e: e.memset(epsc[:, 0:1], EPS), writes=[B_const])
    k.op(DVE, lambda e: e.memset(epsc[:, 1:2], SUBLN_EPS), writes=[B_const])
    junk = sb("junk", [128, D], BF16)
    B_junk = Buf("junk")
    xn = [sb("xn%d" % i, [128, D], BF16) for i in range(2)]
    Bxn = [Buf("xn%d" % i) for i in range(2)]
    ssq = [sb("ssq%d" % i, [128, 4], F32) for i in range(2)]
    Bssq = [Buf("ssq%d" % i) for i in range(2)]
    cnt = {"nt": 0, "tp": 0, "ev": 0}

    def norm_transpose(src_ap, Bsrc, rows, avec, bvec, dst_fn, Bdst, boff=0):
        i = cnt["nt"] % 2
        cnt["nt"] += 1
        sq = ssq[i]
        k.op(ACT, lambda e: e.activation(out=junk[0:rows, :], in_=src_ap, func=AF.Square, accum_out=sq[0:rows, 0:1]),
             reads=[Bsrc], writes=[B_junk, Bssq[i]])
        k.op(ACT, lambda e: e.activation(out=sq[0:rows, 1:2], in_=sq[0:rows, 0:1], func=AF.Ln, scale=1.0 / D, bias=epsc[0:rows, 0:1]),
             reads=[Bssq[i], B_const], writes=[Bssq[i]])
        k.op(ACT, lambda e: e.activation(out=sq[0:rows, 2:3], in_=sq[0:rows, 1:2], func=AF.Exp, scale=-0.5),
             reads=[Bssq[i]], writes=[Bssq[i]])
        k.op(DVE, lambda e: e.tensor_scalar(out=xn[i][0:rows, :], in0=src_ap, scalar1=sq[0:rows, 2:3], scalar2=None, op0=ALU.mult),
             reads=[Bsrc, Bssq[i]], writes=[Bxn[i]])
        for g in range(2):
            hb = cnt["tp"] % 2
            cnt["tp"] += 1
            tpb = tpbs[hb]
            fns = []
            for q in range(8):
                kc = g * 8 + q
                fns.append(lambda e, kc=kc, q=q, tpb=tpb: e.transpose(tpb[:, q * 128: q * 128 + rows],
                                                                      xn[i][0:rows, kc * 128:(kc + 1) * 128], identb[0:rows, 0:rows]))
            k.pe(fns, reads=[Bxn[i], B_const], writes=[Btp[hb]])
            for q in range(8):
                kc = g * 8 + q
                src = tpb[:, q * 128: q * 128 + rows]
                k.op(ACT, lambda e, kc=kc, src=src: e.activation(out=dst_fn(kc), in_=src, func=AF.Identity,
                                                                scale=avec[:, kc:kc + 1], bias=bvec[:, boff + kc:boff + kc + 1]),
                     reads=[Btp[hb], B_mod], writes=[Bdst])

    with ExitStack() as es:
        def sc(name, shape, dt):
            return es.enter_context(nc.sbuf_tensor("sb_" + name, shape, dt))
        wkvr = sc("wkvr", [128, 16, 3072], BF16)
        wst0 = sc("wst0", [128, 1024], F32); wst1 = sc("wst1", [128, 1024], F32)
        xs0 = sc("xs0", [128, 2, D], F32); xs1 = sc("xs1", [128, 2, D], F32)
        hT0 = sc("hT0", [128, 16, TT], BF16); hT1 = sc("hT1", [128, 16, TT], BF16)
        cs0 = sc("cs0", [128, 2, TT], F32); cs1 = sc("cs1", [128, 2, TT], F32)
        k32a = sc("k32a", [128, TT], F32); k32b = sc("k32b", [128, TT], F32)
        t1a = sc("t1a", [128, TT], F32); t1b = sc("t1b", [128, TT], F32)
        t2a = sc("t2a", [128, TT], F32); t2b = sc("t2b", [128, TT], F32)
        ko = sc("ko", [128, 4, TT], BF16); rxo = sc("rxo", [128, 4, TT], F32)
        vt0 = sc("vt0", [128, 8, 2, 129], BF16); vt1 = sc("vt1", [128, 8, 2, 129], BF16)
        wst = [wst0, wst1]
        Bwst = [Buf(), Buf()]
        B_w = Buf("wkvr")
        n = 0
        for kc in range(16 if 'W' not in SKIP else 0):
            for gi in range(3):
                s_ = n % 2
                n += 1
                k.dma(wst[s_][:], win_d[kc * 128:(kc + 1) * 128, 1024 + gi * 1024: 2048 + gi * 1024], Bwst[s_], writes=[Bwst[s_]])
                k.op(ACT, lambda e, s_=s_, kc=kc, gi=gi: e.activation(out=wkvr[:, kc, gi * 1024:(gi + 1) * 1024], in_=wst[s_][:], func=AF.Identity),
                     reads=[Bwst[s_]], writes=[B_w])
        xs = [xs0, xs1]
        Bxs = [Buf(), Buf()]
        hT = [hT0, hT1]
        BhT = [Buf(), Buf()]
        cs = [cs0, cs1]
        Bcs = [Buf(), Buf()]
        k32 = [k32a, k32b]
        Bk32 = [Buf(), Buf()]
        t1 = [t1a, t1b]
        Bt1 = [Buf(), Buf()]
        t2 = [t2a, t2b]
        Bt2 = [Buf(), Buf()]
        Bko = [Buf() for _ in range(4)]
        Brxo = [Buf() for _ in range(4)]
        vt = [vt0, vt1]
        Bvt = [Buf(), Buf()]
        for v_ in vt:
            k.op(POOL, lambda e, v_=v_: e.memset(v_[:, :, :, 128:129], 1.0), writes=[Bvt[0], Bvt[1]])
        B_scr = Buf("scratch")
        ntiles = 1 + S // TT
        if stop_after == 1 and debug:
            ntiles = 5
        NT_P1 = ntiles
        gslot = {"o": 0, "p": 0, "v": 0, "ko": 0, "rx": 0, "k32": 0}

        def load_tile(ti):
            s_ = ti % 2
            src = ctx_d if ti == 0 else x_d
            r0 = 0 if ti == 0 else (ti - 1) * TT
            k.dma(xs[s_][:], src[r0:r0 + TT, :].rearrange("(s p) d -> p s d", p=128), Bxs[s_], writes=[Bxs[s_]])
            if ti > 0:
                k.dma(cs[s_][:, 0, :], cos_d[:, r0:r0 + TT], Bcs[s_], writes=[Bcs[s_]])
                k.dma(cs[s_][:, 1, :], sin_d[:, r0:r0 + TT], Bcs[s_], writes=[])
                Bcs[s_].w = (Bcs[s_].dsem, Bcs[s_].dsem.n)

        def nt_tile(ti):
            s_ = ti % 2
            av, bv = (a1c, modc) if ti == 0 else (a1, modx)
            for su in range(2):
                norm_transpose(xs[s_][:, su, :], Bxs[s_], 128, av, bv,
                               lambda kc, su=su, s_=s_: hT[s_][:, kc, su * 128:(su + 1) * 128], BhT[s_])

        load_tile(0)
        nt_tile(0)
        for ti in range(ntiles):
            s_ = ti % 2
            if ti + 1 < ntiles:
                load_tile(ti + 1)
            key0 = 0 if ti == 0 else NCTX + (ti - 1) * TT
            pending = []
            for h in range(8):
                oslot = gslot["o"] % 2
                gslot["o"] += 1
                ob = banks[oslot][:, 0:TT]
                Bo = Bbank[oslot]
                k.pe([lambda e, kc=kc, h=h, ob=ob: e.matmul(ob, lhsT=wkvr[:, kc, h * 128:(h + 1) * 128], rhs=hT[s_][:, kc, :],
                                                            start=(kc == 0), stop=(kc == 15)) for kc in range(16)],
                     reads=[B_w, BhT[s_]], writes=[Bo])
                ks = gslot["ko"] % 4
                gslot["ko"] += 1
                if ti == 0:
                    k.op(ACT, lambda e, ob=ob, ks=ks: e.activation(out=ko[:, ks, :], in_=ob, func=AF.Identity), reads=[Bo], writes=[Bko[ks]])
                    k.dma(KT_d[h, :, key0:key0 + TT], ko[:, ks, :], Bko[ks], reads=[Bko[ks]])
                    continue
                q_ = gslot["k32"] % 2
                gslot["k32"] += 1
                k.op(ACT, lambda e, ob=ob, q_=q_: e.activation(out=k32[q_][:], in_=ob, func=AF.Identity), reads=[Bo], writes=[Bk32[q_]])

                def post(h=h, q_=q_, ks=ks):
                    pb = banks[2][:, 0:TT]
                    k.pe([lambda e, pb=pb, q_=q_: e.matmul(pb, lhsT=permT[:], rhs=k32[q_][:], start=True, stop=True)],
                         reads=[Bk32[q_], B_const], writes=[Bbank[2]])
                    k.op(DVE, lambda e, pb=pb, q_=q_: e.tensor_tensor(out=t1[q_][:], in0=pb, in1=cs[s_][:, 1, :], op=ALU.mult),
                         reads=[Bbank[2], Bcs[s_]], writes=[Bt1[q_]])
                    k.op(POOL, lambda e, q_=q_: e.tensor_tensor(out=t2[q_][:], in0=k32[q_][:], in1=cs[s_][:, 0, :], op=ALU.mult),
                         reads=[Bk32[q_], Bcs[s_]], writes=[Bt2[q_]])
                    k.op(POOL, lambda e, q_=q_, ks=ks: e.tensor_tensor(out=ko[:, ks, :], in0=t1[q_][:], in1=t2[q_][:], op=ALU.add),
                         reads=[Bt1[q_], Bt2[q_]], writes=[Bko[ks]])
                    k.dma(KT_d[h, :, key0:key0 + TT], ko[:, ks, :], Bko[ks], reads=[Bko[ks]])
                if pending:
                    pending.pop(0)()
                pending.append(post)
            if ti + 1 < ntiles:
                nt_tile(ti + 1)
            while pending:
                pending.pop(0)()
            for b in range(0 if 'R' not in SKIP else 8, 8):
                oslot = gslot["o"] % 2
                gslot["o"] += 1
                ob = banks[oslot][:, 0:TT]
                Bo = Bbank[oslot]
                k.pe([lambda e, kc=kc, b=b, ob=ob: e.matmul(ob, lhsT=wkvr[:, kc, 2048 + b * 128:2048 + (b + 1) * 128], rhs=hT[s_][:, kc, :],
                                                            start=(kc == 0), stop=(kc == 15)) for kc in range(16)],
                     reads=[B_w, BhT[s_]], writes=[Bo])
                rs = gslot["rx"] % 4
                gslot["rx"] += 1
                k.op(ACT, lambda e, ob=ob, rs=rs: e.activation(out=rxo[:, rs, :], in_=ob, func=AF.Identity), reads=[Bo], writes=[Brxo[rs]])
                dst = RXC_d[b, :, :] if ti == 0 else RXL_d[b, :, (ti - 1) * TT: ti * TT]
                k.dma(dst, rxo[:, rs, :], Brxo[rs], reads=[Brxo[rs]])
            vs_ = ti % 2
            for su in range(0 if 'V' not in SKIP else 2, 2):
                for half in range(2):
                    vb_i = 3 + gslot["v"] % 3
                    gslot["v"] += 1
                    vb = banks[vb_i]
                    k.pe([lambda e, kc=kc, su=su, half=half, vb=vb: e.matmul(
                        vb, lhsT=hT[s_][:, kc, su * 128:(su + 1) * 128], rhs=wkvr[:, kc, 1024 + half * 512:1024 + (half + 1) * 512],
                        start=(kc == 0), stop=(kc == 15)) for kc in range(16)],
                        reads=[B_w, BhT[s_]], writes=[Bbank[vb_i]])
                    k.op(DVE, lambda e, su=su, half=half, vb=vb, vs_=vs_: e.tensor_copy(
                        vt[vs_][:, half * 4:(half + 1) * 4, su, 0:128], vb.rearrange("p (h d) -> p h d", h=4)),
                        reads=[Bbank[vb_i]], writes=[Bvt[vs_]])
            kt0 = key0 // 128
            for h in range(0 if 'V' not in SKIP else 8, 8):
                k.dma(VV_d[h, :, kt0:kt0 + 2, :], vt[vs_][:, h, :, :], Bvt[vs_], reads=[Bvt[vs_]])
        bar = []
        for b_ in Bko + Brxo + Bvt + Bxs + Bcs + BhT + Bk32 + Bt1 + Bt2 + Bwst + [B_w] + Bbank + Btp + Bxn + Bssq + [B_junk]:
            bar.append(b_.w)
            bar += b_.rtoks()
            if b_.dsem is not None:
                bar.append((b_.dsem, b_.dsem.n))
    for E in (PE, ACT, DVE, POOL, SP):
        E.wait(bar)

    if debug and stop_after == 1 and 'D' in SKIP:
        return nc
    if debug and stop_after == 1:
        with nc.sbuf_tensor("sb_dbt", [128, 8, 129 * 8], BF16) as dbt, nc.sbuf_tensor("sb_dbf", [128, 8, 1024], F32) as dbf:
            Bd = Buf()
            k.dma(dbt[:, :, 0:1024], KT_d[:, :, 0:1024].rearrange("h p n -> p h n"), Bd, writes=[Bd])
            k.dma(dbg["KT"].rearrange("h p n -> p h n"), dbt[:, :, 0:1024], Bd, reads=[Bd])
            Bd2 = Buf()
            k.dma(dbf[:], RXL_d[:, :, 0:1024].rearrange("h p n -> p h n"), Bd2, writes=[Bd2])
            k.dma(dbg["RX"].rearrange("h p n -> p h n"), dbf[:], Bd2, reads=[Bd2])
            Bd3 = Buf()
            k.dma(dbt[:].rearrange("p h (t d) -> p h t d", d=129), VV_d[:, :, 0:8, :].rearrange("h p t d -> p h t d"), Bd3,
                  reads=[Bd], writes=[Bd3])
            k.dma(dbg["VV"].rearrange("h p t d -> p h t d"), dbt[:].rearrange("p h (t d) -> p h t d", d=129), Bd3, reads=[Bd3])
            fin = [(b_.dsem, b_.dsem.n) for b_ in (Bd, Bd2, Bd3)]
            SP.wait(fin)
        return nc


    EXTP = 17 * 128
    B_QT, B_GZ, B_AT = Buf("QT"), Buf("GZ"), Buf("AT")
    dl = sb("dl", [128, 256], F32)
    subg8 = sb("subg8", [128, 128], F32)
    lamv = sb("lamv", [128, 8], F32)
    mk = sb("mk", [128, 32], F32)
    B_misc = Buf("misc")
    k.dma(dl[:], dlam_d[:, :], B_misc, writes=[B_misc])
    k.dma(subg8[:], subg_d[:, :], B_misc, writes=[B_misc])
    k.dma(mk[:], mk_d[:, :], B_misc, writes=[B_misc])
    k.op(DVE, lambda e: e.tensor_tensor(out=dl[:, 0:64], in0=dl[:, 0:64], in1=dl[:, 64:128], op=ALU.mult), reads=[B_misc], writes=[B_misc])
    k.op(DVE, lambda e: e.tensor_tensor(out=dl[:, 128:192], in0=dl[:, 128:192], in1=dl[:, 192:256], op=ALU.mult), reads=[B_misc], writes=[B_misc])
    k.op(ACT, lambda e: e.activation(out=dl[:, 64:128], in_=dl[:, 0:64], func=AF.Identity, accum_out=lamv[:, 0:1]), reads=[B_misc], writes=[B_misc])
    k.op(ACT, lambda e: e.activation(out=dl[:, 192:256], in_=dl[:, 128:192], func=AF.Identity, accum_out=lamv[:, 1:2]), reads=[B_misc], writes=[B_misc])
    k.op(ACT, lambda e: e.activation(out=lamv[:, 2:4], in_=lamv[:, 0:2], func=AF.Exp), reads=[B_misc], writes=[B_misc])
    k.op(DVE, lambda e: e.tensor_tensor(out=lamv[:, 4:5], in0=lamv[:, 3:4], in1=lamv[:, 2:3], op=ALU.subtract), reads=[B_misc], writes=[B_misc])
    k.op(DVE, lambda e: e.tensor_scalar(out=lamv[:, 4:5], in0=lamv[:, 4:5], scalar1=-LAM_INIT, scalar2=None, op0=ALU.add), reads=[B_misc], writes=[B_misc])
    k.op(DVE, lambda e: e.tensor_scalar(out=subg8[:], in0=subg8[:], scalar1=1.0 - LAM_INIT, scalar2=None, op0=ALU.mult), reads=[B_misc], writes=[B_misc])

    X1_d = nc.dram_tensor("X1s", [17 * 128, D], F32).ap()
    XR_d = nc.dram_tensor("XRs", [128, NKEY], F32).ap()
    WB_d = nc.dram_tensor("WBs", [3, NJ, 128, 2048], BF16).ap()
    es_mix = ExitStack()
    QT = es_mix.enter_context(nc.sbuf_tensor("sb_QT", [128, 8, EXTP], BF16))
    GZ = es_mix.enter_context(nc.sbuf_tensor("sb_GZ", [128, 8, EXTP], BF16))
    AT = es_mix.enter_context(nc.sbuf_tensor("sb_AT", [128, 8, EXTP], BF16))
    with ExitStack() as es:
        def sc(name, shape, dt):
            return es.enter_context(nc.sbuf_tensor("sb_" + name, shape, dt))
        wqz = sc("wqz", [128, 16, 1024], BF16)
        wst0_ = sc("b_wst0", [128, 1024], F32)
        wst = [wst0_, wst0_]
        xs0_ = sc("b_xs0", [128, 2, D], F32)
        xs = [xs0_, xs0_]
        hT = [sc("b_hT0", [128, 16, TT], BF16), sc("b_hT1", [128, 16, TT], BF16)]
        cs = [sc("b_cs0", [128, 2, TT], F32), sc("b_cs1", [128, 2, TT], F32)]
        k32 = [sc("b_k32a", [128, TT], F32), sc("b_k32b", [128, TT], F32)]
        t1 = [sc("b_t1a", [128, TT], F32), sc("b_t1b", [128, TT], F32)]
        t2 = [sc("b_t2a", [128, TT], F32), sc("b_t2b", [128, TT], F32)]
        Bwst, Bxs, BhT, Bcs, Bk32, Bt1, Bt2 = ([Buf(), Buf()] for _ in range(7))
        Bxs[1] = Bxs[0]
        Bwst[1] = Bwst[0]
        B_w = Buf("wqz")
        NOT = 9

        def load_own(ti):
            s_ = ti % 2
            nsub = 2 if ti < 8 else 1
            ncol = nsub * 128
            k.dma(xs[s_][:, 0:nsub, :], xo_d[ti * TT: ti * TT + ncol, :].rearrange("(s p) d -> p s d", p=128), Bxs[s_], writes=[Bxs[s_]])
            k.dma(cs[s_][:, 0, 0:ncol], coso_d[:, ti * TT: ti * TT + ncol], Bcs[s_], writes=[Bcs[s_]])
            k.dma(cs[s_][:, 1, 0:ncol], sino_d[:, ti * TT: ti * TT + ncol], Bcs[s_], writes=[])
            Bcs[s_].w = (Bcs[s_].dsem, Bcs[s_].dsem.n)

        go = 0
        for pas in range(2):
            c0w = 0 if pas == 0 else 4096
            for kc in range(16):
                k.dma(wst[0][:], win_d[kc * 128:(kc + 1) * 128, c0w:c0w + 1024], Bwst[0], writes=[Bwst[0]])
                k.op(ACT, lambda e, kc=kc: e.activation(out=wqz[:, kc, :], in_=wst[0][:], func=AF.Identity), reads=[Bwst[0]], writes=[B_w])
            load_own(0)
            for ti in range(NOT):
                s_ = ti % 2
                nsub = 2 if ti < 8 else 1
                ncol = nsub * 128
                col0 = ti * TT
                for su in range(nsub):
                    norm_transpose(xs[s_][:, su, :], Bxs[s_], 128, a1, modx,
                                   lambda kc, su=su, s_=s_: hT[s_][:, kc, su * 128:(su + 1) * 128], BhT[s_])
                if ti + 1 < NOT:
                    load_own(ti + 1)
                for h in range(8 if pas == 0 else 0):
                    oslot = go % 2
                    go += 1
                    ob = banks[oslot][:, 0:ncol]
                    Bo = Bbank[oslot]
                    k.pe([lambda e, kc=kc, h=h, ob=ob: e.matmul(ob, lhsT=wqz[:, kc, h * 128:(h + 1) * 128], rhs=hT[s_][:, kc, 0:ncol],
                                                                start=(kc == 0), stop=(kc == 15)) for kc in range(16)],
                         reads=[B_w, BhT[s_]], writes=[Bo])
                    q_ = h % 2
                    k.op(ACT, lambda e, ob=ob, q_=q_: e.activation(out=k32[q_][:, 0:ncol], in_=ob, func=AF.Identity), reads=[Bo], writes=[Bk32[q_]])
                    pb = banks[2][:, 0:ncol]
                    k.pe([lambda e, pb=pb, q_=q_: e.matmul(pb, lhsT=permT[:], rhs=k32[q_][:, 0:ncol], start=True, stop=True)],
                         reads=[Bk32[q_], B_const], writes=[Bbank[2]])
                    k.op(DVE, lambda e, pb=pb, q_=q_: e.tensor_tensor(out=t1[q_][:, 0:ncol], in0=pb, in1=cs[s_][:, 1, 0:ncol], op=ALU.mult),
                         reads=[Bbank[2], Bcs[s_]], writes=[Bt1[q_]])
                    k.op(POOL, lambda e, q_=q_: e.tensor_tensor(out=t2[q_][:, 0:ncol], in0=k32[q_][:, 0:ncol], in1=cs[s_][:, 0, 0:ncol], op=ALU.mult),
                         reads=[Bk32[q_], Bcs[s_]], writes=[Bt2[q_]])
                    k.op(POOL, lambda e, q_=q_, h=h: e.tensor_tensor(out=QT[:, h, col0:col0 + ncol], in0=t1[q_][:, 0:ncol], in1=t2[q_][:, 0:ncol], op=ALU.add),
                         reads=[Bt1[q_], Bt2[q_]], writes=[B_QT])
                for b in range(8 if pas == 1 else 0):
                    oslot = go % 2
                    go += 1
                    ob = banks[oslot][:, 0:ncol]
                    Bo = Bbank[oslot]
                    k.pe([lambda e, kc=kc, b=b, ob=ob: e.matmul(ob, lhsT=wqz[:, kc, b * 128:(b + 1) * 128], rhs=hT[s_][:, kc, 0:ncol],
                                                                start=(kc == 0), stop=(kc == 15)) for kc in range(16)],
                         reads=[B_w, BhT[s_]], writes=[Bo])
                    k.op(ACT, lambda e, ob=ob, b=b: e.activation(out=GZ[:, b, col0:col0 + ncol], in_=ob, func=AF.Gelu_apprx_tanh),
                         reads=[Bo], writes=[B_GZ])
        bar = []
        for b_ in Bwst + Bxs + BhT + Bcs + Bk32 + Bt1 + Bt2 + [B_w, B_QT, B_GZ] + Bbank + Bxn + Bssq + [B_junk]:
            bar.append(b_.w)
            bar += b_.rtoks()
    for E in (PE, ACT, DVE, POOL, SP):
        E.wait(bar)

    with ExitStack() as es:
        def sc(name, shape, dt):
            return es.enter_context(nc.sbuf_tensor("sb_" + name, shape, dt))
        Kh = sc("Kh", [128, NKEY], BF16)
        Vh = sc("Vh", [128, NKT, 129], BF16)
        pt = [sc("pt%d" % i, [128, 2, 512], BF16) for i in range(3)]
        Bpt = [Buf() for _ in range(3)]
        o32 = [sc("o32_%d" % i, [128, 128], F32) for i in range(2)]
        Bo32 = [Buf(), Buf()]
        onb = [sc("onb%d" % i, [128, 128], BF16) for i in range(2)]
        Bonb = [Buf(), Buf()]
        r4 = [sc("r4_%d" % i, [128, 8], F32) for i in range(2)]
        Br4 = [Buf(), Buf()]
        B_K, B_V = Buf("Kh"), Buf("Vh")
        CKT = 26
        NCH = (NKT + CKT - 1) // CKT
        B_Kc = [Buf() for _ in range(NCH)]
        B_Vc = [Buf() for _ in range(NCH)]
        pc_f = sc("pc_f", [128, 1024], F32)
        pc_b = sc("pc_b", [128, 1024], BF16)
        B_pcf, B_pcb = Buf(), Buf()
        pc = {"i": 0}
        wsrc = [wup_d, wgate_d, wdown_d]

        def precast_step():
            i_ = pc["i"]
            if i_ >= 3 * NJ * 2:
                return
            pc["i"] += 1
            wi_, j_, hf_ = i_ // (NJ * 2), (i_ // 2) % NJ, i_ % 2
            k.dma(pc_f[:], wsrc[wi_][j_, :, hf_ * 1024:(hf_ + 1) * 1024], B_pcf, writes=[B_pcf])
            k.op(POOL, lambda e: e.tensor_copy(pc_b[:], pc_f[:]), reads=[B_pcf], writes=[B_pcb])
            k.dma(WB_d[wi_, j_, :, hf_ * 1024:(hf_ + 1) * 1024], pc_b[:], B_pcb, reads=[B_pcb])
        nheads = 8 if stop_after > 3 else 1
        qblocks = [(q0, 512) for q0 in range(0, 2048, 512)] + [(2048, 2)]
        if stop_after == 3 and debug:
            qblocks = [(0, 512), (2048, 2)]
        k.op(DVE, lambda e: e.memset(AT[:, :, EXT:EXTP], 0.0), writes=[B_AT])
        it = 0
        fz = 0
        for h in range(nheads):
            for cki in range(NCH):
                k0_, k1_ = cki * CKT, min(NKT, (cki + 1) * CKT)
                k.dma(Kh[:, k0_ * 128:k1_ * 128], KT_d[h, :, k0_ * 128:k1_ * 128], B_Kc[cki], writes=[B_Kc[cki]])
                k.dma(Vh[:, k0_:k1_, :], VV_d[h, :, k0_:k1_, :], B_Vc[cki], writes=[B_Vc[cki]])
            for (q0, nq) in qblocks:
                nqs = (nq + 127) // 128
                rows = min(128, nq)
                for _ in range(7):
                    precast_step()
                def emit_s(kt, it_):
                    sbuf_i = it_ % 2
                    r = it_ % 3
                    b0 = 2 * sbuf_i
                    k.pe([lambda e, c=c, kt=kt, b0=b0: e.matmul(banks[b0 + c][:, 0:nq], lhsT=Kh[c * 64:(c + 1) * 64, kt * 128:(kt + 1) * 128],
                                                               rhs=QT[c * 64:(c + 1) * 64, h, q0:q0 + nq], start=True, stop=True) for c in range(2)],
                         reads=[B_Kc[kt // CKT], B_QT], writes=[Bbank[b0], Bbank[b0 + 1]])
                    sview = psall[:, b0 * 512:(b0 + 2) * 512].rearrange("p (c n) -> p c n", c=2)[:, :, 0:nq]
                    k.op(ACT, lambda e, sview=sview, r=r: e.activation(out=pt[r][:, :, 0:nq], in_=sview, func=AF.Exp, scale=0.125),
                         reads=[Bbank[b0], Bbank[b0 + 1]], writes=[Bpt[r]])

                def emit_pv(kt, it_):
                    r = it_ % 3
                    fns = []
                    accs = []
                    for c in range(2):
                        for qs in range(nqs):
                            ab = 4 + 2 * c + qs // 2
                            co = (qs % 2) * 256
                            if Bbank[ab] not in accs:
                                accs.append(Bbank[ab])
                            fns.append(lambda e, c=c, qs=qs, ab=ab, co=co, kt=kt, r=r: e.matmul(
                                banks[ab][0:rows, co:co + 129], lhsT=pt[r][:, c, qs * 128:qs * 128 + rows], rhs=Vh[:, kt, :],
                                start=(kt == 0 and qs % 2 == 0), stop=(kt == NKT - 1)))
                    k.pe(fns, reads=[Bpt[r], B_Vc[kt // CKT]], writes=accs)

                it0 = it
                for kt in range(NKT):
                    emit_s(kt, it0 + kt)
                    if kt >= 1:
                        emit_pv(kt - 1, it0 + kt - 1)
                emit_pv(NKT - 1, it0 + NKT - 1)
                it = it0 + NKT
                R_ = rows
                for qs in range(nqs):
                    f = fz % 2
                    fz += 1
                    a0 = banks[4 + qs // 2][0:R_, (qs % 2) * 256:(qs % 2) * 256 + 129]
                    a1_ = banks[6 + qs // 2][0:R_, (qs % 2) * 256:(qs % 2) * 256 + 129]
                    k.op(DVE, lambda e, f=f, a0=a0: e.reciprocal(out=r4[f][0:R_, 0:1], in_=a0[:, 128:129]), reads=[Bbank[4 + qs // 2]], writes=[Br4[f]])
                    k.op(DVE, lambda e, f=f, a1_=a1_: e.reciprocal(out=r4[f][0:R_, 1:2], in_=a1_[:, 128:129]), reads=[Bbank[6 + qs // 2]], writes=[Br4[f]])
                    k.op(DVE, lambda e, f=f: e.tensor_tensor(out=r4[f][0:R_, 2:3], in0=r4[f][0:R_, 1:2], in1=lamv[0:R_, 4:5], op=ALU.mult),
                         reads=[Br4[f], B_misc], writes=[Br4[f]])
                    k.op(DVE, lambda e, f=f, a0=a0: e.tensor_scalar(out=o32[f][0:R_, :], in0=a0[:, 0:128], scalar1=r4[f][0:R_, 0:1], scalar2=None, op0=ALU.mult),
                         reads=[Bbank[4 + qs // 2], Br4[f]], writes=[Bo32[f]])
                    k.op(DVE, lambda e, f=f, a1_=a1_: e.scalar_tensor_tensor(out=o32[f][0:R_, :], in0=a1_[:, 0:128], scalar=r4[f][0:R_, 2:3], in1=o32[f][0:R_, :],
                                                                           op0=ALU.mult, op1=ALU.add),
                         reads=[Bbank[6 + qs // 2], Br4[f], Bo32[f]], writes=[Bo32[f]])
                    k.op(ACT, lambda e, f=f: e.activation(out=junk[0:R_, 0:128], in_=o32[f][0:R_, :], func=AF.Square, accum_out=r4[f][0:R_, 3:4]),
                         reads=[Bo32[f]], writes=[B_junk, Br4[f]])
                    k.op(ACT, lambda e, f=f: e.activation(out=r4[f][0:R_, 4:5], in_=r4[f][0:R_, 3:4], func=AF.Ln, scale=1.0 / 128, bias=epsc[0:R_, 1:2]),
                         reads=[Br4[f], B_const], writes=[Br4[f]])
                    k.op(ACT, lambda e, f=f: e.activation(out=r4[f][0:R_, 5:6], in_=r4[f][0:R_, 4:5], func=AF.Exp, scale=-0.5),
                         reads=[Br4[f]], writes=[Br4[f]])
                    k.op(DVE, lambda e, f=f: e.scalar_tensor_tensor(out=onb[f][0:R_, :], in0=o32[f][0:R_, :], scalar=r4[f][0:R_, 5:6], in1=subg8[0:R_, :],
                                                                   op0=ALU.mult, op1=ALU.mult),
                         reads=[Bo32[f], Br4[f], B_misc], writes=[Bonb[f]])
                    tb = banks[0].bitcast(BF16)
                    k.pe([lambda e, f=f, tb=tb: e.transpose(tb[:, 0:R_], onb[f][0:R_, :], identb[0:R_, 0:R_])], reads=[Bonb[f], B_const], writes=[Bbank[0]])
                    k.op(ACT, lambda e, tb=tb, qs=qs: e.activation(out=AT[:, h, q0 + qs * 128: q0 + qs * 128 + R_], in_=tb[:, 0:R_], func=AF.Identity),
                         reads=[Bbank[0]], writes=[B_AT])
        while pc["i"] < 3 * NJ * 2:
            precast_step()
        bar = []
        for b_ in [B_K, B_V, B_QT, B_AT, B_pcf, B_pcb] + B_Kc + B_Vc + Bpt + Bo32 + Bonb + Br4 + Bbank + [B_junk]:
            bar.append(b_.w)
            bar += b_.rtoks()
            if b_.dsem is not None:
                bar.append((b_.dsem, b_.dsem.n))
    for E in (PE, ACT, DVE, POOL, SP):
        E.wait(bar)
    if debug and stop_after == 3:
        Bd = Buf()
        k.dma(dbg["AT"][:, :], AT[:, 0, :], Bd, reads=[B_AT])
        k.dma(dbg["QT"][:, :], QT[:, 0, :], Bd, reads=[B_QT])
        k.dma(dbg["GZ"][:, :], GZ[:, 0, :], Bd, reads=[B_GZ])
        SP.wait([(Bd.dsem, Bd.dsem.n)])
        return nc


    RT, B_RT = QT, B_QT
    k.op(POOL, lambda e: e.memset(RT[:, :, EXT:EXTP], 0.0), writes=[B_RT])
    TN = 1024
    with ExitStack() as es:
        def sc(name, shape, dt):
            return es.enter_context(nc.sbuf_tensor("sb_" + name, shape, dt))
        rgwb = sc("rgwb", [128, 32 * 128], BF16)
        rgst = sc("rgst", [128, 1024], F32)
        cst = sc("cst", [128, 72], F32)
        xt2 = [sc("r_xt%d" % i, [128, TN + 4], F32) for i in range(2)]
        xr2 = [sc("r_xr%d" % i, [128, TN], F32) for i in range(2)]
        xrb2 = [sc("r_xrb%d" % i, [128, TN], BF16) for i in range(2)]
        EA2 = [sc("r_EA%d" % i, [128, TN], F32) for i in range(2)]
        EI2 = [sc("r_EI%d" % i, [128, TN], F32) for i in range(2)]
        A_2 = [sc("r_A%d" % i, [128, TN], F32) for i in range(2)]
        A22 = [sc("r_A2%d" % i, [128, TN], F32) for i in range(2)]
        H2 = [sc("r_H%d" % i, [128, TN], F32) for i in range(2)]
        Hacc = sc("r_Hacc", [128, 2052], F32)
        hcar = sc("r_hcar", [128, 2], F32)
        B_rgw, B_rgst, B_cst, B_Hacc, B_hcar = (Buf() for _ in range(5))
        Bxt2, Bxr2, Bxrb2, BEA2, BEI2, BA2, BA22, BH2 = ([Buf(), Buf()] for _ in range(8))
        for q in range(4):
            k.dma(rgst[:], rgw_d[:, q * 1024:(q + 1) * 1024], B_rgst, writes=[B_rgst])
            k.op(ACT, lambda e, q=q: e.activation(out=rgwb[:, q * 1024:(q + 1) * 1024], in_=rgst[:], func=AF.Identity), reads=[B_rgst], writes=[B_rgw])
        k.op(DVE, lambda e: e.memset(cst[:, 64:65], 1.0), writes=[B_cst])
        k.op(ACT, lambda e: e.activation(out=cst[:, 0:16], in_=vecs[:, V_RLAM:V_RLAM + 16], func=AF.Exp, scale=-1.0), reads=[B_vecs], writes=[B_cst])
        k.op(ACT, lambda e: e.activation(out=cst[:, 0:16], in_=cst[:, 0:16], func=AF.Ln, scale=1.0, bias=cst[:, 64:65]), reads=[B_cst], writes=[B_cst])
        k.op(DVE, lambda e: e.tensor_scalar(out=cst[:, 16:32], in0=cst[:, 0:16], scalar1=-16.0, scalar2=None, op0=ALU.mult), reads=[B_cst], writes=[B_cst])
        k.op(DVE, lambda e: e.tensor_scalar(out=cst[:, 0:16], in0=cst[:, 0:16], scalar1=-8.0, scalar2=None, op0=ALU.mult), reads=[B_cst], writes=[B_cst])
        k.op(DVE, lambda e: e.tensor_scalar(out=cst[:, 32:48], in0=vecs[:, V_RBA:V_RBA + 16], scalar1=-1.0, scalar2=None, op0=ALU.mult), reads=[B_vecs], writes=[B_cst])
        k.op(DVE, lambda e: e.tensor_scalar(out=cst[:, 48:64], in0=vecs[:, V_RBI:V_RBI + 16], scalar1=-1.0, scalar2=None, op0=ALU.mult), reads=[B_vecs], writes=[B_cst])
        nblocks = 8 if stop_after > 2 else 1
        gs = 0
        tc_ = 0
        B_XRd = [Buf() for _ in range(1 + S // TN)]
        for b in range(nblocks):
            k.op(DVE, lambda e: e.memset(Hacc[:], 0.0), writes=[B_Hacc])
            for d in range(2):
                idx = d * 8 + b
                wa = rgwb[:, ((0 * 2 + d) * 8 + b) * 128:((0 * 2 + d) * 8 + b + 1) * 128]
                wi = rgwb[:, ((1 * 2 + d) * 8 + b) * 128:((1 * 2 + d) * 8 + b + 1) * 128]
                k.op(DVE, lambda e: e.memset(hcar[:, 0:1], 0.0), writes=[B_hcar])
                nlt = S // TN
                tiles = [("c", 0)] + [("l", t) for t in (range(nlt) if d == 0 else range(nlt - 1, -1, -1))]
                def tile_geom(kind, t):
                    n = 256 if kind == "c" else TN
                    seqlen = NCTX if kind == "c" else S
                    src = RXC_d if kind == "c" else RXL_d
                    t0 = t * TN
                    lo, hi = max(0, t0 - 2), min(seqlen, t0 + n + 1)
                    xslot = 0 if kind == "c" else 1 + t
                    xoff = 0 if kind == "c" else NCTX + t0
                    return n, src, t0, lo, hi, xslot, xoff

                def emit_load(kind, t, u):
                    n, src, t0, lo, hi, xslot, xoff = tile_geom(kind, t)
                    if d == 0:
                        xt = xt2[u]
                        k.op(DVE, lambda e, xt=xt: e.memset(xt[:, 0:2], 0.0), writes=[Bxt2[u]])
                        k.op(DVE, lambda e, xt=xt, n=n: e.memset(xt[:, n + 2:n + 3], 0.0), writes=[Bxt2[u]])
                        k.dma(xt[:, lo - t0 + 2: hi - t0 + 2], src[b, :, lo:hi], Bxt2[u], writes=[Bxt2[u]])
                    else:
                        k.dma(xr2[u][:, 0:n], XR_d[:, xoff:xoff + n], Bxr2[u], reads=[B_XRd[xslot]], writes=[Bxr2[u]])

                def stage_a(kind, t, u):
                    emit_load(kind, t, u)
                    n, src, t0, lo, hi, xslot, xoff = tile_geom(kind, t)
                    if d == 0:
                        xt, xr = xt2[u], xr2[u]
                        w_ = lambda j: vecs[:, V_RCW + j * 8 + b: V_RCW + j * 8 + b + 1]
                        k.op(ACT, lambda e, n=n, xt=xt, xr=xr: e.activation(out=xr[:, 0:n], in_=xt[:, 0:n], func=AF.Identity, scale=w_(0),
                                                                           bias=vecs[:, V_RCB + b:V_RCB + b + 1]),
                             reads=[Bxt2[u], B_vecs], writes=[Bxr2[u]])
                        for j in range(1, 4):
                            k.op(DVE, lambda e, n=n, j=j, xt=xt, xr=xr: e.scalar_tensor_tensor(out=xr[:, 0:n], in0=xt[:, j:j + n], scalar=w_(j), in1=xr[:, 0:n],
                                                                                              op0=ALU.mult, op1=ALU.add),
                                 reads=[Bxt2[u], B_vecs, Bxr2[u]], writes=[Bxr2[u]])

                stage_a(tiles[0][0], tiles[0][1], tc_ % 2)
                for ti_, (kind, t) in enumerate(tiles):
                    u = tc_ % 2
                    tc_ += 1
                    xt, xr, xrb, EA, EI, A_, A2, H = xt2[u], xr2[u], xrb2[u], EA2[u], EI2[u], A_2[u], A22[u], H2[u]
                    B_xt, B_xr, B_xrb, B_EA, B_EI, B_A, B_A2, B_H = Bxt2[u], Bxr2[u], Bxrb2[u], BEA2[u], BEI2[u], BA2[u], BA22[u], BH2[u]
                    n, src, t0, lo, hi, xslot, xoff = tile_geom(kind, t)
                    if ti_ + 1 < len(tiles):
                        stage_a(tiles[ti_ + 1][0], tiles[ti_ + 1][1], tc_ % 2)
                    if d == 0:
                        k.dma(XR_d[:, xoff:xoff + n], xr[:, 0:n], B_xr, reads=[B_xr], writes=[B_XRd[xslot]])
                    k.op(ACT, lambda e, n=n, xr=xr, xrb=xrb: e.activation(out=xrb[:, 0:n], in_=xr[:, 0:n], func=AF.Identity), reads=[B_xr], writes=[B_xrb])
                    base = 4 * (gs % 2)
                    gs += 1
                    for (w_g, off, dstE, Bd_, nb_) in ((wa, 0, EA, B_EA, cst[:, 32 + idx:33 + idx]), (wi, 2, EI, B_EI, cst[:, 48 + idx:49 + idx])):
                        fns = []
                        for m0 in range(0, n, 512):
                            mw = min(512, n - m0)
                            fns.append(lambda e, w_g=w_g, m0=m0, mw=mw, off=off, base=base, xrb=xrb: e.matmul(
                                psall[:, (base + off) * 512 + m0:(base + off) * 512 + m0 + mw], lhsT=w_g, rhs=xrb[:, m0:m0 + mw],
                                start=True, stop=True))
                        k.pe(fns, reads=[B_rgw, B_xrb], writes=[Bbank[base + off], Bbank[base + off + 1]])
                        k.op(ACT, lambda e, off=off, base=base, n=n, dstE=dstE, nb_=nb_: e.activation(
                            out=dstE[:, 0:n], in_=psall[:, (base + off) * 512:(base + off) * 512 + n], func=AF.Exp, scale=-1.0, bias=nb_),
                            reads=[Bbank[base + off], Bbank[base + off + 1], B_cst], writes=[Bd_])
                    k.op(ACT, lambda e, n=n, EA=EA: e.activation(out=EA[:, 0:n], in_=EA[:, 0:n], func=AF.Ln, scale=1.0, bias=cst[:, 64:65]), reads=[B_EA, B_cst], writes=[B_EA])
                    k.op(ACT, lambda e, n=n, EA=EA: e.activation(out=EA[:, 0:n], in_=EA[:, 0:n], func=AF.Exp, scale=-1.0), reads=[B_EA], writes=[B_EA])
                    k.op(ACT, lambda e, n=n, EI=EI: e.activation(out=EI[:, 0:n], in_=EI[:, 0:n], func=AF.Ln, scale=1.0, bias=cst[:, 64:65]), reads=[B_EI, B_cst], writes=[B_EI])
                    k.op(ACT, lambda e, n=n, EI=EI: e.activation(out=EI[:, 0:n], in_=EI[:, 0:n], func=AF.Exp, scale=-1.0), reads=[B_EI], writes=[B_EI])
                    k.op(ACT, lambda e, n=n, A_=A_, EA=EA: e.activation(out=A_[:, 0:n], in_=EA[:, 0:n], func=AF.Exp, scale=cst[:, idx:idx + 1]), reads=[B_EA, B_cst], writes=[B_A])
                    k.op(ACT, lambda e, n=n, A2=A2, EA=EA: e.activation(out=A2[:, 0:n], in_=EA[:, 0:n], func=AF.Exp, scale=cst[:, 16 + idx:17 + idx]), reads=[B_EA, B_cst], writes=[B_A2])
                    k.op(ACT, lambda e, n=n, A2=A2: e.activation(out=A2[:, 0:n], in_=A2[:, 0:n], func=AF.Ln, scale=-1.0, bias=cst[:, 64:65]), reads=[B_A2, B_cst], writes=[B_A2])
                    k.op(ACT, lambda e, n=n, A2=A2: e.activation(out=A2[:, 0:n], in_=A2[:, 0:n], func=AF.Exp, scale=0.5), reads=[B_A2], writes=[B_A2])
                    k.op(DVE, lambda e, n=n, EI=EI, xr=xr: e.tensor_tensor(out=EI[:, 0:n], in0=EI[:, 0:n], in1=xr[:, 0:n], op=ALU.mult), reads=[B_EI, B_xr], writes=[B_EI])
                    k.op(DVE, lambda e, n=n, EI=EI, A2=A2: e.tensor_tensor(out=EI[:, 0:n], in0=EI[:, 0:n], in1=A2[:, 0:n], op=ALU.mult), reads=[B_EI, B_A2], writes=[B_EI])
                    if d == 0:
                        k.op(DVE, lambda e, n=n, H=H, A_=A_, EI=EI: e.tensor_tensor_scan(out=H[:, 0:n], data0=A_[:, 0:n], data1=EI[:, 0:n], initial=hcar[:, 0:1],
                                                                                      op0=ALU.mult, op1=ALU.add), reads=[B_A, B_EI, B_hcar], writes=[B_H])
                        k.op(DVE, lambda e, n=n, H=H: e.tensor_copy(hcar[:, 0:1], H[:, n - 1:n]), reads=[B_H], writes=[B_hcar])
                    else:
                        k.op(DVE, lambda e, n=n, H=H, A_=A_, EI=EI: e.tensor_tensor_scan(out=H[:, 0:n][:, ::-1], data0=A_[:, 0:n][:, ::-1], data1=EI[:, 0:n][:, ::-1],
                                                                                      initial=hcar[:, 0:1], op0=ALU.mult, op1=ALU.add), reads=[B_A, B_EI, B_hcar], writes=[B_H])
                        k.op(DVE, lambda e, H=H: e.tensor_copy(hcar[:, 0:1], H[:, 0:1]), reads=[B_H], writes=[B_hcar])
                    if kind == "l":
                        per = 2048 // TN
                        tb_, hf_ = t // per, t % per
                        k.op(DVE, lambda e, tb_=tb_, hf_=hf_, H=H: e.scalar_tensor_tensor(out=Hacc[:, hf_ * TN:(hf_ + 1) * TN], in0=H[:, 0:TN], scalar=mk[:, tb_:tb_ + 1],
                                                                                         in1=Hacc[:, hf_ * TN:(hf_ + 1) * TN], op0=ALU.mult, op1=ALU.add),
                             reads=[B_H, B_misc, B_Hacc], writes=[B_Hacc])
                        if hf_ == per - 1:
                            k.op(DVE, lambda e, tb_=tb_, H=H: e.scalar_tensor_tensor(out=Hacc[:, 2048:2049], in0=H[:, TN - 1:TN], scalar=mk[:, 8 + tb_:9 + tb_], in1=Hacc[:, 2048:2049],
                                                                                   op0=ALU.mult, op1=ALU.add), reads=[B_H, B_misc, B_Hacc], writes=[B_Hacc])
                        if hf_ == 0:
                            k.op(DVE, lambda e, tb_=tb_, H=H: e.scalar_tensor_tensor(out=Hacc[:, 2049:2050], in0=H[:, 0:1], scalar=mk[:, 16 + tb_:17 + tb_], in1=Hacc[:, 2049:2050],
                                                                                   op0=ALU.mult, op1=ALU.add), reads=[B_H, B_misc, B_Hacc], writes=[B_Hacc])
            k.op(DVE, lambda e, b=b: e.tensor_tensor(out=RT[:, b, 0:EXT], in0=Hacc[:, 0:EXT], in1=GZ[:, b, 0:EXT], op=ALU.mult),
                 reads=[B_Hacc, B_GZ], writes=[B_RT])
        bar = []
        for b_ in [B_rgw, B_rgst, B_cst, B_Hacc, B_hcar, B_RT, B_GZ] + Bxt2 + Bxr2 + Bxrb2 + BEA2 + BEI2 + BA2 + BA22 + BH2 + Bbank:
            bar.append(b_.w)
            bar += b_.rtoks()
            if b_.dsem is not None:
                bar.append((b_.dsem, b_.dsem.n))
    for E in (PE, ACT, DVE, POOL, SP):
        E.wait(bar)
    if debug and stop_after == 2:
        Bd = Buf()
        k.dma(dbg["QT"][:, :], RT[:, 0, :], Bd, reads=[B_RT])
        SP.wait([(Bd.dsem, Bd.dsem.n)])
        return nc


    gz32 = GZ[:].rearrange("p a b -> p (a b)").bitcast(F32)
    x1t = gz32[:, 0:2048]
    xst = gz32[:, 2048:4096]
    g1b = gz32[:, 4096:6144]
    dgt = gz32[:, 6144:6272]
    B_x1t, B_xst, B_g1b, B_dg, B_X1 = Buf(), Buf(), Buf(), Buf(), Buf()

    def row_broadcast(dst, Bdst, col0):
        for q in range(4):
            for kc4 in range(4):
                kc = q * 4 + kc4
                k.op(DVE, lambda e, kc=kc: e.tensor_scalar(out=dgt, in0=identf[:], scalar1=modx[:, col0 + kc:col0 + kc + 1], scalar2=None, op0=ALU.mult),
                     reads=[B_const, B_mod], writes=[B_dg])
                k.pe([lambda e, kc4=kc4: e.matmul(banks[0][:, kc4 * 128:(kc4 + 1) * 128], lhsT=ones_f[:], rhs=dgt, start=True, stop=True)],
                     reads=[B_dg, B_const], writes=[Bbank[0]])
            k.op(ACT, lambda e, q=q: e.activation(out=dst[:, q * 512:(q + 1) * 512], in_=banks[0], func=AF.Identity), reads=[Bbank[0]], writes=[Bdst])

    row_broadcast(g1b, B_g1b, 32)
    with ExitStack() as es:
        def sc(name, shape, dt):
            return es.enter_context(nc.sbuf_tensor("sb_" + name, shape, dt))
        wo = sc("wo", [128, 16, D], BF16)
        B_wo = Buf()
        for ch in range(16):
            for hf in range(2):
                k.dma(xst[:, 0:1024], wout_d[ch * 128:(ch + 1) * 128, hf * 1024:(hf + 1) * 1024], B_xst, writes=[B_xst])
                k.op(ACT, lambda e, ch=ch, hf=hf: e.activation(out=wo[:, ch, hf * 1024:(hf + 1) * 1024], in_=xst[:, 0:1024], func=AF.Identity), reads=[B_xst], writes=[B_wo])
        for ts in range(17):
            k.dma(xst, xo_d[ts * 128:(ts + 1) * 128, :], B_xst, writes=[B_xst])
            for nb in range(4):
                k.pe([lambda e, ch=ch, nb=nb, ts=ts: e.matmul(banks[nb], lhsT=(AT if ch < 8 else RT)[:, ch % 8, ts * 128:(ts + 1) * 128],
                                                              rhs=wo[:, ch, nb * 512:(nb + 1) * 512], start=(ch == 0), stop=(ch == 15)) for ch in range(16)],
                     reads=[B_AT, B_RT, B_wo], writes=[Bbank[nb]])
            k.op(DVE, lambda e: e.tensor_tensor(out=x1t, in0=psall[:, 0:2048], in1=g1b, op=ALU.mult),
                 reads=[Bbank[0], Bbank[1], Bbank[2], Bbank[3], B_g1b], writes=[B_x1t])
            k.op(DVE, lambda e: e.tensor_tensor(out=x1t, in0=x1t, in1=xst, op=ALU.add), reads=[B_x1t, B_xst], writes=[B_x1t])
            k.dma(X1_d[ts * 128:(ts + 1) * 128, :], x1t, B_x1t, reads=[B_x1t], writes=[B_X1])
        bar = []
        for b_ in [B_wo, B_x1t, B_xst, B_g1b, B_dg, B_AT, B_RT, B_X1] + Bbank:
            bar.append(b_.w)
            bar += b_.rtoks()
            if b_.dsem is not None:
                bar.append((b_.dsem, b_.dsem.n))
    for E in (PE, ACT, DVE, POOL, SP):
        E.wait(bar)
    es_mix.close()
    if debug and stop_after == 4:
        return nc

    with ExitStack() as es:
        def sc(name, shape, dt):
            return es.enter_context(nc.sbuf_tensor("sb_" + name, shape, dt))
        h2T = sc("h2T", [128, 16, EXT], BF16)
        actT = sc("actT", [128, NJ, 512], BF16)
        hTh = actT[:, 0:4, :].rearrange("p a (b c) -> p (a b) c", c=128)
        x1t = sc("f_x1t", [128, D], F32)
        x2t = sc("f_x2t", [128, D], F32)
        g2b = sc("f_g2b", [128, D], F32)
        fgb = sc("f_fgb", [128, D], F32)
        wug = [sc("f_wug%d" % i, [128, 2, D], BF16) for i in range(3)]
        wdb = [sc("f_wd%d" % i, [128, D], BF16) for i in range(3)]
        cvt = sc("f_cvt", [128, 512], F32)
        glt = sc("f_glt", [128, 512], F32)
        fr = sc("f_fr", [128, 4], F32)
        B_h2T, B_hTh, B_act, B_x1t, B_x2t, B_g2b, B_fgb, B_cvt, B_glt, B_fr = (Buf() for _ in range(10))
        B_hTh = B_act
        Bwst = []
        Bwug = [Buf(), Buf(), Buf()]
        Bwd = [Buf(), Buf(), Buf()]
        dgt = x2t[:, 0:128]
        B_dg = B_x2t

        def row_broadcast2(dst, Bdst, col0):
            for q in range(4):
                for kc4 in range(4):
                    kc = q * 4 + kc4
                    k.op(DVE, lambda e, kc=kc: e.tensor_scalar(out=dgt, in0=identf[:], scalar1=modx[:, col0 + kc:col0 + kc + 1], scalar2=None, op0=ALU.mult),
                         reads=[B_const, B_mod], writes=[B_dg])
                    k.pe([lambda e, kc4=kc4: e.matmul(banks[0][:, kc4 * 128:(kc4 + 1) * 128], lhsT=ones_f[:], rhs=dgt, start=True, stop=True)],
                         reads=[B_dg, B_const], writes=[Bbank[0]])
                k.op(ACT, lambda e, q=q: e.activation(out=dst[:, q * 512:(q + 1) * 512], in_=banks[0], func=AF.Identity), reads=[Bbank[0]], writes=[Bdst])

        row_broadcast2(g2b, B_g2b, 80)
        k.dma(fgb[:], fing_d[:, :], B_fgb, writes=[B_fgb])
        for ts in range(17):
            k.dma(x1t[:], X1_d[ts * 128:(ts + 1) * 128, :], B_x1t, writes=[B_x1t])
            if ts < 16:
                norm_transpose(x1t[:], B_x1t, 128, a2, modx, lambda kc, ts=ts: h2T[:, kc, 1 + ts * 128: 1 + (ts + 1) * 128], B_h2T, boff=48)
            else:
                norm_transpose(x1t[:], B_x1t, 128, a2, modx, lambda kc: hTh[:, kc, :], B_hTh, boff=48)
                k.op(POOL, lambda e: e.tensor_scalar(out=h2T[:, :, 0:1], in0=hTh[:, :, 0:1], scalar1=mk[:, 24:25], scalar2=None, op0=ALU.mult),
                     reads=[B_hTh, B_misc], writes=[B_h2T])
                k.op(POOL, lambda e: e.tensor_scalar(out=h2T[:, :, 2049:2050], in0=hTh[:, :, 1:2], scalar1=mk[:, 25:26], scalar2=None, op0=ALU.mult),
                     reads=[B_hTh, B_misc], writes=[B_h2T])
        wcnt = {"s": 0, "ug": 0, "d": 0}

        def load_cast(src_ap, dst_ap, Bdst_):
            s_ = wcnt["s"] % 2
            wcnt["s"] += 1
            k.dma(wst[s_][:], src_ap, Bwst[s_], writes=[Bwst[s_]])
            k.op(ACT, lambda e, s_=s_: e.activation(out=dst_ap, in_=wst[s_][:], func=AF.Identity), reads=[Bwst[s_]], writes=[Bdst_])

        nwin = 4 if stop_after > 5 else 1
        for w in range(nwin):
            c0 = 512 * w
            for j in range(NJ):
                u_ = wcnt["ug"] % 3
                wcnt["ug"] += 1
                k.dma(wug[u_][:, 0, :], WB_d[0, j, :, :], Bwug[u_], writes=[Bwug[u_]])
                k.dma(wug[u_][:, 1, :], WB_d[1, j, :, :], Bwug[u_], writes=[])
                Bwug[u_].w = (Bwug[u_].dsem, Bwug[u_].dsem.n)
                ub = 4 if j % 2 == 0 else 0
                gb = ub + 1
                k.pe([lambda e, kc=kc, u_=u_, ub=ub: e.matmul(banks[ub], lhsT=wug[u_][:, 0, kc * 128:(kc + 1) * 128], rhs=h2T[:, kc, c0 + 1:c0 + 513],
                                                              start=(kc == 0), stop=(kc == 15)) for kc in range(16)],
                     reads=[Bwug[u_], B_h2T], writes=[Bbank[ub]])
                k.pe([lambda e, kc=kc, u_=u_, gb=gb: e.matmul(banks[gb], lhsT=wug[u_][:, 1, kc * 128:(kc + 1) * 128], rhs=h2T[:, kc, c0:c0 + 512],
                                                              start=(kc == 0), stop=(kc == 15)) for kc in range(16)],
                     reads=[Bwug[u_], B_h2T], writes=[Bbank[gb]])
                k.pe([lambda e, kc=kc, u_=u_, gb=gb: e.matmul(banks[gb + 1][:, 0:2], lhsT=wug[u_][:, 1, kc * 128:(kc + 1) * 128], rhs=h2T[:, kc, c0 + 512:c0 + 514],
                                                              start=(kc == 0), stop=(kc == 15)) for kc in range(16)],
                     reads=[Bwug[u_], B_h2T], writes=[Bbank[gb + 1]])
                gps = psall[:, gb * 512: gb * 512 + 514]
                cw = lambda t_: vecs[:, V_FCW + t_ * 43 + j: V_FCW + t_ * 43 + j + 1]
                k.op(DVE, lambda e, gps=gps: e.tensor_scalar(out=cvt[:], in0=gps[:, 0:512], scalar1=cw(0), scalar2=None, op0=ALU.mult),
                     reads=[Bbank[gb], Bbank[gb + 1], B_vecs], writes=[B_cvt])
                k.op(DVE, lambda e, gps=gps: e.scalar_tensor_tensor(out=cvt[:], in0=gps[:, 1:513], scalar=cw(1), in1=cvt[:], op0=ALU.mult, op1=ALU.add),
                     reads=[Bbank[gb], Bbank[gb + 1], B_vecs, B_cvt], writes=[B_cvt])
                k.op(DVE, lambda e, gps=gps: e.scalar_tensor_tensor(out=cvt[:], in0=gps[:, 2:514], scalar=cw(2), in1=cvt[:], op0=ALU.mult, op1=ALU.add),
                     reads=[Bbank[gb], Bbank[gb + 1], B_vecs, B_cvt], writes=[B_cvt])
                k.op(ACT, lambda e, j=j: e.activation(out=glt[:], in_=cvt[:], func=AF.Gelu_apprx_tanh, bias=vecs[:, V_FCB + j:V_FCB + j + 1]),
                     reads=[B_cvt, B_vecs], writes=[B_glt])
                k.op(DVE, lambda e, j=j, ub=ub: e.tensor_tensor(out=actT[:, j, :], in0=banks[ub], in1=glt[:], op=ALU.mult),
                     reads=[Bbank[ub], B_glt], writes=[B_act])
            for pair in range(2):
                for j in range(NJ):
                    d_ = wcnt["d"] % 3
                    wcnt["d"] += 1
                    k.dma(wdb[d_][:], WB_d[2, j, :, :], Bwd[d_], writes=[Bwd[d_]])
                    fns = []
                    for t2_ in range(2):
                        ts4 = pair * 2 + t2_
                        for nb in range(4):
                            fns.append(lambda e, j=j, ts4=ts4, nb=nb, t2_=t2_, d_=d_: e.matmul(
                                banks[t2_ * 4 + nb], lhsT=actT[:, j, ts4 * 128:(ts4 + 1) * 128], rhs=wdb[d_][:, nb * 512:(nb + 1) * 512],
                                start=(j == 0), stop=(j == NJ - 1)))
                    k.pe(fns, reads=[B_act, Bwd[d_]], writes=Bbank)
                for t2_ in range(2):
                    ts4 = pair * 2 + t2_
                    row0 = c0 + ts4 * 128
                    k.dma(x1t[:], X1_d[row0:row0 + 128, :], B_x1t, writes=[B_x1t])
                    k.op(DVE, lambda e, t2_=t2_: e.tensor_tensor(out=x2t[:], in0=psall[:, t2_ * 2048:(t2_ + 1) * 2048], in1=g2b[:], op=ALU.mult),
                         reads=Bbank + [B_g2b], writes=[B_x2t])
                    k.op(DVE, lambda e: e.tensor_tensor(out=x2t[:], in0=x2t[:], in1=x1t[:], op=ALU.add), reads=[B_x2t, B_x1t], writes=[B_x2t])
                    k.op(ACT, lambda e: e.activation(out=junk[:, :], in_=x2t[:], func=AF.Square, accum_out=fr[:, 0:1]), reads=[B_x2t], writes=[B_junk, B_fr])
                    k.op(ACT, lambda e: e.activation(out=fr[:, 1:2], in_=fr[:, 0:1], func=AF.Ln, scale=1.0 / D, bias=epsc[:, 0:1]), reads=[B_fr, B_const], writes=[B_fr])
                    k.op(ACT, lambda e: e.activation(out=fr[:, 2:3], in_=fr[:, 1:2], func=AF.Exp, scale=-0.5), reads=[B_fr], writes=[B_fr])
                    k.op(DVE, lambda e: e.scalar_tensor_tensor(out=x1t[:], in0=x2t[:], scalar=fr[:, 2:3], in1=fgb[:], op0=ALU.mult, op1=ALU.mult),
                         reads=[B_x2t, B_fr, B_fgb, B_x1t], writes=[B_x1t])
                    k.dma(out_d[row0:row0 + 128, :], x1t[:], B_x1t, reads=[B_x1t])
        fin = [(B_x1t.dsem, B_x1t.dsem.n)]
        SP.wait(fin)
        bar = []
        for b_ in [B_h2T, B_hTh, B_act, B_x1t, B_x2t, B_g2b, B_fgb, B_cvt, B_glt, B_fr] + Bwst + Bwug + Bwd + Bbank:
            bar.append(b_.w)
            bar += b_.rtoks()
    for E in (PE, ACT, DVE, POOL, SP):
        E.wait(bar)
    return nc


def rope_tables(tok):
    inv = (10000.0 ** (-np.arange(16, dtype=np.float32) / 16)).astype(np.float32)
    tok = np.asarray(tok)
    row = (tok // 64).astype(np.float32)
    col = (tok % 64).astype(np.float32)
    ang = np.stack([row[None, :] * inv[:, None], col[None, :] * inv[:, None]], 0).astype(np.float32)
    cos = np.cos(ang).astype(np.float32)
    sin = np.sin(ang).astype(np.float32)
    C = np.zeros((128, len(tok)), np.float32)
    Sn = np.zeros((128, len(tok)), np.float32)
    for c in range(2):
        for ax in range(2):
            for half in range(2):
                p0 = c * 64 + ax * 32 + half * 16
                C[p0:p0 + 16] = cos[ax]
                Sn[p0:p0 + 16] = sin[ax] * (-1.0 if half == 0 else 1.0)
    return C, Sn


def pcl(v):
    v = np.asarray(v, np.float32)
    return np.ascontiguousarray(v.reshape(-1, 128).T)


def host_inputs(inp):
    f32 = np.float32
    x = np.ascontiguousarray(inp["x"][0], f32)
    ctx = np.ascontiguousarray(inp["ctx"][0], f32)
    shared = {}
    shared["x"] = x
    shared["ctx"] = ctx
    cv = np.stack([pcl(inp["c"][0]), pcl(inp["c_ctx"])], -1)
    shared["cvec"] = np.ascontiguousarray(cv.reshape(128, 32))
    vecs = np.zeros((128, NV), f32)
    vecs[:, V_BADA:V_BADA + 96] = pcl(inp["b_ada"][0])
    vecs[:, V_N1G:V_N1G + 16] = pcl(inp["norm1_g"][0])
    vecs[:, V_N2G:V_N2G + 16] = pcl(inp["norm2_g"][0])
    for j in range(3):
        vecs[:, V_FCW + j * 43:V_FCW + (j + 1) * 43] = pcl(inp["ffn_conv_w"][0, j])
    vecs[:, V_FCB:V_FCB + 43] = pcl(inp["ffn_conv_b"][0])
    for j in range(4):
        vecs[:, V_RCW + j * 8:V_RCW + (j + 1) * 8] = pcl(inp["rec_conv_w"][0, j])
    vecs[:, V_RCB:V_RCB + 8] = pcl(inp["rec_conv_b"][0])
    for d in range(2):
        vecs[:, V_RBA + d * 8:V_RBA + (d + 1) * 8] = pcl(inp["rg_ba"][0, d])
        vecs[:, V_RBI + d * 8:V_RBI + (d + 1) * 8] = pcl(inp["rg_bi"][0, d])
        vecs[:, V_RLAM + d * 8:V_RLAM + (d + 1) * 8] = pcl(inp["rg_lambda"][0, d])
    shared["vecs"] = vecs
    shared["wada"] = np.ascontiguousarray(inp["w_ada"][0], f32)
    shared["win"] = np.ascontiguousarray(inp["w_in"][0], f32)
    shared["wout"] = np.ascontiguousarray(inp["w_out"][0], f32)
    for nm, key in (("wup", "w_up"), ("wgate", "w_gate")):
        w = np.asarray(inp[key][0], f32).reshape(16, 128, NJ, 128)
        shared[nm] = np.ascontiguousarray(w.transpose(2, 1, 0, 3).reshape(NJ, 128, 2048))
    shared["wdown"] = np.ascontiguousarray(np.asarray(inp["w_down"][0], f32).reshape(NJ, 128, 2048))
    rg = np.stack([np.asarray(inp["rg_wa"][0], f32), np.asarray(inp["rg_wi"][0], f32)], 0)
    shared["rgw"] = np.ascontiguousarray(rg.transpose(3, 0, 1, 2, 4).reshape(128, 32 * 128))
    C, Sn = rope_tables(np.arange(S))
    shared["cos"] = C
    shared["sin"] = Sn
    perm = np.zeros((128, 128), f32)
    for p in range(128):
        perm[p ^ 16, p] = 1.0
    shared["perm"] = perm
    shared["identf"] = np.eye(128, dtype=f32)
    shared["identb"] = np.eye(128).astype(ml_dtypes.bfloat16)
    shared["dlam"] = np.ascontiguousarray(np.broadcast_to(np.asarray(inp["diff_lambda"][0], f32).reshape(1, 256), (128, 256)))
    shared["subg"] = np.ascontiguousarray(np.broadcast_to(np.asarray(inp["subln_g"][0], f32).reshape(1, 128), (128, 128)))
    shared["fing"] = np.ascontiguousarray(np.broadcast_to(np.asarray(inp["final_g"], f32).reshape(1, D), (128, D)))
    maps = []
    for c in range(NCORES):
        m = dict(shared)
        xo = np.zeros((17 * 128, D), f32)
        t0 = c * OWN
        xo[0:OWN] = x[t0:t0 + OWN]
        toks = np.zeros(EXT, np.int64)
        toks[0:OWN] = np.arange(t0, t0 + OWN)
        if c > 0:
            xo[OWN] = x[t0 - 1]
            toks[OWN] = t0 - 1
        if c < NCORES - 1:
            xo[OWN + 1] = x[t0 + OWN]
            toks[OWN + 1] = t0 + OWN
        m["xo"] = xo
        Co, So = rope_tables(toks)
        Cp = np.zeros((128, 17 * 128), f32)
        Sp_ = np.zeros((128, 17 * 128), f32)
        Cp[:, :EXT] = Co
        Sp_[:, :EXT] = So
        m["coso"] = Cp
        m["sino"] = Sp_
        mk = np.zeros((128, 32), f32)
        mk[:, c] = 1.0
        if c > 0:
            mk[:, 8 + c - 1] = 1.0
            mk[:, 24] = 1.0
        if c < NCORES - 1:
            mk[:, 16 + c + 1] = 1.0
            mk[:, 25] = 1.0
        m["mk"] = mk
        maps.append(m)
    return maps


STOP_AFTER = 99


def kernel(**inputs):
    maps = host_inputs(inputs)
    nc = build_program(stop_after=STOP_AFTER)
    res = run_bass_kernel_spmd(nc, maps, core_ids=list(range(NCORES)))
    out = np.concatenate([np.asarray(r["out"], np.float32) for r in res.results], 0)
    return out.reshape(1, S, D)
```
